# Optimizing a Trainium2 kernel written in Bass

```python
import math
import jax, jax.numpy as jnp
from jax import lax
import numpy as np

D_MODEL = 1024
BATCH = 4
SEQ = 8192
DEPTH = 2

MLA_HEADS = 8
MLA_NOPE_DIM = 128
MLA_ROPE_DIM = 64
MLA_V_DIM = 128
MLA_Q_RANK = 384
MLA_KV_RANK = 256
ROPE_THETA = 10000.0
SWA_Q_HEADS = 16
SWA_KV_HEADS = 4
SWA_HEAD_DIM = 64
WINDOW = 128
REL_BUCKETS = 32
REL_MAX_DIST = 128
D_FF = 4 * D_MODEL
BLOCK = 128
LN_EPS = 1e-5
RMS_EPS = 1e-6

kernel_name = 'yoco_mla_swa_sink_t5_deepnorm'


def _layernorm(x, g, b):
    xf = x.astype(jnp.float32)
    mu = xf.mean(-1, keepdims=True)
    var = jnp.square(xf - mu).mean(-1, keepdims=True)
    return ((xf - mu) * lax.rsqrt(var + LN_EPS) * g + b).astype(x.dtype)


def _rmsnorm(x, g):
    xf = x.astype(jnp.float32)
    return (xf * lax.rsqrt(jnp.mean(xf * xf, -1, keepdims=True) + RMS_EPS) * g).astype(x.dtype)


def _rope(x, pos):
    half = x.shape[-1] // 2
    inv = ROPE_THETA ** (-jnp.arange(half, dtype=jnp.float32) / half)
    ang = pos.astype(jnp.float32)[:, None] * inv[None, :]
    ang = ang.reshape(ang.shape[:1] + (1,) * (x.ndim - 3) + ang.shape[1:])
    cos, sin = jnp.cos(ang), jnp.sin(ang)
    xf = x.astype(jnp.float32)
    x1, x2 = xf[..., :half], xf[..., half:]
    return jnp.concatenate([x1 * cos - x2 * sin, x1 * sin + x2 * cos], -1).astype(x.dtype)


def _t5_bucket(dist):
    n = jnp.maximum(dist, 0)
    max_exact = REL_BUCKETS // 2
    nf = jnp.maximum(n, 1).astype(jnp.float32)
    large = max_exact + (jnp.log(nf / max_exact) / math.log(REL_MAX_DIST / max_exact)
                         * (REL_BUCKETS - max_exact)).astype(jnp.int32)
    large = jnp.minimum(large, REL_BUCKETS - 1)
    return jnp.where(n < max_exact, n, large)


def _to_blocks(t, nb):
    return t.reshape((t.shape[0], nb, BLOCK) + t.shape[2:]).swapaxes(0, 1)


def _mla(x, w_in, g_q, g_kv, w_uq, w_uk, w_uv, w_o, pos):
    B, S, _ = x.shape
    nb = S // BLOCK
    h = x @ w_in
    c_q = _rmsnorm(h[..., :MLA_Q_RANK], g_q)
    c_kv = _rmsnorm(h[..., MLA_Q_RANK:MLA_Q_RANK + MLA_KV_RANK], g_kv)
    k_r = _rope(h[..., MLA_Q_RANK + MLA_KV_RANK:], pos)
    q = jnp.einsum('bsr,rhd->bshd', c_q, w_uq)
    q_n = q[..., :MLA_NOPE_DIM]
    q_r = _rope(q[..., MLA_NOPE_DIM:], pos)
    q_lat = jnp.einsum('bshn,chn->bshc', q_n, w_uk)
    scale = (MLA_NOPE_DIM + MLA_ROPE_DIM) ** -0.5
    kpos = jnp.arange(S)

    def block(args):
        ql, qr, i = args
        s = (jnp.einsum('bqhc,bkc->bhqk', ql, c_kv)
             + jnp.einsum('bqhr,bkr->bhqk', qr, k_r)).astype(jnp.float32) * scale
        qpos = i * BLOCK + jnp.arange(BLOCK)
        s = jnp.where(kpos[None, :] <= qpos[:, None], s, -jnp.inf)
        p = jax.nn.softmax(s, axis=-1).astype(x.dtype)
        return jnp.einsum('bhqk,bkc->bqhc', p, c_kv)

    o_lat = lax.map(block, (_to_blocks(q_lat, nb), _to_blocks(q_r, nb), jnp.arange(nb)))
    o_lat = o_lat.swapaxes(0, 1).reshape(B, S, MLA_HEADS, MLA_KV_RANK)
    o = jnp.einsum('bshc,chv->bshv', o_lat, w_uv).reshape(B, S, MLA_HEADS * MLA_V_DIM)
    return o @ w_o


def _shared_kv(x, w_kv):
    B, S, _ = x.shape
    nb = S // BLOCK
    kv = (x @ w_kv).reshape(B, nb, BLOCK, 2, SWA_KV_HEADS, SWA_HEAD_DIM)
    k, v = kv[:, :, :, 0], kv[:, :, :, 1]

    def band(t):
        prev = jnp.pad(t, ((0, 0), (1, 0), (0, 0), (0, 0), (0, 0)))[:, :-1]
        return jnp.concatenate([prev, t], axis=2)

    return band(k), band(v)


def _swa(x, w_q, sinks, w_o, k_band, v_band, rel_bias):
    B, S, _ = x.shape
    nb = S // BLOCK
    G = SWA_Q_HEADS // SWA_KV_HEADS
    q = (x @ w_q).reshape(B, nb, BLOCK, SWA_KV_HEADS, G, SWA_HEAD_DIM)
    s = jnp.einsum('bnqkgd,bnjkd->bnkgqj', q, k_band).astype(jnp.float32) * SWA_HEAD_DIM ** -0.5
    i = jnp.arange(BLOCK)
    j = jnp.arange(2 * BLOCK)
    dist = i[:, None] + BLOCK - j[None, :]
    bias = rel_bias[_t5_bucket(dist)].astype(jnp.float32)
    bias = bias.transpose(2, 0, 1).reshape(SWA_KV_HEADS, G, BLOCK, 2 * BLOCK)
    kpos = jnp.arange(nb)[:, None] * BLOCK - BLOCK + j[None, :]
    valid = (dist >= 0) & (dist < WINDOW)
    mask = valid[None] & (kpos >= 0)[:, None, :]
    s = jnp.where(mask[None, :, None, None], s + bias, -jnp.inf)
    sink = sinks.astype(jnp.float32).reshape(SWA_KV_HEADS, G)[..., None]
    m = jnp.maximum(s.max(-1), sink)
    p = jnp.exp(s - m[..., None])
    denom = p.sum(-1) + jnp.exp(sink - m)
    p = (p / denom[..., None]).astype(x.dtype)
    o = jnp.einsum('bnkgqj,bnjkd->bnqkgd', p, v_band).reshape(B, S, SWA_Q_HEADS * SWA_HEAD_DIM)
    return o @ w_o


def _mlp(x, w_up, w_down):
    h = jax.nn.relu(x @ w_up)
    return (h * h) @ w_down


def setup_inputs(seed: int = 0) -> dict:
    key = jax.random.key(seed)
    ks = jax.random.split(key, 20)
    n_a = DEPTH // 2
    n_b = DEPTH - n_a
    beta = (8 * DEPTH) ** -0.25
    f32 = jnp.float32

    def nrm(k, shape, fan_in, gain=1.0):
        return jax.random.normal(k, shape, f32) * (gain * fan_in ** -0.5)

    in_w = MLA_Q_RANK + MLA_KV_RANK + MLA_ROPE_DIM
    kv_k = nrm(ks[8], (D_MODEL, 1, SWA_KV_HEADS * SWA_HEAD_DIM), D_MODEL)
    kv_v = nrm(ks[9], (D_MODEL, 1, SWA_KV_HEADS * SWA_HEAD_DIM), D_MODEL, beta)
    return {
        'x': jax.random.normal(ks[0], (BATCH, SEQ, D_MODEL), f32),
        'mla_w_in': nrm(ks[1], (n_a, D_MODEL, in_w), D_MODEL),
        'mla_g_q': 1.0 + 0.05 * jax.random.normal(ks[2], (n_a, MLA_Q_RANK), f32),
        'mla_g_kv': 1.0 + 0.05 * jax.random.normal(ks[3], (n_a, MLA_KV_RANK), f32),
        'mla_w_uq': nrm(ks[4], (n_a, MLA_Q_RANK, MLA_HEADS, MLA_NOPE_DIM + MLA_ROPE_DIM), MLA_Q_RANK),
        'mla_w_uk': nrm(ks[5], (n_a, MLA_KV_RANK, MLA_HEADS, MLA_NOPE_DIM), MLA_KV_RANK),
        'mla_w_uv': nrm(ks[6], (n_a, MLA_KV_RANK, MLA_HEADS, MLA_V_DIM), MLA_KV_RANK, beta),
        'mla_w_o': nrm(ks[7], (n_a, MLA_HEADS * MLA_V_DIM, D_MODEL), MLA_HEADS * MLA_V_DIM, beta),
        'kv_w_shared': jnp.concatenate([kv_k, kv_v], axis=1).reshape(D_MODEL, 2 * SWA_KV_HEADS * SWA_HEAD_DIM),
        'swa_w_q': nrm(ks[10], (n_b, D_MODEL, SWA_Q_HEADS * SWA_HEAD_DIM), D_MODEL),
        'swa_sinks': 0.5 * jax.random.normal(ks[11], (n_b, SWA_Q_HEADS), f32),
        'swa_w_o': nrm(ks[12], (n_b, SWA_Q_HEADS * SWA_HEAD_DIM, D_MODEL), SWA_Q_HEADS * SWA_HEAD_DIM, beta),
        'rel_bias': 0.5 * jax.random.normal(ks[13], (REL_BUCKETS, SWA_Q_HEADS), f32),
        'mlp_w_up': nrm(ks[14], (DEPTH, D_MODEL, D_FF), D_MODEL),
        'mlp_w_down': nrm(ks[15], (DEPTH, D_FF, D_MODEL), D_FF, beta),
        'ln_mix_g': 1.0 + 0.05 * jax.random.normal(ks[16], (DEPTH, D_MODEL), f32),
        'ln_mix_b': 0.02 * jax.random.normal(ks[17], (DEPTH, D_MODEL), f32),
        'ln_mlp_g': 1.0 + 0.05 * jax.random.normal(ks[18], (DEPTH, D_MODEL), f32),
        'ln_mlp_b': 0.02 * jax.random.normal(ks[19], (DEPTH, D_MODEL), f32),
    }


def reference(x, mla_w_in, mla_g_q, mla_g_kv, mla_w_uq, mla_w_uk, mla_w_uv, mla_w_o,
              kv_w_shared, swa_w_q, swa_sinks, swa_w_o, rel_bias,
              mlp_w_up, mlp_w_down, ln_mix_g, ln_mix_b, ln_mlp_g, ln_mlp_b):
    alpha = (2 * DEPTH) ** 0.25
    n_a = DEPTH // 2
    pos = jnp.arange(x.shape[1])
    k_band = v_band = None
    for l in range(DEPTH):
        if l < n_a:
            y = _mla(x, mla_w_in[l], mla_g_q[l], mla_g_kv[l], mla_w_uq[l], mla_w_uk[l],
                     mla_w_uv[l], mla_w_o[l], pos)
        else:
            if l == n_a:
                k_band, v_band = _shared_kv(x, kv_w_shared)
            b = l - n_a
            y = _swa(x, swa_w_q[b], swa_sinks[b], swa_w_o[b], k_band, v_band, rel_bias)
        x = _layernorm(alpha * x + y, ln_mix_g[l], ln_mix_b[l])
        x = _layernorm(alpha * x + _mlp(x, mlp_w_up[l], mlp_w_down[l]), ln_mlp_g[l], ln_mlp_b[l])
    return x
```

```python
import math
import numpy as np
import ml_dtypes
import concourse.bass as bass
import concourse.mybir as mybir
from concourse.bass_utils import run_bass_kernel_spmd

F32 = mybir.dt.float32
BF16 = mybir.dt.bfloat16
AF = mybir.ActivationFunctionType
ALU = mybir.AluOpType
AX = mybir.AxisListType

D = 1024
QR = 384
CKV = 256
RD = 64
H = 8
NOPE = 128
FF = 4096
NEG = -30000.0
ALPHA = 4 ** 0.25
LN_EPS = 1e-5
RMS_EPS = 1e-6
DEBUG = False
DEFER_POST = True
SPLIT_LN = True
NSLAB = 39
SLAB = 4096
RING = 5


class Buf:
    __slots__ = ("w", "r", "name", "excl")

    def __init__(self, name, excl=False):
        self.name = name
        self.w = None
        self.r = {}
        self.excl = excl


class Prog:
    def __init__(self, nc, stack):
        self.nc = nc
        self.eng = {"pe": nc.tensor, "act": nc.scalar, "dve": nc.vector, "pool": nc.gpsimd, "sp": nc.sync}
        self.sem = {}
        self.cnt = {}
        self.pending = {}
        self.waited = {}
        self.stack = stack
        self.log = {e: [] for e in self.eng}
        for e in self.eng:
            self.sem[e] = stack.enter_context(nc.semaphore("pg_" + e))
            self.cnt[e] = 0
            self.pending[e] = False

    def dsem(self, name):
        if name not in self.sem:
            self.sem[name] = self.stack.enter_context(self.nc.semaphore("d_" + name))
            self.cnt[name] = 0
        return name

    def _wait(self, e, t):
        if t is None:
            return
        p, v = t
        if v is None:
            v = self.cnt[p]
        if p == e and e == "pe":
            return
        if self.waited.get((e, p), 0) >= v:
            return
        if p in self.eng:
            assert v <= self.cnt[p], ("deadlock: waiting on unsignalled", e, p, v, self.cnt[p])
        self.eng[e].wait_ge(self.sem[p], v)
        self.log[e].append(("wait", p, v))
        self.waited[(e, p)] = v

    def _deps(self, e, reads, writes):
        for b in reads:
            self._wait(e, b.w)
        for b in writes:
            self._wait(e, b.w)
            for t in b.r.items():
                self._wait(e, t)

    def _mark(self, t, reads, writes):
        for b in reads:
            if t[1] is None:
                b.r[t[0]] = None
            elif b.r.get(t[0], 0) is not None and b.r.get(t[0], 0) < t[1]:
                b.r[t[0]] = t[1]
        for b in writes:
            b.w = t
            b.r = {}

    def op(self, e, fn, reads=(), writes=(), sig=True):
        ex = [b for b in reads if b.excl]
        if ex:
            reads = [b for b in reads if not b.excl]
            writes = list(writes) + ex
        self._deps(e, reads, writes)
        ins = fn(self.eng[e])
        if sig:
            self.cnt[e] += 1
            ins.then_inc(self.sem[e], 1)
            self.log[e].append(("inc", e, 1))
            self.pending[e] = False
            t = (e, self.cnt[e])
        else:
            assert e == "pe"
            self.pending[e] = True
            t = (e, self.cnt[e] + 1)
        self._mark(t, reads, writes)
        return t

    def check(self):
        val = {k: 0 for k in self.sem}
        pos = {e: 0 for e in self.eng}
        progress = True
        while progress:
            progress = False
            for e in self.eng:
                lg = self.log[e]
                while pos[e] < len(lg):
                    kind, p, v = lg[pos[e]]
                    if kind == "wait":
                        if val[p] < v:
                            break
                    else:
                        val[p] += v
                    pos[e] += 1
                    progress = True
        stuck = {e: (pos[e], len(self.log[e]), self.log[e][pos[e]], val[self.log[e][pos[e]][1]])
                 for e in self.eng if pos[e] < len(self.log[e])}
        assert not stuck, ("DEADLOCK", stuck)

    def dma(self, q, out, in_, semname, reads=(), writes=(), group=False):
        self.dsem(semname)
        self._deps(q, reads, writes)
        self.cnt[semname] += 16
        self.eng[q].dma_start(out=out, in_=in_).then_inc(self.sem[semname], 16)
        self.log[q].append(("inc", semname, 16))
        t = (semname, None if group else self.cnt[semname])
        self._mark(t, reads, writes)
        return t


def _slot_blocks(core_parity, NB):
    CH = NB // 4
    if core_parity == 0:
        return list(range(0, CH)) + list(range(3 * CH, 4 * CH)) + [3 * CH - 1]
    return list(range(CH, 2 * CH)) + list(range(2 * CH, 3 * CH)) + [CH - 1]


def _n_off(s, NB):
    CH = NB // 4
    if s < CH:
        return CH + s
    if s < 2 * CH:
        return 3 * CH + (s - CH)
    return 3 * CH - 1


def build_nc(NB):
    CH = NB // 4
    NS = 2 * CH + 1
    S = NB * 128
    nc = bass.Bass("TRN2", target_bir_lowering=False)

    def din(name, shape, dt=F32):
        return nc.dram_tensor(name, list(shape), dt, kind="ExternalInput").ap()

    xq = din("xq", [NS * 128, D])
    xseqT = din("xseqT", [S, D])
    xqT = din("xqT", [NS * 128, D])
    csq = din("csq", [NS * 128, 64])
    css = din("css", [S, 64])
    maskrows = din("maskrows", [NS, 64, 128], BF16)
    onehot = din("onehot", [64, S], BF16)
    ident_d = din("ident", [128, 128], BF16)
    triT_d = din("triT", [128, 512], BF16)
    w_in = din("mla_w_in", [D, QR + CKV + RD])
    g_q = din("mla_g_q", [QR])
    g_kv = din("mla_g_kv", [CKV])
    w_uq = din("mla_w_uq", [QR, H * 192])
    w_ukT = din("w_ukT", [128, H * CKV])
    w_uv = din("mla_w_uv", [CKV, H * 128])
    w_o = din("mla_w_o", [D, D])
    kv_w = din("kv_w_shared", [D, 512])
    swa_wq = din("swa_w_q", [D, D])
    swa_wo = din("swa_w_o", [D, D])
    sinks = din("swa_sinks", [16])
    biasT = din("biasT", [2, 128, 16 * 128])
    swa_flags = din("swa_flags", [128, 4])
    w_up = din("mlp_w_up", [2, D, FF])
    w_down = din("mlp_w_down", [2, FF, D])
    ln_g = din("ln_g", [4, D])
    ln_b = din("ln_b", [4, D])
    ln_gT = din("ln_gT", [128, 32])
    ln_bT = din("ln_bT", [128, 32])
    out = nc.dram_tensor("out", [2 * CH * 128, D], F32, kind="ExternalOutput").ap()
    wscr = nc.dram_tensor("wscr", [NSLAB, 128, SLAB], BF16).ap()
    dbg = {}

    def dbg_dump(name, ap2d):
        if not DEBUG:
            return
        shp = list(ap2d.shape)
        d = nc.dram_tensor("dbg_" + name, shp, ap2d.dtype, kind="ExternalOutput").ap()
        nc.all_engine_barrier()
        P.dma("sp", d, ap2d, "out0_0")
        dbg[name] = d

    from contextlib import ExitStack
    with ExitStack() as st:
        P = Prog(nc, st)

        def sb(name, shape, dt, stack=st):
            return stack.enter_context(nc.sbuf_tensor("s_" + name, list(shape), dt))

        banks = [st.enter_context(nc.psum_tensor("bank%d" % i, [128, 512], F32)) for i in range(8)]
        BQ = [[Buf("bank%d" % i, excl=True)] * 4 for i in range(8)]

        class _BK:
            def __getitem__(self, i):
                return BQ[i]
        BKL = _BK()

        def BKr(i, lo=0, hi=512):
            return [BQ[i][0]]

        def bkf(i):
            return banks[i][:]

        def bkh(i):
            return banks[i][:].bitcast(BF16)

        ident = sb("ident", [128, 128], BF16)
        triT = sb("triT", [128, 512], BF16)
        ones_c = sb("ones_c", [128, 1], BF16)
        OT = sb("OT", [128, NS, 8, 128], BF16)
        B_const = Buf("const")
        B_OT = [Buf("OT%d" % s) for s in range(NS)]

        P.dma("sp", ident[:], ident_d, "c0", writes=[B_const], group=True)
        P.dma("sp", triT[:], triT_d, "c0", writes=[B_const], group=True)
        B_ones = Buf("ones")
        P.op("dve", lambda e: e.memset(ones_c[:], 1.0), writes=[B_ones])

        def transposes(src_aps, bank, reads, n_part_out=128):
            t = None
            hv = bkh(bank)
            for i, a in enumerate(src_aps):
                m = a.shape[-1]
                last = i == len(src_aps) - 1
                t = P.op("pe", lambda e, a=a, i=i, m=m: e.transpose(hv[0:m, i * 128:(i + 1) * 128], a, ident[:]),
                         reads=list(reads) + [B_const], writes=BKr(bank), sig=last)
            return t

        with ExitStack() as sa:
            def sba(name, shape, dt):
                return sb(name, shape, dt, sa)

            KT_c = sba("KT_c", [128, 2, S], BF16)
            KT_r = sba("KT_r", [128, S], BF16)
            Vst = sba("Vst", [128, NB, CKV], BF16)
            w_inq_s = sba("w_inq", [128, 8, QR], BF16)
            w_inkv_s = sba("w_inkv", [128, 8, CKV + RD], BF16)
            w_uq_s = sba("w_uq", [128, 3, H * 192], BF16)
            w_ukT_s = sba("w_ukT", [128, H, CKV], BF16)
            w_uv_s = sba("w_uv", [128, 2, H * 128], BF16)
            gq_rep = sba("gq_rep", [128, QR], F32)
            gkv_rep = sba("gkv_rep", [128, CKV], F32)
            B_wA = Buf("wA")
            B_K = [Buf("K%d" % k) for k in range(NB)]
            B_KTr_const = Buf("ktr_const")

            P.dma("pool", w_inq_s[:], w_in[:, 0:QR].rearrange("(c p) n -> p c n", p=128), "c1", writes=[B_wA], group=True)
            P.dma("pool", w_inkv_s[:], w_in[:, QR:QR + CKV + RD].rearrange("(c p) n -> p c n", p=128), "c1", writes=[B_wA], group=True)
            P.dma("pool", w_uq_s[:], w_uq.rearrange("(c p) n -> p c n", p=128), "c1", writes=[B_wA], group=True)
            P.dma("pool", w_ukT_s[:], w_ukT.rearrange("p (h c) -> p h c", h=H), "c1", writes=[B_wA], group=True)
            P.dma("pool", w_uv_s[:], w_uv.rearrange("(c p) n -> p c n", p=128), "c1", writes=[B_wA], group=True)
            P.dma("sp", gq_rep[:], g_q.partition_broadcast(128), "c0", writes=[B_wA], group=True)
            P.dma("sp", gkv_rep[:], g_kv.partition_broadcast(128), "c0", writes=[B_wA], group=True)
            P.dma("sp", KT_r[64:128, :], onehot, "c0", writes=[B_KTr_const], group=True)

            xb = [sba("xb%d" % i, [128, D], BF16) for i in range(2)]
            xb += [OT[:, 0].rearrange("p h t -> p (h t)"), OT[:, 1].rearrange("p h t -> p (h t)")]
            B_xb = [Buf("xb%d" % i) for i in range(4)]
            cs = [sba("cs%d" % i, [128, 64], F32) for i in range(4)]
            B_cs = [Buf("cs%d" % i) for i in range(4)]
            xT = sba("xT", [128, 8, 128], BF16)
            B_xT = Buf("xT")
            h_sb = sba("h_sb", [128, QR + CKV + RD], F32)
            B_h = Buf("h_sb")
            stat = sba("stat", [128, 8], F32)
            B_stat = Buf("stat")
            rtmp = sba("rtmp", [128, 2, 8, 64], F32)
            B_rtmpA = Buf("rtmpA")
            B_rtmpB = Buf("rtmpB")
            kr_sb = sba("kr_sb", [128, RD], BF16)
            B_kr = Buf("kr")
            cq = sba("cq", [128, QR], BF16)
            B_cq = Buf("cq")
            cqT = sba("cqT", [128, 3, 128], BF16)
            B_cqT = Buf("cqT")
            Vown = sba("Vown", [128, CKV], BF16)
            B_Vown = Buf("Vown")
            KTown_c = sba("KTown_c", [128, 2, 128], BF16)
            KTown_r = sba("KTown_r", [128, 128], BF16)
            B_KTown = Buf("KTown")
            qn = h_sb[:].bitcast(BF16)[:, 0:H * NOPE].rearrange("p (h n) -> p h n", h=H)
            B_qn = B_h
            qr_f = sba("qr_f", [128, H, RD], F32)
            B_qrf = Buf("qr_f")
            junk = qr_f[:].rearrange("p h r -> p (h r)")[:, 0:QR]
            B_junk = B_qrf
            qr = sba("qr", [128, H, RD], BF16)
            B_qr = Buf("qr")
            qnT = xT
            B_qnT = B_xT
            QT_c = sba("QT_c", [128, 2, H, 128], BF16)
            QT_r = sba("QT_r", [128, H, 128], BF16)
            B_QTc = Buf("QT_c")
            B_QTr_lo = Buf("QT_r_lo")
            B_QTr_hi = Buf("QT_r_hi")
            PT = [sba("PT%d" % i, [128, 512], BF16) for i in range(3)]
            B_PT = [Buf("PT%d" % i) for i in range(3)]
            olat = rtmp[:, 0].bitcast(BF16).rearrange("p h f -> p (h f)").rearrange("p (h c) -> p h c", h=4)
            B_olat = B_rtmpA
            olatT = rtmp[:, 1].bitcast(BF16).rearrange("p h f -> p (h f)").rearrange("p (h j t) -> p h j t", h=4, j=2)
            B_olatT = B_rtmpB
            rden = sba("rden", [128, 4], F32)
            B_rden = Buf("rden")

            P.op("dve", lambda e: e.memset(KTown_r[:], 0.0), writes=[B_KTown])

            def rms_scale(src_ap, n, col, eps):
                P.op("dve", lambda e: e.scalar_tensor_tensor(out=junk[:, 0:n], in0=src_ap, scalar=1.0, in1=src_ap,
                                                             op0=ALU.mult, op1=ALU.mult, accum_out=stat[:, col:col + 1]),
                     reads=[B_h], writes=[B_junk, B_stat])
                P.op("act", lambda e: e.activation(out=stat[:, col:col + 1], in_=stat[:, col:col + 1], func=AF.Ln,
                                                   scale=1.0 / n, bias=eps),
                     writes=[B_stat])
                P.op("act", lambda e: e.activation(out=stat[:, col:col + 1], in_=stat[:, col:col + 1], func=AF.Exp,
                                                   scale=-0.5),
                     writes=[B_stat])

            def rope(src3, dst3, nh, csb, bcs, bsrc, bdst):
                s4 = src3.rearrange("p h (t f) -> p h t f", t=2)
                A = rtmp[:, 0, 0:nh, :].rearrange("p h (t f) -> p h t f", t=2)
                Bm = rtmp[:, 1, 0:nh, :].rearrange("p h (t f) -> p h t f", t=2)
                cosb = csb[:, 0:32].unsqueeze(1).unsqueeze(1).to_broadcast([128, nh, 2, 32])
                sinb = csb[:, 32:64].unsqueeze(1).unsqueeze(1).to_broadcast([128, nh, 2, 32])
                P.op("dve", lambda e: e.tensor_tensor(out=A, in0=s4, in1=cosb, op=ALU.mult), reads=[bcs, bsrc], writes=[B_rtmpA])
                P.op("dve", lambda e: e.tensor_tensor(out=Bm, in0=s4, in1=sinb, op=ALU.mult), reads=[bcs, bsrc], writes=[B_rtmpB])
                d4 = dst3.rearrange("p h (t f) -> p h t f", t=2)
                P.op("dve", lambda e: e.tensor_tensor(out=d4[:, :, 0, :], in0=A[:, :, 0, :], in1=Bm[:, :, 1, :], op=ALU.subtract),
                     reads=[B_rtmpA, B_rtmpB], writes=[bdst])
                return P.op("dve", lambda e: e.tensor_tensor(out=d4[:, :, 1, :], in0=Bm[:, :, 0, :], in1=A[:, :, 1, :], op=ALU.add),
                            reads=[B_rtmpA, B_rtmpB], writes=[bdst])

            def load_x_block(src_rows, cs_rows, i):
                P.dma("pool", xb[i][:, :], src_rows, "xb%d" % i, writes=[B_xb[i]])
                P.dma("sp", cs[i][:], cs_rows, "cs%d" % i, writes=[B_cs[i]])

            BX = 0

            def make_xT(i):
                bx = BX
                transposes([xb[i][:, c * 128:(c + 1) * 128] for c in range(8)], bx, [B_xb[i]])
                P.op("act", lambda e: e.copy(out=xT[:].rearrange("p c t -> p (c t)"), in_=bkh(bx)),
                     reads=BKr(bx), writes=[B_xT])

            def kv_from_h(i, v_dst, bv):
                rms_scale(h_sb[:, QR:QR + CKV], CKV, 1, RMS_EPS)
                P.op("dve", lambda e: e.scalar_tensor_tensor(out=v_dst, in0=h_sb[:, QR:QR + CKV], scalar=stat[:, 1:2],
                                                             in1=gkv_rep[:], op0=ALU.mult, op1=ALU.mult),
                     reads=[B_h, B_stat, B_wA], writes=[bv])
                rope(h_sb[:, QR + CKV:QR + CKV + RD].unsqueeze(1), kr_sb[:].unsqueeze(1), 1, cs[i], B_cs[i], B_h, B_kr)

            scr_jobs = []

            def slab_view(idx, inner):
                return wscr[idx].rearrange("p (c n) -> p c n", n=inner)

            k = 0
            for half in range(2):
                scr_jobs.append((slab_view(k, 1024), w_o[half * 512:(half + 1) * 512, :].rearrange("(c p) n -> p c n", p=128)))
                k += 1
            for L in range(2):
                if L == 1:
                    scr_jobs.append((slab_view(k, 512), kv_w.rearrange("(c p) n -> p c n", p=128)))
                    k += 1
                    for wsrc in (swa_wq, swa_wo):
                        for half in range(2):
                            scr_jobs.append((slab_view(k, 1024), wsrc[half * 512:(half + 1) * 512, :].rearrange("(c p) n -> p c n", p=128)))
                            k += 1
                for j in range(8):
                    scr_jobs.append((slab_view(k, 512), w_up[L][:, j * 512:(j + 1) * 512].rearrange("(c p) n -> p c n", p=128)))
                    k += 1
                    scr_jobs.append((slab_view(k, 1024), w_down[L][j * 512:(j + 1) * 512, :].rearrange("(c p) n -> p c n", p=128)))
                    k += 1
            assert k == NSLAB
            B_scr = Buf("scr")

            def issue_scr(n):
                for _ in range(n):
                    if scr_jobs:
                        o, i_ = scr_jobs.pop(0)
                        P.dma("pool", o, i_, "scr", writes=[B_scr])

            NSET = 4
            xTr = [xb[i4][:, :].rearrange("p (c t) -> p c t", c=8) for i4 in range(4)]
            sets = []
            for j in range(NSET):
                if NS >= 2 + 2 * NSET:
                    A_ = OT[:, 2 + 2 * j].rearrange("p h t -> p (h t)").bitcast(F32)
                    B_ = OT[:, 3 + 2 * j].rearrange("p h t -> p (h t)").bitcast(F32)
                else:
                    A_ = sba("p1A%d" % j, [128, 512], F32)[:]
                    B_ = sba("p1B%d" % j, [128, 512], F32)[:]
                sets.append(dict(h=A_[:, 0:320], rA=A_[:, 320:384], rB=A_[:, 384:448], st=A_[:, 448:456],
                                 junk=B_[:, 0:256], kr=B_[:, 256:288].bitcast(BF16),
                                 B_h=Buf("p1h%d" % j), B_st=Buf("p1st%d" % j), B_junk=Buf("p1j%d" % j),
                                 B_r=Buf("p1r%d" % j), B_kr=Buf("p1kr%d" % j), bh=2 * j, bt=2 * j + 1))

            def p1_S0(k):
                i4 = k % 4
                P.dma("pool", xb[i4][:, :], xseqT[k * 128:(k + 1) * 128, :], "xb%d" % i4, writes=[B_xb[i4]])
                P.dma("sp", cs[i4][:], css[k * 128:(k + 1) * 128, :], "cs%d" % i4, writes=[B_cs[i4]])

            def p1_S1(k):
                i4 = k % 4
                W = sets[k % NSET]
                for c in range(8):
                    P.op("pe", lambda e, c=c: e.matmul(bkf(W["bh"])[:, 0:CKV + RD], xTr[i4][:, c, :], w_inkv_s[:, c, :],
                                                       start=(c == 0), stop=(c == 7)),
                         reads=[B_xb[i4], B_wA], writes=BKr(W["bh"]), sig=(c == 7))
                P.op("act", lambda e: e.copy(out=W["h"], in_=bkf(W["bh"])[:, 0:CKV + RD]), reads=BKr(W["bh"]), writes=[W["B_h"]])

            def p1_S2(k):
                W = sets[k % NSET]
                P.op("dve", lambda e: e.scalar_tensor_tensor(out=W["junk"], in0=W["h"][:, 0:CKV], scalar=1.0, in1=W["h"][:, 0:CKV],
                                                             op0=ALU.mult, op1=ALU.mult, accum_out=W["st"][:, 0:1]),
                     reads=[W["B_h"]], writes=[W["B_junk"], W["B_st"]])
                P.op("act", lambda e: e.activation(out=W["st"][:, 0:1], in_=W["st"][:, 0:1], func=AF.Ln, scale=1.0 / CKV, bias=RMS_EPS),
                     writes=[W["B_st"]])
                P.op("act", lambda e: e.activation(out=W["st"][:, 0:1], in_=W["st"][:, 0:1], func=AF.Exp, scale=-0.5), writes=[W["B_st"]])

            def p1_S3(k):
                i4 = k % 4
                W = sets[k % NSET]
                P.op("dve", lambda e: e.scalar_tensor_tensor(out=Vst[:, k, :], in0=W["h"][:, 0:CKV], scalar=W["st"][:, 0:1],
                                                             in1=gkv_rep[:], op0=ALU.mult, op1=ALU.mult),
                     reads=[W["B_h"], W["B_st"], B_wA], writes=[B_K[k]])
                s4 = W["h"][:, CKV:CKV + RD].rearrange("p (t f) -> p t f", t=2)
                A4 = W["rA"].rearrange("p (t f) -> p t f", t=2)
                B4 = W["rB"].rearrange("p (t f) -> p t f", t=2)
                cosb = cs[i4][:, 0:32].unsqueeze(1).to_broadcast([128, 2, 32])
                sinb = cs[i4][:, 32:64].unsqueeze(1).to_broadcast([128, 2, 32])
                P.op("dve", lambda e: e.tensor_tensor(out=A4, in0=s4, in1=cosb, op=ALU.mult), reads=[W["B_h"], B_cs[i4]], writes=[W["B_r"]])
                P.op("dve", lambda e: e.tensor_tensor(out=B4, in0=s4, in1=sinb, op=ALU.mult), reads=[W["B_h"], B_cs[i4]], writes=[W["B_r"]])
                d4 = W["kr"].rearrange("p (t f) -> p t f", t=2)
                P.op("dve", lambda e: e.tensor_tensor(out=d4[:, 0, :], in0=A4[:, 0, :], in1=B4[:, 1, :], op=ALU.subtract),
                     reads=[W["B_r"]], writes=[W["B_kr"]])
                P.op("dve", lambda e: e.tensor_tensor(out=d4[:, 1, :], in0=B4[:, 0, :], in1=A4[:, 1, :], op=ALU.add),
                     reads=[W["B_r"]], writes=[W["B_kr"]])

            def p1_S4(k):
                W = sets[k % NSET]
                transposes([Vst[:, k, 0:128], Vst[:, k, 128:256], W["kr"]], W["bt"], [B_K[k], W["B_kr"]])
                h2 = bkh(W["bt"])
                P.op("act", lambda e: e.copy(out=KT_c[:, :, k * 128:(k + 1) * 128], in_=h2[:, 0:256].rearrange("p (j t) -> p j t", j=2)),
                     reads=BKr(W["bt"]), writes=[B_K[k]])
                P.op("dve", lambda e: e.tensor_copy(out=KT_r[0:64, k * 128:(k + 1) * 128], in_=h2[0:64, 256:384]),
                     reads=BKr(W["bt"]), writes=[B_K[k]])

            for t in range(NB + 3):
                if t < NB:
                    p1_S0(t)
                if 0 <= t - 3 < NB:
                    p1_S4(t - 3)
                if 0 <= t - 2 < NB:
                    p1_S3(t - 2)
                if 0 <= t - 1 < NB:
                    p1_S2(t - 1)
                if t < NB:
                    p1_S1(t)
            nc.all_engine_barrier()
            scr_per_slot = -(-len(scr_jobs) // max(1, min(NS - 1, 20)))

            scale = float((NOPE + RD) ** -0.5)
            for s in range(NS):
                i = s % 2
                n_off = _n_off(s, NB)
                P.dma("pool", xb[i][:, :], xqT[s * 128:(s + 1) * 128, :], "xb%d" % i, writes=[B_xb[i]])
                P.dma("sp", cs[i][:], csq[s * 128:(s + 1) * 128, :], "cs%d" % i, writes=[B_cs[i]])
                xTs = xb[i][:, :].rearrange("p (c t) -> p c t", c=8)
                issue_scr(scr_per_slot)
                P.dma("sp", QT_r[64:128, :, :], maskrows[s].unsqueeze(1).to_broadcast([64, H, 128]), "mrow",
                      writes=[B_QTr_hi])
                for c in range(8):
                    P.op("pe", lambda e, c=c: e.matmul(bkf(1)[:, 0:QR], xTs[:, c, :], w_inq_s[:, c, :],
                                                       start=(c == 0), stop=(c == 7)),
                         reads=[B_xb[i], B_wA], writes=BKr(1), sig=(c == 7))
                for c in range(8):
                    P.op("pe", lambda e, c=c: e.matmul(bkf(2)[:, 0:CKV + RD], xTs[:, c, :], w_inkv_s[:, c, :],
                                                       start=(c == 0), stop=(c == 7)),
                         reads=[B_xb[i], B_wA], writes=BKr(2), sig=(c == 7))
                P.op("act", lambda e: e.copy(out=h_sb[:, 0:QR], in_=bkf(1)[:, 0:QR]), reads=BKr(1), writes=[B_h])
                P.op("act", lambda e: e.copy(out=h_sb[:, QR:QR + CKV + RD], in_=bkf(2)[:, 0:CKV + RD]),
                     reads=BKr(2), writes=[B_h])
                P.op("dve", lambda e: e.scalar_tensor_tensor(out=junk[:, 0:QR], in0=h_sb[:, 0:QR], scalar=1.0 / QR, in1=h_sb[:, 0:QR],
                                                             op0=ALU.mult, op1=ALU.mult, accum_out=stat[:, 0:1]),
                     reads=[B_h], writes=[B_junk, B_stat])
                P.op("dve", lambda e: e.scalar_tensor_tensor(out=junk[:, 0:CKV], in0=h_sb[:, QR:QR + CKV], scalar=1.0 / CKV,
                                                             in1=h_sb[:, QR:QR + CKV], op0=ALU.mult, op1=ALU.mult, accum_out=stat[:, 1:2]),
                     reads=[B_h], writes=[B_junk, B_stat])
                P.op("act", lambda e: e.activation(out=stat[:, 0:2], in_=stat[:, 0:2], func=AF.Ln, bias=RMS_EPS), writes=[B_stat])
                P.op("act", lambda e: e.activation(out=stat[:, 0:2], in_=stat[:, 0:2], func=AF.Exp, scale=-0.5), writes=[B_stat])
                P.op("dve", lambda e: e.scalar_tensor_tensor(out=cq[:], in0=h_sb[:, 0:QR], scalar=stat[:, 0:1],
                                                             in1=gq_rep[:], op0=ALU.mult, op1=ALU.mult),
                     reads=[B_h, B_stat, B_wA], writes=[B_cq])
                P.op("dve", lambda e: e.scalar_tensor_tensor(out=Vown[:], in0=h_sb[:, QR:QR + CKV], scalar=stat[:, 1:2],
                                                             in1=gkv_rep[:], op0=ALU.mult, op1=ALU.mult),
                     reads=[B_h, B_stat, B_wA], writes=[B_Vown])
                rope(h_sb[:, QR + CKV:QR + CKV + RD].unsqueeze(1), kr_sb[:].unsqueeze(1), 1, cs[i], B_cs[i], B_h, B_kr)
                transposes([cq[:, 0:128], cq[:, 128:256], cq[:, 256:384], Vown[:, 0:128], Vown[:, 128:256], kr_sb[:]],
                           0, [B_cq, B_Vown, B_kr])
                h0 = bkh(0)
                P.op("act", lambda e: e.copy(out=cqT[:].rearrange("p c t -> p (c t)"), in_=h0[:, 0:384]),
                     reads=BKr(0), writes=[B_cqT])
                P.op("dve", lambda e: e.tensor_copy(out=KTown_c[:].rearrange("p c t -> p (c t)"), in_=h0[:, 384:640]),
                     reads=BKr(0), writes=[B_KTown])
                P.op("dve", lambda e: e.tensor_copy(out=KTown_r[0:64, :], in_=h0[0:64, 640:768]),
                     reads=BKr(0), writes=[B_KTown])
                for g in range(3):
                    for c in range(3):
                        P.op("pe", lambda e, g=g, c=c: e.matmul(bkf(3 + g), cqT[:, c, :], w_uq_s[:, c, g * 512:(g + 1) * 512],
                                                                start=(c == 0), stop=(c == 2)),
                             reads=[B_cqT, B_wA], writes=BKr(3 + g), sig=(c == 2))
                P.op("act", lambda e: e.copy(out=qn[:, 0:4, :].rearrange("p h n -> p (h n)"), in_=bkf(3)), reads=BKr(3), writes=[B_qn])
                P.op("dve", lambda e: e.tensor_copy(out=qn[:, 4:8, :].rearrange("p h n -> p (h n)"), in_=bkf(4)), reads=BKr(4), writes=[B_qn])
                P.op("act", lambda e: e.copy(out=qr_f[:].rearrange("p h r -> p (h r)"), in_=bkf(5)), reads=BKr(5), writes=[B_qrf])
                rope(qr_f[:], qr[:], H, cs[i], B_cs[i], B_qrf, B_qr)
                transposes([qn[:, h, :] for h in range(H)], 6, [B_qn])
                P.op("act", lambda e: e.copy(out=qnT[:].rearrange("p h t -> p (h t)"), in_=bkh(6)), reads=BKr(6), writes=[B_qnT])
                transposes([qr[:, h, :] for h in range(H)], 7, [B_qr])
                P.op("dve", lambda e: e.tensor_copy(out=QT_r[0:64, :, :].rearrange("p h t -> p (h t)"), in_=bkh(7)[0:64, :]),
                     reads=BKr(7), writes=[B_QTr_lo])
                for j in range(2):
                    for h in range(H):
                        bk = j * 2 + h // 4
                        P.op("pe", lambda e, j=j, h=h, bk=bk: e.matmul(bkf(bk)[:, (h % 4) * 128:(h % 4 + 1) * 128],
                                                                       w_ukT_s[:, h, j * 128:(j + 1) * 128], qnT[:, h, :],
                                                                       start=True, stop=True, skip_group_check=True),
                             reads=[B_qnT, B_wA], writes=BKr(bk), sig=(h % 4 == 3))
                for j in range(2):
                    for hg in range(2):
                        bk = j * 2 + hg
                        eng = "act" if hg == 0 else "dve"
                        dst = QT_c[:, j, hg * 4:(hg + 1) * 4, :].rearrange("p h t -> p (h t)")
                        if eng == "act":
                            P.op("act", lambda e, dst=dst, bk=bk: e.copy(out=dst, in_=bkf(bk)), reads=BKr(bk), writes=[B_QTc])
                        else:
                            P.op("dve", lambda e, dst=dst, bk=bk: e.tensor_copy(out=dst, in_=bkf(bk)), reads=BKr(bk), writes=[B_QTc])

                deferred = []
                for hg in range(2):
                    ob = [0, 1] if hg == 0 else [2, 3]
                    db = 4
                    sring = [5, 6, 7]
                    nun = n_off + 1

                    def s_mm(u):
                        r = sring[u % 3]
                        diag = (u == n_off)
                        hs = slice(hg * 4, hg * 4 + 4)
                        if diag:
                            l0, l1, l2 = KTown_c[:, 0, :], KTown_c[:, 1, :], KTown_r[:]
                            rd = [B_KTown]
                        else:
                            l0 = KT_c[:, 0, u * 128:(u + 1) * 128]
                            l1 = KT_c[:, 1, u * 128:(u + 1) * 128]
                            l2 = KT_r[:, u * 128:(u + 1) * 128]
                            rd = [B_K[u], B_KTr_const]
                        P.op("pe", lambda e: e.matmul(bkf(r), l0, QT_c[:, 0, hs, :].rearrange("p h t -> p (h t)"), start=True, stop=False),
                             reads=rd + [B_QTc], writes=BKr(r), sig=False)
                        P.op("pe", lambda e: e.matmul(bkf(r), l1, QT_c[:, 1, hs, :].rearrange("p h t -> p (h t)"), start=False, stop=False),
                             reads=rd + [B_QTc], writes=BKr(r), sig=False)
                        P.op("pe", lambda e: e.matmul(bkf(r), l2, QT_r[:, hs, :].rearrange("p h t -> p (h t)"), start=False, stop=(not diag)),
                             reads=rd + [B_QTr_lo, B_QTr_hi], writes=BKr(r), sig=(not diag))
                        if diag:
                            P.op("pe", lambda e: e.matmul(bkf(r), ident[:], triT[:], start=False, stop=True),
                                 reads=[B_const], writes=BKr(r), sig=True)

                    def exp_u(u):
                        r = sring[u % 3]
                        pt = u % 3
                        P.op("act", lambda e: e.activation(out=PT[pt][:], in_=bkf(r), func=AF.Exp, scale=scale),
                             reads=BKr(r), writes=[B_PT[pt]])

                    def pv_mm(u):
                        pt = u % 3
                        diag = (u == n_off)
                        last = (u == nun - 1)
                        vsrc = Vown[:] if diag else Vst[:, u, :]
                        rd = [B_Vown] if diag else [B_K[u]]
                        for hh in range(4):
                            bk = ob[hh // 2]
                            col = (hh % 2) * 256
                            P.op("pe", lambda e, hh=hh, bk=bk, col=col: e.matmul(
                                bkf(bk)[:, col:col + 256], PT[pt][:, hh * 128:(hh + 1) * 128], vsrc,
                                start=(u == 0 and hh % 2 == 0), stop=last, skip_group_check=True),
                                reads=rd + [B_PT[pt]], writes=BKr(bk), sig=(last and hh % 2 == 1))
                            P.op("pe", lambda e, hh=hh: e.matmul(
                                bkf(db)[:, hg * 4 + hh:hg * 4 + hh + 1], PT[pt][:, hh * 128:(hh + 1) * 128], ones_c[:],
                                start=(u == 0 and hh == 0), stop=last, skip_group_check=True),
                                reads=[B_PT[pt], B_ones], writes=BKr(db), sig=(hh == 3))

                    s_mm(0)
                    for u in range(nun):
                        exp_u(u)
                        if u + 1 < nun:
                            s_mm(u + 1)
                        pv_mm(u)
                        if deferred and u >= 1:
                            deferred.pop(0)()
                    while deferred:
                        deferred.pop(0)()

                    P.op("dve", lambda e: e.reciprocal(out=rden[:], in_=bkf(db)[:, hg * 4:hg * 4 + 4]),
                         reads=BKr(db), writes=[B_rden])
                    for half in range(2):
                        bk = ob[half]
                        P.op("dve", lambda e, half=half, bk=bk: e.tensor_tensor(
                            out=olat[:, half * 2:half * 2 + 2, :],
                            in0=bkf(bk).rearrange("p (h c) -> p h c", h=2),
                            in1=rden[:, half * 2:half * 2 + 2].unsqueeze(2).to_broadcast([128, 2, CKV]), op=ALU.mult),
                            reads=BKr(bk) + [B_rden], writes=[B_olat])

                    def stage_b(ob=ob):
                        transposes([olat[:, hh, j * 128:(j + 1) * 128] for hh in range(4) for j in range(2)], ob[0], [B_olat])
                        P.op("act", lambda e: e.copy(out=olatT[:].rearrange("p h j t -> p (h j t)"), in_=bkh(ob[0])),
                             reads=BKr(ob[0]), writes=[B_olatT])

                    def stage_c(ob=ob, hg=hg, s=s):
                        for hh in range(4):
                            h = hg * 4 + hh
                            for j in range(2):
                                P.op("pe", lambda e, hh=hh, h=h, j=j: e.matmul(
                                    bkf(ob[1])[:, hh * 128:(hh + 1) * 128], w_uv_s[:, j, h * 128:(h + 1) * 128], olatT[:, hh, j, :],
                                    start=(j == 0), stop=(j == 1), skip_group_check=True),
                                    reads=[B_olatT, B_wA], writes=BKr(ob[1]), sig=(hh == 3 and j == 1))
                        P.op("dve", lambda e: e.tensor_copy(out=OT[:, s, hg * 4:hg * 4 + 4, :].rearrange("p h t -> p (h t)"), in_=bkf(ob[1])),
                             reads=BKr(ob[1]), writes=[B_OT[s]])

                    if hg == 0 and DEFER_POST:
                        stage_b()
                        deferred.extend([stage_c])
                    else:
                        stage_b()
                        stage_c()

            dbg_dump("V", Vst[:].rearrange("p k c -> p (k c)"))
            dbg_dump("KTc", KT_c[:].rearrange("p j s -> p (j s)"))
            dbg_dump("KTr", KT_r[:])
            dbg_dump("OT", OT[:].rearrange("p s h t -> p (s h t)"))
            dbg_dump("QTc", QT_c[:].rearrange("p j h t -> p (j h t)"))
            dbg_dump("QTr", QT_r[:].rearrange("p h t -> p (h t)"))
            dbg_dump("cq", cq[:])
            dbg_dump("qn", qn[:].rearrange("p h n -> p (h n)"))
            dbg_dump("qr", qr[:].rearrange("p h n -> p (h n)"))
            dbg_dump("olat", olat[:].rearrange("p h n -> p (h n)"))
            dbg_dump("rden", rden[:])
            dbg_dump("hsb", h_sb[:])
            dbg_dump("stat", stat[:])

        nc.all_engine_barrier()
        with ExitStack() as sbk:
            def sbb(name, shape, dt):
                return sb(name, shape, dt, sbk)

            ring = [sbb("ring%d" % i, [128, SLAB], BF16) for i in range(RING)]
            B_ring = [Buf("ring%d" % i) for i in range(RING)]
            lng = sbb("lng", [128, 4, D], F32)
            lnb = sbb("lnb", [128, 4, D], F32)
            ET = sbb("ET", [128, 2, 16 * 128], BF16)
            qT_full = sbb("qT", [128, 16, 256], BF16)
            ETf = qT_full[:].rearrange("p h t -> p (h t)").bitcast(F32)
            sink_e = sbb("sink_e", [128, 16], F32)
            flags = sbb("flags", [128, 4], F32)
            B_cB = Buf("constB")
            for L4 in range(4):
                P.dma("sp", lng[:, L4, :], ln_g[L4].partition_broadcast(128), "c2", writes=[B_cB], group=True)
                P.dma("sp", lnb[:, L4, :], ln_b[L4].partition_broadcast(128), "c2", writes=[B_cB], group=True)
            P.dma("sp", sink_e[:], sinks.partition_broadcast(128), "c2", writes=[B_cB], group=True)
            P.dma("sp", flags[:], swa_flags, "c2", writes=[B_cB], group=True)
            for t2 in range(2):
                P.dma("sp", ETf, biasT[t2], "c2", writes=[B_cB], group=True)
                t_et = P.op("act", lambda e, t2=t2: e.activation(out=ET[:, t2, :], in_=ETf, func=AF.Exp), reads=[B_cB], writes=[B_cB])
            P.op("act", lambda e: e.activation(out=sink_e[:], in_=sink_e[:], func=AF.Exp), reads=[B_cB], writes=[B_cB])

            xs_all = sbb("xs", [128, 2, 2, D], F32)
            B_xs_all = [[Buf("xs00"), Buf("xs01")], [Buf("xs10"), Buf("xs11")]]
            xs = xs_all[:, 0]
            B_xs = B_xs_all[0]
            cur = {"par": 0}
            zb = [sbb("zb%d" % i, [128, D], BF16) for i in range(2)]
            B_zb = [Buf("zb0"), Buf("zb1")]
            gT = sbb("gT", [128, 4, 8], F32)
            bT = sbb("bT", [128, 4, 8], F32)
            P.dma("sp", gT[:], ln_gT.rearrange("p (l c) -> p l c", l=4), "c2", writes=[B_cB], group=True)
            P.dma("sp", bT[:], ln_bT.rearrange("p (l c) -> p l c", l=4), "c2", writes=[B_cB], group=True)
            aT = sbb("aT", [128, 8, 256], BF16)
            B_aTs = [Buf("aT0"), Buf("aT1")]
            hT = [sbb("hT%d" % i, [128, 4, 256], BF16) for i in range(2)]
            B_hT = [Buf("hT0"), Buf("hT1")]
            bst_all = sbb("bst", [128, 2, 2, 2, 6], F32)
            mv_all = sbb("mv", [128, 2, 2, 4], F32)
            B_st_all = [[Buf("bst00"), Buf("bst01")], [Buf("bst10"), Buf("bst11")]]
            bst = bst_all[:, 0]
            mv = mv_all[:, 0]
            B_st = B_st_all[0]
            qT = qT_full[0:64]
            B_qT = Buf("qT")
            B_qT.w = t_et
            kT = sbb("kT", [64, 4, 256], BF16)
            B_kT = Buf("kT")
            Vaug = sbb("Vaug", [128, 2, 4, 65], BF16)
            B_Va = Buf("Vaug")
            kT_p = sbb("kT_p", [64, 4, 128], BF16)
            Va_p = sbb("Va_p", [128, 4, 65], BF16)
            B_prev = Buf("prev")
            kT_e = sbb("kT_e", [64, 4, 128], BF16)
            Va_e = sbb("Va_e", [128, 4, 65], BF16)
            B_ext = Buf("ext")
            eS = [sbb("eS%d" % i, [128, 512], F32) for i in range(2)]
            B_eS = [Buf("eS0"), Buf("eS1")]
            PB = [sbb("PB%d" % i, [128, 512], BF16) for i in range(4)]
            B_PB = [Buf("PB%d" % i) for i in range(4)]
            osb = sbb("osb", [128, 16, 64], BF16)
            B_osb = Buf("osb")
            oT2 = sbb("oT2", [128, 8, 128], BF16)
            B_oT2 = Buf("oT2")
            rd16 = sbb("rd16", [128, 16], F32)
            B_rd16 = Buf("rd16")

            P.op("dve", lambda e: e.memset(Vaug[:, :, :, 64:65], 1.0), writes=[B_Va])
            P.op("dve", lambda e: e.memset(Va_p[:, :, 64:65], 1.0), writes=[B_prev])
            P.op("dve", lambda e: e.memset(Va_e[:, :, 64:65], 1.0), writes=[B_ext])

            state = {"n": 0}

            def mlp_seq(b0, insert=()):
                seq = [b0]
                for j in range(8):
                    if j + 1 < 8:
                        seq.append(b0 + 2 * (j + 1))
                    if j == 7:
                        seq += list(insert)
                    seq.append(b0 + 2 * j + 1)
                return seq

            slab_plan = []
            slab_plan += [0, 1] + mlp_seq(2) + [18]
            slab_plan += [0, 1]
            for bi_ in range(CH):
                slab_plan += mlp_seq(2) + [18, 19, 20, 21, 22] + mlp_seq(23, insert=([0, 1] if bi_ + 1 < CH else ()))
            issued = {"n": 0}
            slab_ticket = {}

            def issue_slabs(upto):
                while issued["n"] < min(upto, len(slab_plan)):
                    g = issued["n"]
                    r = g % RING
                    P.dma("sp", ring[r][:], wscr[slab_plan[g]], "ring%d" % r, reads=[B_scr], writes=[B_ring[r]])
                    issued["n"] += 1

            def next_slab():
                g = state["n"]
                state["n"] += 1
                issue_slabs(g + 1)
                return g % RING

            def release_and_prefetch():
                issue_slabs(state["n"] + RING - 1)

            def ln_part1a(sl, li):
                P.op("dve", lambda e: e.bn_stats(out=bst[:, sl, 0, :], in_=xs[:, sl, 0:512]), reads=[B_xs[sl]], writes=[B_st[sl]])
                P.op("dve", lambda e: e.bn_stats(out=bst[:, sl, 1, :], in_=xs[:, sl, 512:1024]), reads=[B_xs[sl]], writes=[B_st[sl]])
                P.op("dve", lambda e: e.bn_aggr(out=mv[:, sl, 0:2], in_=bst[:, sl].rearrange("p c (t j) -> p (c t) j", j=3)),
                     writes=[B_st[sl]])
                P.op("act", lambda e: e.activation(out=mv[:, sl, 2:3], in_=mv[:, sl, 1:2], func=AF.Ln, bias=LN_EPS), writes=[B_st[sl]])
                P.op("act", lambda e: e.activation(out=mv[:, sl, 2:3], in_=mv[:, sl, 2:3], func=AF.Exp, scale=-0.5), writes=[B_st[sl]])

            def ln_part1b(sl, li):
                P.op("dve", lambda e: e.tensor_scalar(out=zb[sl][:], in0=xs[:, sl, :], scalar1=mv[:, sl, 0:1], scalar2=mv[:, sl, 2:3],
                                                      op0=ALU.subtract, op1=ALU.mult),
                     reads=[B_st[sl], B_xs[sl]], writes=[B_zb[sl]])

            def ln_part1(sl, li, need_aT=True):
                ln_part1a(sl, li)
                if need_aT:
                    ln_part1b(sl, li)

            def ln_part2(sl, li, need_aT=True):
                if need_aT:
                    bk = 6 + sl
                    transposes([zb[sl][:, c * 128:(c + 1) * 128] for c in range(8)], bk, [B_zb[sl]])
                    for c in range(8):
                        if sl == 0:
                            P.op("act", lambda e, c=c: e.activation(
                                out=aT[:, c, sl * 128:(sl + 1) * 128], in_=bkh(bk)[:, c * 128:(c + 1) * 128], func=AF.Identity,
                                scale=gT[:, li, c:c + 1], bias=bT[:, li, c:c + 1]),
                                reads=BKr(bk) + [B_cB], writes=[B_aTs[sl]])
                        else:
                            P.op("dve", lambda e, c=c: e.tensor_scalar(
                                out=aT[:, c, sl * 128:(sl + 1) * 128], in0=bkh(bk)[:, c * 128:(c + 1) * 128],
                                scalar1=gT[:, li, c:c + 1], scalar2=bT[:, li, c:c + 1], op0=ALU.mult, op1=ALU.add),
                                reads=BKr(bk) + [B_cB], writes=[B_aTs[sl]])
                x_ap = xs[:, sl, :]
                P.op("dve", lambda e: e.scalar_tensor_tensor(out=x_ap, in0=x_ap, scalar=mv[:, sl, 0:1], in1=lng[:, li, :],
                                                             op0=ALU.subtract, op1=ALU.mult),
                     reads=[B_st[sl], B_cB], writes=[B_xs[sl]])
                P.op("dve", lambda e: e.scalar_tensor_tensor(out=x_ap, in0=x_ap, scalar=mv[:, sl, 2:3], in1=lnb[:, li, :],
                                                             op0=ALU.mult, op1=ALU.add),
                     reads=[B_st[sl], B_cB], writes=[B_xs[sl]])

            def ln_stage(nsl, li, need_aT=True):
                for sl in range(nsl):
                    ln_part1a(sl, li)
                if need_aT:
                    for sl in range(nsl):
                        ln_part1b(sl, li)
                for sl in range(nsl):
                    ln_part2(sl, li, need_aT)

            def mlp(nsl, hook=None):
                T = nsl * 128
                accb = [[0, 1], [2, 3]]

                def up(j):
                    ru = next_slab()
                    wu = ring[ru][:].rearrange("p (c n) -> p c n", n=512)
                    hb = hT[j % 2]
                    for fc in range(4):
                        bk = 4 + fc
                        for c in range(8):
                            P.op("pe", lambda e, fc=fc, c=c, bk=bk: e.matmul(
                                bkf(bk)[:, 0:T], wu[:, c, fc * 128:(fc + 1) * 128], aT[:, c, 0:T],
                                start=(c == 0), stop=(c == 7)),
                                reads=[B_ring[ru]] + B_aTs, writes=BKr(bk), sig=(c == 7))
                        src = bkf(bk)[:, 0:T]
                        es = fc % 2
                        P.op("act", lambda e, src=src, es=es: e.activation(out=eS[es][:, 0:T], in_=src, func=AF.Relu),
                             reads=BKr(bk), writes=[B_eS[es]])
                        P.op("dve", lambda e, fc=fc, es=es: e.tensor_tensor(out=hb[:, fc, 0:T], in0=eS[es][:, 0:T], in1=eS[es][:, 0:T], op=ALU.mult),
                             reads=[B_eS[es]], writes=[B_hT[j % 2]])

                def down(j):
                    rd = next_slab()
                    wd = ring[rd][:].rearrange("p (c n) -> p c n", n=1024)
                    hb = hT[j % 2]
                    for sl in range(nsl):
                        for dh in range(2):
                            bk = accb[sl][dh]
                            for fc in range(4):
                                P.op("pe", lambda e, sl=sl, dh=dh, fc=fc, bk=bk: e.matmul(
                                    bkf(bk), hb[:, fc, sl * 128:(sl + 1) * 128], wd[:, fc, dh * 512:(dh + 1) * 512],
                                    start=(j == 0 and fc == 0), stop=(j == 7 and fc == 3)),
                                    reads=[B_hT[j % 2], B_ring[rd]], writes=BKr(bk), sig=(fc == 3))
                    release_and_prefetch()

                up(0)
                for j in range(8):
                    if j + 1 < 8:
                        up(j + 1)
                    if j == 7 and hook is not None:
                        hook()
                    down(j)
                for sl in range(nsl):
                    for dh in range(2):
                        bk = accb[sl][dh]
                        xa = xs[:, sl, dh * 512:(dh + 1) * 512]
                        P.op("dve", lambda e, xa=xa, bk=bk: e.scalar_tensor_tensor(out=xa, in0=xa, scalar=ALPHA, in1=bkf(bk),
                                                                                   op0=ALU.mult, op1=ALU.add),
                             reads=BKr(bk), writes=[B_xs[sl]])

            def swa_kv(nsl, sl_list_global):
                T = nsl * 128
                r = next_slab()
                wk = ring[r][:].rearrange("p (c n) -> p c n", n=512)
                for kh in range(4):
                    for c in range(8):
                        P.op("pe", lambda e, kh=kh, c=c: e.matmul(bkf(4 + kh)[0:64, 0:T],
                                                                   wk[:, c, kh * 64:(kh + 1) * 64], aT[:, c, 0:T],
                                                                   start=(c == 0), stop=(c == 7), skip_group_check=True),
                             reads=[B_ring[r]] + B_aTs, writes=BKr(4 + kh), sig=(c == 7))
                for kh in range(4):
                    P.op("dve", lambda e, kh=kh: e.tensor_copy(out=kT[:, kh, 0:T], in_=bkf(4 + kh)[0:64, 0:T]),
                         reads=BKr(4 + kh), writes=[B_kT])
                for sl in range(nsl):
                    for c in range(8):
                        P.op("pe", lambda e, sl=sl, c=c: e.matmul(bkf(3)[:, sl * 256:(sl + 1) * 256], aT[:, c, sl * 128:(sl + 1) * 128],
                                                                   wk[:, c, 256:512], start=(c == 0), stop=(c == 7), skip_group_check=True),
                             reads=[B_ring[r]] + B_aTs, writes=BKr(3), sig=(c == 7))
                    P.op("act", lambda e, sl=sl: e.copy(out=Vaug[:, sl, :, 0:64],
                                                        in_=bkf(3)[:, sl * 256:(sl + 1) * 256].rearrange("p (k d) -> p k d", k=4)),
                         reads=BKr(3), writes=[B_Va])

            def swa_q(nsl):
                T = nsl * 128
                for half in range(2):
                    r = next_slab()
                    wq = ring[r][:].rearrange("p (c n) -> p c n", n=1024)
                    if half == 0:
                        r0, wq0 = r, wq
                    else:
                        r1, wq1 = r, wq
                for h in range(16):
                    bk = 4 + h % 4
                    colo = 0
                    for c in range(8):
                        wsel = wq0 if c < 4 else wq1
                        rsel = r0 if c < 4 else r1
                        P.op("pe", lambda e, h=h, c=c, wsel=wsel, bk=bk, colo=colo: e.matmul(
                            bkf(bk)[0:64, colo:colo + T], wsel[:, c % 4, h * 64:(h + 1) * 64], aT[:, c, 0:T],
                            start=(c == 0), stop=(c == 7), skip_group_check=True),
                            reads=[B_ring[rsel]] + B_aTs, writes=BKr(bk, colo, colo + T), sig=(c == 7))
                    P.op("act" if h % 2 == 0 else "dve",
                         (lambda e, h=h, bk=bk, colo=colo: e.copy(out=qT[:, h, 0:T], in_=bkf(bk)[0:64, colo:colo + T])) if h % 2 == 0 else
                         (lambda e, h=h, bk=bk, colo=colo: e.tensor_copy(out=qT[:, h, 0:T], in_=bkf(bk)[0:64, colo:colo + T])),
                         reads=BKr(bk, colo, colo + T), writes=[B_qT])
                release_and_prefetch()

            def swa_attn(sl, keys):
                def o_ap(h):
                    return bkf(h // 7)[:, (h % 7) * 65:(h % 7) * 65 + 65]
                first_in_bank = {0: True, 1: True, 2: True}
                nk = len(keys)
                tiles = [(ki, kh) for ki in range(nk) for kh in range(4)]

                def score(t):
                    ki, kh = tiles[t]
                    k_ap, v_ap, et_idx, flag_ap, kbufs = keys[ki]
                    sbk_ = (6, 7, 3)[t % 3]
                    P.op("pe", lambda e: e.matmul(bkf(sbk_), k_ap[:, kh, :], qT[:, kh * 4:kh * 4 + 4, sl * 128:(sl + 1) * 128],
                                                  start=True, stop=True),
                         reads=list(kbufs) + [B_qT], writes=BKr(sbk_))

                def softmax_pv(t):
                    ki, kh = tiles[t]
                    k_ap, v_ap, et_idx, flag_ap, kbufs = keys[ki]
                    sbk_ = (6, 7, 3)[t % 3]
                    es = t % 2
                    pb = t % 4
                    P.op("act", lambda e: e.activation(out=eS[es][:], in_=bkf(sbk_), func=AF.Exp, scale=0.125),
                         reads=BKr(sbk_), writes=[B_eS[es]])
                    et_ap = ET[:, et_idx, kh * 512:(kh + 1) * 512]
                    if flag_ap is None:
                        P.op("dve", lambda e: e.tensor_tensor(out=PB[pb][:], in0=eS[es][:], in1=et_ap, op=ALU.mult),
                             reads=[B_eS[es], B_cB], writes=[B_PB[pb]])
                    else:
                        P.op("dve", lambda e: e.scalar_tensor_tensor(out=PB[pb][:], in0=eS[es][:], scalar=flag_ap, in1=et_ap,
                                                                     op0=ALU.mult, op1=ALU.mult),
                             reads=[B_eS[es], B_cB], writes=[B_PB[pb]])
                    for g in range(4):
                        h = kh * 4 + g
                        bko = h // 7
                        st_ = first_in_bank[bko] and ki == 0
                        if st_:
                            first_in_bank[bko] = False
                        P.op("pe", lambda e, h=h, g=g, st_=st_: e.matmul(
                            o_ap(h), PB[pb][:, g * 128:(g + 1) * 128], v_ap[:, kh, :],
                            start=st_, stop=(ki == nk - 1), skip_group_check=True),
                            reads=list(kbufs) + [B_PB[pb]], writes=BKr(bko), sig=(g == 3))

                nt = len(tiles)
                for t in range(min(3, nt)):
                    score(t)
                for t in range(nt):
                    softmax_pv(t)
                    if t + 3 < nt:
                        score(t + 3)
                for b_ in range(3):
                    nh = 7 if b_ < 2 else 2
                    P.op("dve", lambda e, b_=b_, nh=nh: e.tensor_tensor(
                        out=rd16[:, b_ * 7:b_ * 7 + nh],
                        in0=bkf(b_)[:, 0:nh * 65].rearrange("p (h c) -> p h c", c=65)[:, :, 64],
                        in1=sink_e[:, b_ * 7:b_ * 7 + nh], op=ALU.add),
                        reads=BKr(b_) + [B_cB], writes=[B_rd16])
                P.op("dve", lambda e: e.reciprocal(out=rd16[:], in_=rd16[:]), writes=[B_rd16])
                for b_ in range(3):
                    nh = 7 if b_ < 2 else 2
                    P.op("dve", lambda e, b_=b_, nh=nh: e.tensor_tensor(
                        out=osb[:, b_ * 7:b_ * 7 + nh, :],
                        in0=bkf(b_)[:, 0:nh * 65].rearrange("p (h c) -> p h c", c=65)[:, :, 0:64],
                        in1=rd16[:, b_ * 7:b_ * 7 + nh].unsqueeze(2).to_broadcast([128, nh, 64]), op=ALU.mult),
                        reads=BKr(b_) + [B_rd16], writes=[B_osb])

            def out_proj(lhs_of_chunk, lhs_bufs, r0, r1, banks2):
                for half, r in ((0, r0), (1, r1)):
                    w = ring[r][:].rearrange("p (c n) -> p c n", n=1024)
                    for dh in range(2):
                        for c4 in range(4):
                            c = half * 4 + c4
                            P.op("pe", lambda e, dh=dh, c4=c4, c=c, w=w: e.matmul(
                                bkf(banks2[dh]), lhs_of_chunk(c), w[:, c4, dh * 512:(dh + 1) * 512],
                                start=(c == 0), stop=(c == 7), skip_group_check=True),
                                reads=list(lhs_bufs) + [B_ring[r]], writes=BKr(banks2[dh]), sig=(c4 == 3))

            def residual_from_banks(sl, banks2):
                for dh in range(2):
                    xa = xs[:, sl, dh * 512:(dh + 1) * 512]
                    P.op("dve", lambda e, xa=xa, dh=dh: e.scalar_tensor_tensor(out=xa, in0=xa, scalar=ALPHA, in1=bkf(banks2[dh]),
                                                                               op0=ALU.mult, op1=ALU.add),
                         reads=BKr(banks2[dh]), writes=[B_xs[sl]])

            def set_par(par):
                nonlocal xs, B_xs, mv, bst, B_st
                cur["par"] = par
                xs = xs_all[:, par]
                B_xs = B_xs_all[par]
                mv = mv_all[:, par]
                bst = bst_all[:, par]
                B_st = B_st_all[par]

            def layer0_front(slots):
                nsl = len(slots)
                for sl, s in enumerate(slots):
                    P.dma("sp", xs[:, sl, :], xq[s * 128:(s + 1) * 128, :], "xs%d_%d" % (cur["par"], sl), writes=[B_xs[sl]])
                r0 = next_slab()
                r1 = next_slab()
                pb = [[4, 5], [6, 7]]
                for sl, s in enumerate(slots):
                    out_proj(lambda c, s=s: OT[:, s, c, :], [B_OT[s]], r0, r1, pb[sl])
                release_and_prefetch()
                for sl in range(nsl):
                    residual_from_banks(sl, pb[sl])
                for sl in range(nsl):
                    ln_part1a(sl, 0)
                for sl in range(nsl):
                    ln_part1b(sl, 0)

            def layer0_back(nsl):
                for sl in range(nsl):
                    ln_part2(sl, 0)
                mlp(nsl)
                ln_stage(nsl, 1)

            def save_kv(dst_k, dst_v, sl, bdst):
                P.op("pool", lambda e: e.tensor_copy(out=dst_k[:], in_=kT[:, :, sl * 128:(sl + 1) * 128]), reads=[B_kT], writes=[bdst])
                P.op("pool", lambda e: e.tensor_copy(out=dst_v[:], in_=Vaug[:, sl, :, :]), reads=[B_Va], writes=[bdst])

            set_par(0)
            layer0_front([2 * CH])
            layer0_back(1)
            swa_kv(1, None)
            save_kv(kT_e, Va_e, 0, B_ext)
            release_and_prefetch()

            set_par(1)
            layer0_front([0, 1])
            for bi in range(CH):
                slots = [2 * bi, 2 * bi + 1]
                par_now = (bi + 1) % 2
                set_par(par_now)
                layer0_back(2)
                swa_kv(2, slots)
                swa_q(2)
                for sl, s in enumerate(slots):
                    keys = []
                    if s == 0:
                        keys.append((kT_e[:], Va_e[:], 0, flags[:, 0:1], [B_ext]))
                    elif s == CH:
                        keys.append((kT_e[:], Va_e[:], 0, flags[:, 1:2], [B_ext]))
                        keys.append((kT_p[:], Va_p[:], 0, flags[:, 2:3], [B_prev]))
                    elif sl == 0:
                        keys.append((kT_p[:], Va_p[:], 0, None, [B_prev]))
                    else:
                        keys.append((kT[:, :, 0:128], Vaug[:, 0, :, :], 0, None, [B_kT, B_Va]))
                    keys.append((kT[:, :, sl * 128:(sl + 1) * 128], Vaug[:, sl, :, :], 1, None, [B_kT, B_Va]))
                    swa_attn(sl, keys)
                    transposes([osb[:, 2 * c:2 * c + 2, :].rearrange("p h d -> p (h d)") for c in range(8)], 3, [B_osb])
                    P.op("act", lambda e: e.copy(out=oT2[:].rearrange("p c t -> p (c t)"), in_=bkh(3)), reads=BKr(3), writes=[B_oT2])
                    if sl == 0:
                        r0 = next_slab()
                        r1 = next_slab()
                    out_proj(lambda c: oT2[:, c, :], [B_oT2], r0, r1, [4, 5])
                    residual_from_banks(sl, [4, 5])
                    if SPLIT_LN:
                        ln_part1(sl, 2)
                save_kv(kT_p, Va_p, 1, B_prev)
                release_and_prefetch()
                if SPLIT_LN:
                    for sl in range(2):
                        ln_part2(sl, 2)
                else:
                    ln_stage(2, 2)

                def front_next(bi=bi, par_now=par_now):
                    if bi + 1 < CH:
                        set_par((bi + 2) % 2)
                        layer0_front([2 * bi + 2, 2 * bi + 3])
                        set_par(par_now)
                mlp(2, hook=front_next)
                ln_stage(2, 3, need_aT=False)
                for sl, s in enumerate(slots):
                    P.dma("sp", out[s * 128:(s + 1) * 128, :], xs[:, sl, :], "out%d_%d" % (cur["par"], sl), reads=[B_xs[sl]])
            dbg_dump("xs", xs.rearrange("p s d -> p (s d)"))
            dbg_dump("aT", aT[:].rearrange("p c t -> p (c t)"))
            dbg_dump("osb", osb[:].rearrange("p h d -> p (h d)"))
            dbg_dump("qT", qT[:].rearrange("p h t -> p (h t)"))
            dbg_dump("kT", kT[:].rearrange("p h t -> p (h t)"))
            dbg_dump("ET", ET[:].rearrange("p a b -> p (a b)"))
            for nm in ("out0_0", "out0_1", "out1_0", "out1_1"):
                if nm in P.sem:
                    nc.sync.wait_ge(P.sem[nm], P.cnt[nm])
            P.check()
    return nc


def _t5_bucket_np(dist):
    n = np.maximum(dist, 0)
    max_exact = 16
    nf = np.maximum(n, 1).astype(np.float32)
    large = max_exact + (np.log(nf / np.float32(max_exact)) / np.float32(math.log(128 / max_exact))
                         * np.float32(32 - max_exact)).astype(np.int32)
    large = np.minimum(large, 31)
    return np.where(n < max_exact, n, large)


def _rope_tables(pos):
    half = 32
    inv = (np.float32(10000.0) ** (-(np.arange(half, dtype=np.float32) / np.float32(half)))).astype(np.float32)
    ang = (pos.astype(np.float32)[:, None] * inv[None, :]).astype(np.float32)
    return np.concatenate([np.cos(ang.astype(np.float64)), np.sin(ang.astype(np.float64))], axis=1).astype(np.float32)


_NC_CACHE = {}


def _run(inputs, NB, n_batch):
    CH = NB // 4
    NS = 2 * CH + 1
    S = NB * 128
    bf = ml_dtypes.bfloat16
    f32 = np.float32
    x = np.ascontiguousarray(inputs["x"], dtype=f32)
    assert x.shape == (n_batch, S, D)
    if NB not in _NC_CACHE:
        _NC_CACHE[NB] = build_nc(NB)
    nc = _NC_CACHE[NB]

    ident = np.eye(128, dtype=f32).astype(bf)
    kk = np.arange(128)[:, None]
    qq = np.arange(128)[None, :]
    tri = np.where(kk > qq, NEG, 0.0).astype(f32)
    triT = np.tile(tri, (1, 4)).astype(bf)
    onehot = np.zeros((64, S), dtype=f32)
    for kb in range(NB):
        onehot[kb, kb * 128:(kb + 1) * 128] = 1.0
    onehot = onehot.astype(bf)
    css = _rope_tables(np.arange(S))
    rel_bias = np.asarray(inputs["rel_bias"], dtype=f32)
    ii = np.arange(128)[:, None]
    jj = np.arange(256)[None, :]
    dist = ii + 128 - jj
    bias = rel_bias[_t5_bucket_np(dist)]
    valid = (dist >= 0) & (dist < 128)
    bias = np.where(valid[:, :, None], bias, f32(NEG)).astype(f32)
    biasT = np.ascontiguousarray(bias.transpose(1, 2, 0))
    biasT = biasT.reshape(2, 128, 16 * 128)

    w_uk = np.asarray(inputs["mla_w_uk"], dtype=f32)[0]
    w_ukT = np.ascontiguousarray(w_uk.transpose(2, 1, 0)).reshape(128, H * CKV)
    ln_g = np.stack([inputs["ln_mix_g"][0], inputs["ln_mlp_g"][0], inputs["ln_mix_g"][1], inputs["ln_mlp_g"][1]]).astype(f32)
    ln_b = np.stack([inputs["ln_mix_b"][0], inputs["ln_mlp_b"][0], inputs["ln_mix_b"][1], inputs["ln_mlp_b"][1]]).astype(f32)
    common = {
        "onehot": onehot, "ident": ident, "triT": triT, "css": css,
        "mla_w_in": np.ascontiguousarray(inputs["mla_w_in"][0], dtype=f32),
        "mla_g_q": np.ascontiguousarray(inputs["mla_g_q"][0], dtype=f32),
        "mla_g_kv": np.ascontiguousarray(inputs["mla_g_kv"][0], dtype=f32),
        "mla_w_uq": np.ascontiguousarray(np.concatenate(
            [np.asarray(inputs["mla_w_uq"][0], dtype=f32)[:, :, 0:NOPE].reshape(QR, H * NOPE),
             np.asarray(inputs["mla_w_uq"][0], dtype=f32)[:, :, NOPE:NOPE + RD].reshape(QR, H * RD)], axis=1)),
        "w_ukT": w_ukT,
        "mla_w_uv": np.ascontiguousarray(inputs["mla_w_uv"][0], dtype=f32).reshape(CKV, H * 128),
        "mla_w_o": np.ascontiguousarray(inputs["mla_w_o"][0], dtype=f32),
        "kv_w_shared": np.ascontiguousarray(inputs["kv_w_shared"], dtype=f32),
        "swa_w_q": np.ascontiguousarray(inputs["swa_w_q"][0], dtype=f32),
        "swa_w_o": np.ascontiguousarray(inputs["swa_w_o"][0], dtype=f32),
        "swa_sinks": np.ascontiguousarray(inputs["swa_sinks"][0], dtype=f32),
        "biasT": biasT,
        "mlp_w_up": np.ascontiguousarray(inputs["mlp_w_up"], dtype=f32),
        "mlp_w_down": np.ascontiguousarray(inputs["mlp_w_down"], dtype=f32),
        "ln_g": ln_g, "ln_b": ln_b,
        "ln_gT": np.ascontiguousarray(ln_g.reshape(4, 8, 128).transpose(2, 0, 1)).reshape(128, 32),
        "ln_bT": np.ascontiguousarray(ln_b.reshape(4, 8, 128).transpose(2, 0, 1)).reshape(128, 32),
    }
    xT_all = np.ascontiguousarray(x.reshape(n_batch, NB, 128, 8, 128).transpose(0, 1, 4, 3, 2)).reshape(n_batch, S, D)
    in_maps = []
    for core in range(2 * n_batch):
        b, par = core // 2, core % 2
        blocks = _slot_blocks(par, NB)
        rows = np.concatenate([np.arange(bl * 128, (bl + 1) * 128) for bl in blocks])
        maskrows = np.zeros((NS, 64, 128), dtype=f32)
        for s, bl in enumerate(blocks):
            maskrows[s, bl:, :] = NEG
        flags = np.zeros((128, 4), dtype=f32)
        if par == 0:
            flags[:, 0] = 0.0; flags[:, 1] = 1.0; flags[:, 2] = 0.0
        else:
            flags[:, 0] = 1.0; flags[:, 1] = 0.0; flags[:, 2] = 1.0
        m = dict(common)
        m["xq"] = np.ascontiguousarray(x[b][rows])
        m["xseqT"] = xT_all[b]
        m["xqT"] = np.ascontiguousarray(xT_all[b][rows])
        m["csq"] = np.ascontiguousarray(css[rows])
        m["maskrows"] = maskrows.astype(bf)
        m["swa_flags"] = flags
        in_maps.append(m)
    res = run_bass_kernel_spmd(nc, in_maps, core_ids=list(range(2 * n_batch)))
    if DEBUG:
        global _LAST_RES
        _LAST_RES = res.results
    outp = np.zeros((n_batch, S, D), dtype=f32)
    for core in range(2 * n_batch):
        b, par = core // 2, core % 2
        blocks = _slot_blocks(par, NB)[:2 * CH]
        o = res.results[core]["out"]
        for s, bl in enumerate(blocks):
            outp[b, bl * 128:(bl + 1) * 128] = o[s * 128:(s + 1) * 128]
    return outp


def kernel(**inputs):
    return _run(inputs, 64, 4)
```

```python
import math
import numpy as np
import ml_dtypes
import concourse.bass as bass
import concourse.mybir as mybir
from concourse.bass_utils import run_bass_kernel_spmd

F32 = mybir.dt.float32
BF16 = mybir.dt.bfloat16
AF = mybir.ActivationFunctionType
ALU = mybir.AluOpType
AX = mybir.AxisListType

D = 1024
QR = 384
CKV = 256
RD = 64
H = 8
NOPE = 128
FF = 4096
NEG = -30000.0
ALPHA = 4 ** 0.25
LN_EPS = 1e-5
RMS_EPS = 1e-6
DEBUG = False
DEFER_POST = True
SPLIT_LN = True
NSLAB = 39
SLAB = 4096
RING = 5


class Buf:
    __slots__ = ("w", "r", "name", "excl")

    def __init__(self, name, excl=False):
        self.name = name
        self.w = None
        self.r = {}
        self.excl = excl


class Prog:
    def __init__(self, nc, stack):
        self.nc = nc
        self.eng = {"pe": nc.tensor, "act": nc.scalar, "dve": nc.vector, "pool": nc.gpsimd, "sp": nc.sync}
        self.sem = {}
        self.cnt = {}
        self.pending = {}
        self.waited = {}
        self.stack = stack
        self.log = {e: [] for e in self.eng}
        for e in self.eng:
            self.sem[e] = stack.enter_context(nc.semaphore("pg_" + e))
            self.cnt[e] = 0
            self.pending[e] = False

    def dsem(self, name):
        if name not in self.sem:
            self.sem[name] = self.stack.enter_context(self.nc.semaphore("d_" + name))
            self.cnt[name] = 0
        return name

    def _wait(self, e, t):
        if t is None:
            return
        p, v = t
        if v is None:
            v = self.cnt[p]
        if p == e and e == "pe":
            return
        if self.waited.get((e, p), 0) >= v:
            return
        if p in self.eng:
            assert v <= self.cnt[p], ("deadlock: waiting on unsignalled", e, p, v, self.cnt[p])
        self.eng[e].wait_ge(self.sem[p], v)
        self.log[e].append(("wait", p, v))
        self.waited[(e, p)] = v

    def _deps(self, e, reads, writes):
        for b in reads:
            self._wait(e, b.w)
        for b in writes:
            self._wait(e, b.w)
            for t in b.r.items():
                self._wait(e, t)

    def _mark(self, t, reads, writes):
        for b in reads:
            if t[1] is None:
                b.r[t[0]] = None
            elif b.r.get(t[0], 0) is not None and b.r.get(t[0], 0) < t[1]:
                b.r[t[0]] = t[1]
        for b in writes:
            b.w = t
            b.r = {}

    def op(self, e, fn, reads=(), writes=(), sig=True):
        ex = [b for b in reads if b.excl]
        if ex:
            reads = [b for b in reads if not b.excl]
            writes = list(writes) + ex
        self._deps(e, reads, writes)
        ins = fn(self.eng[e])
        if sig:
            self.cnt[e] += 1
            ins.then_inc(self.sem[e], 1)
            self.log[e].append(("inc", e, 1))
            self.pending[e] = False
            t = (e, self.cnt[e])
        else:
            assert e == "pe"
            self.pending[e] = True
            t = (e, self.cnt[e] + 1)
        self._mark(t, reads, writes)
        return t

    def check(self):
        val = {k: 0 for k in self.sem}
        pos = {e: 0 for e in self.eng}
        progress = True
        while progress:
            progress = False
            for e in self.eng:
                lg = self.log[e]
                while pos[e] < len(lg):
                    kind, p, v = lg[pos[e]]
                    if kind == "wait":
                        if val[p] < v:
                            break
                    else:
                        val[p] += v
                    pos[e] += 1
                    progress = True
        stuck = {e: (pos[e], len(self.log[e]), self.log[e][pos[e]], val[self.log[e][pos[e]][1]])
                 for e in self.eng if pos[e] < len(self.log[e])}
        assert not stuck, ("DEADLOCK", stuck)

    def dma(self, q, out, in_, semname, reads=(), writes=(), group=False):
        self.dsem(semname)
        self._deps(q, reads, writes)
        self.cnt[semname] += 16
        self.eng[q].dma_start(out=out, in_=in_).then_inc(self.sem[semname], 16)
        self.log[q].append(("inc", semname, 16))
        t = (semname, None if group else self.cnt[semname])
        self._mark(t, reads, writes)
        return t


def _slot_blocks(core_parity, NB):
    CH = NB // 4
    if core_parity == 0:
        return list(range(0, CH)) + list(range(3 * CH, 4 * CH)) + [3 * CH - 1]
    return list(range(CH, 2 * CH)) + list(range(2 * CH, 3 * CH)) + [CH - 1]


def _n_off(s, NB):
    CH = NB // 4
    if s < CH:
        return CH + s
    if s < 2 * CH:
        return 3 * CH + (s - CH)
    return 3 * CH - 1


def build_nc(NB):
    CH = NB // 4
    NS = 2 * CH + 1
    S = NB * 128
    nc = bass.Bass("TRN2", target_bir_lowering=False)

    def din(name, shape, dt=F32):
        return nc.dram_tensor(name, list(shape), dt, kind="ExternalInput").ap()

    xq = din("xq", [NS * 128, D])
    xseqT = din("xseqT", [S, D])
    xqT = din("xqT", [NS * 128, D])
    csq = din("csq", [NS * 128, 64])
    css = din("css", [S, 64])
    maskrows = din("maskrows", [NS, 64, 128], BF16)
    onehot = din("onehot", [64, S], BF16)
    ident_d = din("ident", [128, 128], BF16)
    triT_d = din("triT", [128, 512], BF16)
    w_in = din("mla_w_in", [D, QR + CKV + RD])
    g_q = din("mla_g_q", [QR])
    g_kv = din("mla_g_kv", [CKV])
    w_uq = din("mla_w_uq", [QR, H * 192])
    w_ukT = din("w_ukT", [128, H * CKV])
    w_uv = din("mla_w_uv", [CKV, H * 128])
    w_o = din("mla_w_o", [D, D])
    kv_w = din("kv_w_shared", [D, 512])
    swa_wq = din("swa_w_q", [D, D])
    swa_wo = din("swa_w_o", [D, D])
    sinks = din("swa_sinks", [16])
    biasT = din("biasT", [2, 128, 16 * 128])
    swa_flags = din("swa_flags", [128, 4])
    w_up = din("mlp_w_up", [2, D, FF])
    w_down = din("mlp_w_down", [2, FF, D])
    ln_g = din("ln_g", [4, D])
    ln_b = din("ln_b", [4, D])
    ln_gT = din("ln_gT", [128, 32])
    ln_bT = din("ln_bT", [128, 32])
    out = nc.dram_tensor("out", [2 * CH * 128, D], F32, kind="ExternalOutput").ap()
    wscr = nc.dram_tensor("wscr", [NSLAB, 128, SLAB], BF16).ap()
    dbg = {}

    def dbg_dump(name, ap2d):
        if not DEBUG:
            return
        shp = list(ap2d.shape)
        d = nc.dram_tensor("dbg_" + name, shp, ap2d.dtype, kind="ExternalOutput").ap()
        nc.all_engine_barrier()
        P.dma("sp", d, ap2d, "out0_0")
        dbg[name] = d

    from contextlib import ExitStack
    with ExitStack() as st:
        P = Prog(nc, st)

        def sb(name, shape, dt, stack=st):
            return stack.enter_context(nc.sbuf_tensor("s_" + name, list(shape), dt))

        banks = [st.enter_context(nc.psum_tensor("bank%d" % i, [128, 512], F32)) for i in range(8)]
        BQ = [[Buf("bank%d" % i, excl=True)] * 4 for i in range(8)]

        class _BK:
            def __getitem__(self, i):
                return BQ[i]
        BKL = _BK()

        def BKr(i, lo=0, hi=512):
            return [BQ[i][0]]

        def bkf(i):
            return banks[i][:]

        def bkh(i):
            return banks[i][:].bitcast(BF16)

        ident = sb("ident", [128, 128], BF16)
        triT = sb("triT", [128, 512], BF16)
        ones_c = sb("ones_c", [128, 1], BF16)
        OT = sb("OT", [128, NS, 8, 128], BF16)
        B_const = Buf("const")
        B_OT = [Buf("OT%d" % s) for s in range(NS)]

        P.dma("sp", ident[:], ident_d, "c0", writes=[B_const], group=True)
        P.dma("sp", triT[:], triT_d, "c0", writes=[B_const], group=True)
        B_ones = Buf("ones")
        P.op("dve", lambda e: e.memset(ones_c[:], 1.0), writes=[B_ones])

        def transposes(src_aps, bank, reads, n_part_out=128):
            t = None
            hv = bkh(bank)
            for i, a in enumerate(src_aps):
                m = a.shape[-1]
                last = i == len(src_aps) - 1
                t = P.op("pe", lambda e, a=a, i=i, m=m: e.transpose(hv[0:m, i * 128:(i + 1) * 128], a, ident[:]),
                         reads=list(reads) + [B_const], writes=BKr(bank), sig=last)
            return t

        with ExitStack() as sa:
            def sba(name, shape, dt):
                return sb(name, shape, dt, sa)

            KT_c = sba("KT_c", [128, 2, S], BF16)
            KT_r = sba("KT_r", [128, S], BF16)
            Vst = sba("Vst", [128, NB, CKV], BF16)
            w_inq_s = sba("w_inq", [128, 8, QR], BF16)
            w_inkv_s = sba("w_inkv", [128, 8, CKV + RD], BF16)
            w_uq_s = sba("w_uq", [128, 3, H * 192], BF16)
            w_ukT_s = sba("w_ukT", [128, H, CKV], BF16)
            w_uv_s = sba("w_uv", [128, 2, H * 128], BF16)
            gq_rep = sba("gq_rep", [128, QR], F32)
            gkv_rep = sba("gkv_rep", [128, CKV], F32)
            B_wA = Buf("wA")
            B_K = [Buf("K%d" % k) for k in range(NB)]
            B_KTr_const = Buf("ktr_const")

            P.dma("pool", w_inq_s[:], w_in[:, 0:QR].rearrange("(c p) n -> p c n", p=128), "c1", writes=[B_wA], group=True)
            P.dma("pool", w_inkv_s[:], w_in[:, QR:QR + CKV + RD].rearrange("(c p) n -> p c n", p=128), "c1", writes=[B_wA], group=True)
            P.dma("pool", w_uq_s[:], w_uq.rearrange("(c p) n -> p c n", p=128), "c1", writes=[B_wA], group=True)
            P.dma("pool", w_ukT_s[:], w_ukT.rearrange("p (h c) -> p h c", h=H), "c1", writes=[B_wA], group=True)
            P.dma("pool", w_uv_s[:], w_uv.rearrange("(c p) n -> p c n", p=128), "c1", writes=[B_wA], group=True)
            P.dma("sp", gq_rep[:], g_q.partition_broadcast(128), "c0", writes=[B_wA], group=True)
            P.dma("sp", gkv_rep[:], g_kv.partition_broadcast(128), "c0", writes=[B_wA], group=True)
            P.dma("sp", KT_r[64:128, :], onehot, "c0", writes=[B_KTr_const], group=True)

            xb = [sba("xb%d" % i, [128, D], BF16) for i in range(2)]
            xb += [OT[:, 0].rearrange("p h t -> p (h t)"), OT[:, 1].rearrange("p h t -> p (h t)")]
            B_xb = [Buf("xb%d" % i) for i in range(4)]
            cs = [sba("cs%d" % i, [128, 64], F32) for i in range(4)]
            B_cs = [Buf("cs%d" % i) for i in range(4)]
            xT = sba("xT", [128, 8, 128], BF16)
            B_xT = Buf("xT")
            h_sb = sba("h_sb", [128, QR + CKV + RD], F32)
            B_h = Buf("h_sb")
            stat = sba("stat", [128, 8], F32)
            B_stat = Buf("stat")
            rtmp = sba("rtmp", [128, 2, 8, 64], F32)
            B_rtmpA = Buf("rtmpA")
            B_rtmpB = Buf("rtmpB")
            kr_sb = sba("kr_sb", [128, RD], BF16)
            B_kr = Buf("kr")
            cq = sba("cq", [128, QR], BF16)
            B_cq = Buf("cq")
            cqT = sba("cqT", [128, 3, 128], BF16)
            B_cqT = Buf("cqT")
            Vown = sba("Vown", [128, CKV], BF16)
            B_Vown = Buf("Vown")
            KTown_c = sba("KTown_c", [128, 2, 128], BF16)
            KTown_r = sba("KTown_r", [128, 128], BF16)
            B_KTown = Buf("KTown")
            qn = h_sb[:].bitcast(BF16)[:, 0:H * NOPE].rearrange("p (h n) -> p h n", h=H)
            B_qn = B_h
            qr_f = sba("qr_f", [128, H, RD], F32)
            B_qrf = Buf("qr_f")
            junk = qr_f[:].rearrange("p h r -> p (h r)")[:, 0:QR]
            B_junk = B_qrf
            qr = sba("qr", [128, H, RD], BF16)
            B_qr = Buf("qr")
            qnT = xT
            B_qnT = B_xT
            QT_c = sba("QT_c", [128, 2, H, 128], BF16)
            QT_r = sba("QT_r", [128, H, 128], BF16)
            B_QTc = Buf("QT_c")
            B_QTr_lo = Buf("QT_r_lo")
            B_QTr_hi = Buf("QT_r_hi")
            PT = [sba("PT%d" % i, [128, 512], BF16) for i in range(3)]
            B_PT = [Buf("PT%d" % i) for i in range(3)]
            olat = rtmp[:, 0].bitcast(BF16).rearrange("p h f -> p (h f)").rearrange("p (h c) -> p h c", h=4)
            B_olat = B_rtmpA
            olatT = rtmp[:, 1].bitcast(BF16).rearrange("p h f -> p (h f)").rearrange("p (h j t) -> p h j t", h=4, j=2)
            B_olatT = B_rtmpB
            rden = sba("rden", [128, 4], F32)
            B_rden = Buf("rden")

            P.op("dve", lambda e: e.memset(KTown_r[:], 0.0), writes=[B_KTown])

            def rms_scale(src_ap, n, col, eps):
                P.op("dve", lambda e: e.scalar_tensor_tensor(out=junk[:, 0:n], in0=src_ap, scalar=1.0, in1=src_ap,
                                                             op0=ALU.mult, op1=ALU.mult, accum_out=stat[:, col:col + 1]),
                     reads=[B_h], writes=[B_junk, B_stat])
                P.op("act", lambda e: e.activation(out=stat[:, col:col + 1], in_=stat[:, col:col + 1], func=AF.Ln,
                                                   scale=1.0 / n, bias=eps),
                     writes=[B_stat])
                P.op("act", lambda e: e.activation(out=stat[:, col:col + 1], in_=stat[:, col:col + 1], func=AF.Exp,
                                                   scale=-0.5),
                     writes=[B_stat])

            def rope(src3, dst3, nh, csb, bcs, bsrc, bdst):
                s4 = src3.rearrange("p h (t f) -> p h t f", t=2)
                A = rtmp[:, 0, 0:nh, :].rearrange("p h (t f) -> p h t f", t=2)
                Bm = rtmp[:, 1, 0:nh, :].rearrange("p h (t f) -> p h t f", t=2)
                cosb = csb[:, 0:32].unsqueeze(1).unsqueeze(1).to_broadcast([128, nh, 2, 32])
                sinb = csb[:, 32:64].unsqueeze(1).unsqueeze(1).to_broadcast([128, nh, 2, 32])
                P.op("dve", lambda e: e.tensor_tensor(out=A, in0=s4, in1=cosb, op=ALU.mult), reads=[bcs, bsrc], writes=[B_rtmpA])
                P.op("dve", lambda e: e.tensor_tensor(out=Bm, in0=s4, in1=sinb, op=ALU.mult), reads=[bcs, bsrc], writes=[B_rtmpB])
                d4 = dst3.rearrange("p h (t f) -> p h t f", t=2)
                P.op("dve", lambda e: e.tensor_tensor(out=d4[:, :, 0, :], in0=A[:, :, 0, :], in1=Bm[:, :, 1, :], op=ALU.subtract),
                     reads=[B_rtmpA, B_rtmpB], writes=[bdst])
                return P.op("dve", lambda e: e.tensor_tensor(out=d4[:, :, 1, :], in0=Bm[:, :, 0, :], in1=A[:, :, 1, :], op=ALU.add),
                            reads=[B_rtmpA, B_rtmpB], writes=[bdst])

            def load_x_block(src_rows, cs_rows, i):
                P.dma("pool", xb[i][:, :], src_rows, "xb%d" % i, writes=[B_xb[i]])
                P.dma("sp", cs[i][:], cs_rows, "cs%d" % i, writes=[B_cs[i]])

            BX = 0

            def make_xT(i):
                bx = BX
                transposes([xb[i][:, c * 128:(c + 1) * 128] for c in range(8)], bx, [B_xb[i]])
                P.op("act", lambda e: e.copy(out=xT[:].rearrange("p c t -> p (c t)"), in_=bkh(bx)),
                     reads=BKr(bx), writes=[B_xT])

            def kv_from_h(i, v_dst, bv):
                rms_scale(h_sb[:, QR:QR + CKV], CKV, 1, RMS_EPS)
                P.op("dve", lambda e: e.scalar_tensor_tensor(out=v_dst, in0=h_sb[:, QR:QR + CKV], scalar=stat[:, 1:2],
                                                             in1=gkv_rep[:], op0=ALU.mult, op1=ALU.mult),
                     reads=[B_h, B_stat, B_wA], writes=[bv])
                rope(h_sb[:, QR + CKV:QR + CKV + RD].unsqueeze(1), kr_sb[:].unsqueeze(1), 1, cs[i], B_cs[i], B_h, B_kr)

            scr_jobs = []

            def slab_view(idx, inner):
                return wscr[idx].rearrange("p (c n) -> p c n", n=inner)

            k = 0
            for half in range(2):
                scr_jobs.append((slab_view(k, 1024), w_o[half * 512:(half + 1) * 512, :].rearrange("(c p) n -> p c n", p=128)))
                k += 1
            for L in range(2):
                if L == 1:
                    scr_jobs.append((slab_view(k, 512), kv_w.rearrange("(c p) n -> p c n", p=128)))
                    k += 1
                    for wsrc in (swa_wq, swa_wo):
                        for half in range(2):
                            scr_jobs.append((slab_view(k, 1024), wsrc[half * 512:(half + 1) * 512, :].rearrange("(c p) n -> p c n", p=128)))
                            k += 1
                for j in range(8):
                    scr_jobs.append((slab_view(k, 512), w_up[L][:, j * 512:(j + 1) * 512].rearrange("(c p) n -> p c n", p=128)))
                    k += 1
                    scr_jobs.append((slab_view(k, 1024), w_down[L][j * 512:(j + 1) * 512, :].rearrange("(c p) n -> p c n", p=128)))
                    k += 1
            assert k == NSLAB
            B_scr = Buf("scr")

            def issue_scr(n):
                for _ in range(n):
                    if scr_jobs:
                        o, i_ = scr_jobs.pop(0)
                        P.dma("pool", o, i_, "scr", writes=[B_scr])

            NSET = 4
            xTr = [xb[i4][:, :].rearrange("p (c t) -> p c t", c=8) for i4 in range(4)]
            sets = []
            for j in range(NSET):
                if NS >= 2 + 2 * NSET:
                    A_ = OT[:, 2 + 2 * j].rearrange("p h t -> p (h t)").bitcast(F32)
                    B_ = OT[:, 3 + 2 * j].rearrange("p h t -> p (h t)").bitcast(F32)
                else:
                    A_ = sba("p1A%d" % j, [128, 512], F32)[:]
                    B_ = sba("p1B%d" % j, [128, 512], F32)[:]
                sets.append(dict(h=A_[:, 0:320], rA=A_[:, 320:384], rB=A_[:, 384:448], st=A_[:, 448:456],
                                 junk=B_[:, 0:256], kr=B_[:, 256:288].bitcast(BF16),
                                 B_h=Buf("p1h%d" % j), B_st=Buf("p1st%d" % j), B_junk=Buf("p1j%d" % j),
                                 B_r=Buf("p1r%d" % j), B_kr=Buf("p1kr%d" % j), bh=2 * j, bt=2 * j + 1))

            def p1_S0(k):
                i4 = k % 4
                P.dma("pool", xb[i4][:, :], xseqT[k * 128:(k + 1) * 128, :], "xb%d" % i4, writes=[B_xb[i4]])
                P.dma("sp", cs[i4][:], css[k * 128:(k + 1) * 128, :], "cs%d" % i4, writes=[B_cs[i4]])

            def p1_S1(k):
                i4 = k % 4
                W = sets[k % NSET]
                for c in range(8):
                    P.op("pe", lambda e, c=c: e.matmul(bkf(W["bh"])[:, 0:CKV + RD], xTr[i4][:, c, :], w_inkv_s[:, c, :],
                                                       start=(c == 0), stop=(c == 7)),
                         reads=[B_xb[i4], B_wA], writes=BKr(W["bh"]), sig=(c == 7))
                P.op("act", lambda e: e.copy(out=W["h"], in_=bkf(W["bh"])[:, 0:CKV + RD]), reads=BKr(W["bh"]), writes=[W["B_h"]])

            def p1_S2(k):
                W = sets[k % NSET]
                P.op("dve", lambda e: e.scalar_tensor_tensor(out=W["junk"], in0=W["h"][:, 0:CKV], scalar=1.0, in1=W["h"][:, 0:CKV],
                                                             op0=ALU.mult, op1=ALU.mult, accum_out=W["st"][:, 0:1]),
                     reads=[W["B_h"]], writes=[W["B_junk"], W["B_st"]])
                P.op("act", lambda e: e.activation(out=W["st"][:, 0:1], in_=W["st"][:, 0:1], func=AF.Ln, scale=1.0 / CKV, bias=RMS_EPS),
                     writes=[W["B_st"]])
                P.op("act", lambda e: e.activation(out=W["st"][:, 0:1], in_=W["st"][:, 0:1], func=AF.Exp, scale=-0.5), writes=[W["B_st"]])

            def p1_S3(k):
                i4 = k % 4
                W = sets[k % NSET]
                P.op("dve", lambda e: e.scalar_tensor_tensor(out=Vst[:, k, :], in0=W["h"][:, 0:CKV], scalar=W["st"][:, 0:1],
                                                             in1=gkv_rep[:], op0=ALU.mult, op1=ALU.mult),
                     reads=[W["B_h"], W["B_st"], B_wA], writes=[B_K[k]])
                s4 = W["h"][:, CKV:CKV + RD].rearrange("p (t f) -> p t f", t=2)
                A4 = W["rA"].rearrange("p (t f) -> p t f", t=2)
                B4 = W["rB"].rearrange("p (t f) -> p t f", t=2)
                cosb = cs[i4][:, 0:32].unsqueeze(1).to_broadcast([128, 2, 32])
                sinb = cs[i4][:, 32:64].unsqueeze(1).to_broadcast([128, 2, 32])
                P.op("dve", lambda e: e.tensor_tensor(out=A4, in0=s4, in1=cosb, op=ALU.mult), reads=[W["B_h"], B_cs[i4]], writes=[W["B_r"]])
                P.op("dve", lambda e: e.tensor_tensor(out=B4, in0=s4, in1=sinb, op=ALU.mult), reads=[W["B_h"], B_cs[i4]], writes=[W["B_r"]])
                d4 = W["kr"].rearrange("p (t f) -> p t f", t=2)
                P.op("dve", lambda e: e.tensor_tensor(out=d4[:, 0, :], in0=A4[:, 0, :], in1=B4[:, 1, :], op=ALU.subtract),
                     reads=[W["B_r"]], writes=[W["B_kr"]])
                P.op("dve", lambda e: e.tensor_tensor(out=d4[:, 1, :], in0=B4[:, 0, :], in1=A4[:, 1, :], op=ALU.add),
                     reads=[W["B_r"]], writes=[W["B_kr"]])

            def p1_S4(k):
                W = sets[k % NSET]
                transposes([Vst[:, k, 0:128], Vst[:, k, 128:256], W["kr"]], W["bt"], [B_K[k], W["B_kr"]])
                h2 = bkh(W["bt"])
                P.op("act", lambda e: e.copy(out=KT_c[:, :, k * 128:(k + 1) * 128], in_=h2[:, 0:256].rearrange("p (j t) -> p j t", j=2)),
                     reads=BKr(W["bt"]), writes=[B_K[k]])
                P.op("dve", lambda e: e.tensor_copy(out=KT_r[0:64, k * 128:(k + 1) * 128], in_=h2[0:64, 256:384]),
                     reads=BKr(W["bt"]), writes=[B_K[k]])

            for t in range(NB + 3):
                if t < NB:
                    p1_S0(t)
                if 0 <= t - 3 < NB:
                    p1_S4(t - 3)
                if 0 <= t - 2 < NB:
                    p1_S3(t - 2)
                if 0 <= t - 1 < NB:
                    p1_S2(t - 1)
                if t < NB:
                    p1_S1(t)
            nc.all_engine_barrier()
            scr_per_slot = -(-len(scr_jobs) // max(1, min(NS - 1, 20)))

            scale = float((NOPE + RD) ** -0.5)
            for s in range(NS):
                i = s % 2
                n_off = _n_off(s, NB)
                P.dma("pool", xb[i][:, :], xqT[s * 128:(s + 1) * 128, :], "xb%d" % i, writes=[B_xb[i]])
                P.dma("sp", cs[i][:], csq[s * 128:(s + 1) * 128, :], "cs%d" % i, writes=[B_cs[i]])
                xTs = xb[i][:, :].rearrange("p (c t) -> p c t", c=8)
                issue_scr(scr_per_slot)
                P.dma("sp", QT_r[64:128, :, :], maskrows[s].unsqueeze(1).to_broadcast([64, H, 128]), "mrow",
                      writes=[B_QTr_hi])
                for c in range(8):
                    P.op("pe", lambda e, c=c: e.matmul(bkf(1)[:, 0:QR], xTs[:, c, :], w_inq_s[:, c, :],
                                                       start=(c == 0), stop=(c == 7)),
                         reads=[B_xb[i], B_wA], writes=BKr(1), sig=(c == 7))
                for c in range(8):
                    P.op("pe", lambda e, c=c: e.matmul(bkf(2)[:, 0:CKV + RD], xTs[:, c, :], w_inkv_s[:, c, :],
                                                       start=(c == 0), stop=(c == 7)),
                         reads=[B_xb[i], B_wA], writes=BKr(2), sig=(c == 7))
                P.op("act", lambda e: e.copy(out=h_sb[:, 0:QR], in_=bkf(1)[:, 0:QR]), reads=BKr(1), writes=[B_h])
                P.op("act", lambda e: e.copy(out=h_sb[:, QR:QR + CKV + RD], in_=bkf(2)[:, 0:CKV + RD]),
                     reads=BKr(2), writes=[B_h])
                P.op("dve", lambda e: e.scalar_tensor_tensor(out=junk[:, 0:QR], in0=h_sb[:, 0:QR], scalar=1.0 / QR, in1=h_sb[:, 0:QR],
                                                             op0=ALU.mult, op1=ALU.mult, accum_out=stat[:, 0:1]),
                     reads=[B_h], writes=[B_junk, B_stat])
                P.op("dve", lambda e: e.scalar_tensor_tensor(out=junk[:, 0:CKV], in0=h_sb[:, QR:QR + CKV], scalar=1.0 / CKV,
                                                             in1=h_sb[:, QR:QR + CKV], op0=ALU.mult, op1=ALU.mult, accum_out=stat[:, 1:2]),
                     reads=[B_h], writes=[B_junk, B_stat])
                P.op("act", lambda e: e.activation(out=stat[:, 0:2], in_=stat[:, 0:2], func=AF.Ln, bias=RMS_EPS), writes=[B_stat])
                P.op("act", lambda e: e.activation(out=stat[:, 0:2], in_=stat[:, 0:2], func=AF.Exp, scale=-0.5), writes=[B_stat])
                P.op("dve", lambda e: e.scalar_tensor_tensor(out=cq[:], in0=h_sb[:, 0:QR], scalar=stat[:, 0:1],
                                                             in1=gq_rep[:], op0=ALU.mult, op1=ALU.mult),
                     reads=[B_h, B_stat, B_wA], writes=[B_cq])
                P.op("dve", lambda e: e.scalar_tensor_tensor(out=Vown[:], in0=h_sb[:, QR:QR + CKV], scalar=stat[:, 1:2],
                                                             in1=gkv_rep[:], op0=ALU.mult, op1=ALU.mult),
                     reads=[B_h, B_stat, B_wA], writes=[B_Vown])
                rope(h_sb[:, QR + CKV:QR + CKV + RD].unsqueeze(1), kr_sb[:].unsqueeze(1), 1, cs[i], B_cs[i], B_h, B_kr)
                transposes([cq[:, 0:128], cq[:, 128:256], cq[:, 256:384], Vown[:, 0:128], Vown[:, 128:256], kr_sb[:]],
                           0, [B_cq, B_Vown, B_kr])
                h0 = bkh(0)
                P.op("act", lambda e: e.copy(out=cqT[:].rearrange("p c t -> p (c t)"), in_=h0[:, 0:384]),
                     reads=BKr(0), writes=[B_cqT])
                P.op("dve", lambda e: e.tensor_copy(out=KTown_c[:].rearrange("p c t -> p (c t)"), in_=h0[:, 384:640]),
                     reads=BKr(0), writes=[B_KTown])
                P.op("dve", lambda e: e.tensor_copy(out=KTown_r[0:64, :], in_=h0[0:64, 640:768]),
                     reads=BKr(0), writes=[B_KTown])
                for g in range(3):
                    for c in range(3):
                        P.op("pe", lambda e, g=g, c=c: e.matmul(bkf(3 + g), cqT[:, c, :], w_uq_s[:, c, g * 512:(g + 1) * 512],
                                                                start=(c == 0), stop=(c == 2)),
                             reads=[B_cqT, B_wA], writes=BKr(3 + g), sig=(c == 2))
                P.op("act", lambda e: e.copy(out=qn[:, 0:4, :].rearrange("p h n -> p (h n)"), in_=bkf(3)), reads=BKr(3), writes=[B_qn])
                P.op("dve", lambda e: e.tensor_copy(out=qn[:, 4:8, :].rearrange("p h n -> p (h n)"), in_=bkf(4)), reads=BKr(4), writes=[B_qn])
                P.op("act", lambda e: e.copy(out=qr_f[:].rearrange("p h r -> p (h r)"), in_=bkf(5)), reads=BKr(5), writes=[B_qrf])
                rope(qr_f[:], qr[:], H, cs[i], B_cs[i], B_qrf, B_qr)
                transposes([qn[:, h, :] for h in range(H)], 6, [B_qn])
                P.op("act", lambda e: e.copy(out=qnT[:].rearrange("p h t -> p (h t)"), in_=bkh(6)), reads=BKr(6), writes=[B_qnT])
                transposes([qr[:, h, :] for h in range(H)], 7, [B_qr])
                P.op("dve", lambda e: e.tensor_copy(out=QT_r[0:64, :, :].rearrange("p h t -> p (h t)"), in_=bkh(7)[0:64, :]),
                     reads=BKr(7), writes=[B_QTr_lo])
                for j in range(2):
                    for h in range(H):
                        bk = j * 2 + h // 4
                        P.op("pe", lambda e, j=j, h=h, bk=bk: e.matmul(bkf(bk)[:, (h % 4) * 128:(h % 4 + 1) * 128],
                                                                       w_ukT_s[:, h, j * 128:(j + 1) * 128], qnT[:, h, :],
                                                                       start=True, stop=True, skip_group_check=True),
                             reads=[B_qnT, B_wA], writes=BKr(bk), sig=(h % 4 == 3))
                for j in range(2):
                    for hg in range(2):
                        bk = j * 2 + hg
                        eng = "act" if hg == 0 else "dve"
                        dst = QT_c[:, j, hg * 4:(hg + 1) * 4, :].rearrange("p h t -> p (h t)")
                        if eng == "act":
                            P.op("act", lambda e, dst=dst, bk=bk: e.copy(out=dst, in_=bkf(bk)), reads=BKr(bk), writes=[B_QTc])
                        else:
                            P.op("dve", lambda e, dst=dst, bk=bk: e.tensor_copy(out=dst, in_=bkf(bk)), reads=BKr(bk), writes=[B_QTc])

                deferred = []
                for hg in range(2):
                    ob = [0, 1] if hg == 0 else [2, 3]
                    db = 4
                    sring = [5, 6, 7]
                    nun = n_off + 1

                    def s_mm(u):
                        r = sring[u % 3]
                        diag = (u == n_off)
                        hs = slice(hg * 4, hg * 4 + 4)
                        if diag:
                            l0, l1, l2 = KTown_c[:, 0, :], KTown_c[:, 1, :], KTown_r[:]
                            rd = [B_KTown]
                        else:
                            l0 = KT_c[:, 0, u * 128:(u + 1) * 128]
                            l1 = KT_c[:, 1, u * 128:(u + 1) * 128]
                            l2 = KT_r[:, u * 128:(u + 1) * 128]
                            rd = [B_K[u], B_KTr_const]
                        P.op("pe", lambda e: e.matmul(bkf(r), l0, QT_c[:, 0, hs, :].rearrange("p h t -> p (h t)"), start=True, stop=False),
                             reads=rd + [B_QTc], writes=BKr(r), sig=False)
                        P.op("pe", lambda e: e.matmul(bkf(r), l1, QT_c[:, 1, hs, :].rearrange("p h t -> p (h t)"), start=False, stop=False),
                             reads=rd + [B_QTc], writes=BKr(r), sig=False)
                        P.op("pe", lambda e: e.matmul(bkf(r), l2, QT_r[:, hs, :].rearrange("p h t -> p (h t)"), start=False, stop=(not diag)),
                             reads=rd + [B_QTr_lo, B_QTr_hi], writes=BKr(r), sig=(not diag))
                        if diag:
                            P.op("pe", lambda e: e.matmul(bkf(r), ident[:], triT[:], start=False, stop=True),
                                 reads=[B_const], writes=BKr(r), sig=True)

                    def exp_u(u):
                        r = sring[u % 3]
                        pt = u % 3
                        P.op("act", lambda e: e.activation(out=PT[pt][:], in_=bkf(r), func=AF.Exp, scale=scale),
                             reads=BKr(r), writes=[B_PT[pt]])

                    def pv_mm(u):
                        pt = u % 3
                        diag = (u == n_off)
                        last = (u == nun - 1)
                        vsrc = Vown[:] if diag else Vst[:, u, :]
                        rd = [B_Vown] if diag else [B_K[u]]
                        for hh in range(4):
                            bk = ob[hh // 2]
                            col = (hh % 2) * 256
                            P.op("pe", lambda e, hh=hh, bk=bk, col=col: e.matmul(
                                bkf(bk)[:, col:col + 256], PT[pt][:, hh * 128:(hh + 1) * 128], vsrc,
                                start=(u == 0 and hh % 2 == 0), stop=last, skip_group_check=True),
                                reads=rd + [B_PT[pt]], writes=BKr(bk), sig=(last and hh % 2 == 1))
                            P.op("pe", lambda e, hh=hh: e.matmul(
                                bkf(db)[:, hg * 4 + hh:hg * 4 + hh + 1], PT[pt][:, hh * 128:(hh + 1) * 128], ones_c[:],
                                start=(u == 0 and hh == 0), stop=last, skip_group_check=True),
                                reads=[B_PT[pt], B_ones], writes=BKr(db), sig=(hh == 3))

                    s_mm(0)
                    for u in range(nun):
                        exp_u(u)
                        if u + 1 < nun:
                            s_mm(u + 1)
                        pv_mm(u)
                        if deferred and u >= 1:
                            deferred.pop(0)()
                    while deferred:
                        deferred.pop(0)()

                    P.op("dve", lambda e: e.reciprocal(out=rden[:], in_=bkf(db)[:, hg * 4:hg * 4 + 4]),
                         reads=BKr(db), writes=[B_rden])
                    for half in range(2):
                        bk = ob[half]
                        P.op("dve", lambda e, half=half, bk=bk: e.tensor_tensor(
                            out=olat[:, half * 2:half * 2 + 2, :],
                            in0=bkf(bk).rearrange("p (h c) -> p h c", h=2),
                            in1=rden[:, half * 2:half * 2 + 2].unsqueeze(2).to_broadcast([128, 2, CKV]), op=ALU.mult),
                            reads=BKr(bk) + [B_rden], writes=[B_olat])

                    def stage_b(ob=ob):
                        transposes([olat[:, hh, j * 128:(j + 1) * 128] for hh in range(4) for j in range(2)], ob[0], [B_olat])
                        P.op("act", lambda e: e.copy(out=olatT[:].rearrange("p h j t -> p (h j t)"), in_=bkh(ob[0])),
                             reads=BKr(ob[0]), writes=[B_olatT])

                    def stage_c(ob=ob, hg=hg, s=s):
                        for hh in range(4):
                            h = hg * 4 + hh
                            for j in range(2):
                                P.op("pe", lambda e, hh=hh, h=h, j=j: e.matmul(
                                    bkf(ob[1])[:, hh * 128:(hh + 1) * 128], w_uv_s[:, j, h * 128:(h + 1) * 128], olatT[:, hh, j, :],
                                    start=(j == 0), stop=(j == 1), skip_group_check=True),
                                    reads=[B_olatT, B_wA], writes=BKr(ob[1]), sig=(hh == 3 and j == 1))
                        P.op("dve", lambda e: e.tensor_copy(out=OT[:, s, hg * 4:hg * 4 + 4, :].rearrange("p h t -> p (h t)"), in_=bkf(ob[1])),
                             reads=BKr(ob[1]), writes=[B_OT[s]])

                    if hg == 0 and DEFER_POST:
                        stage_b()
                        deferred.extend([stage_c])
                    else:
                        stage_b()
                        stage_c()

            dbg_dump("V", Vst[:].rearrange("p k c -> p (k c)"))
            dbg_dump("KTc", KT_c[:].rearrange("p j s -> p (j s)"))
            dbg_dump("KTr", KT_r[:])
            dbg_dump("OT", OT[:].rearrange("p s h t -> p (s h t)"))
            dbg_dump("QTc", QT_c[:].rearrange("p j h t -> p (j h t)"))
            dbg_dump("QTr", QT_r[:].rearrange("p h t -> p (h t)"))
            dbg_dump("cq", cq[:])
            dbg_dump("qn", qn[:].rearrange("p h n -> p (h n)"))
            dbg_dump("qr", qr[:].rearrange("p h n -> p (h n)"))
            dbg_dump("olat", olat[:].rearrange("p h n -> p (h n)"))
            dbg_dump("rden", rden[:])
            dbg_dump("hsb", h_sb[:])
            dbg_dump("stat", stat[:])

        nc.all_engine_barrier()
        with ExitStack() as sbk:
            def sbb(name, shape, dt):
                return sb(name, shape, dt, sbk)

            ring = [sbb("ring%d" % i, [128, SLAB], BF16) for i in range(RING)]
            B_ring = [Buf("ring%d" % i) for i in range(RING)]
            lng = sbb("lng", [128, 4, D], F32)
            lnb = sbb("lnb", [128, 4, D], F32)
            ET = sbb("ET", [128, 2, 16 * 128], BF16)
            qT_full = sbb("qT", [128, 16, 256], BF16)
            ETf = qT_full[:].rearrange("p h t -> p (h t)").bitcast(F32)
            sink_e = sbb("sink_e", [128, 16], F32)
            flags = sbb("flags", [128, 4], F32)
            B_cB = Buf("constB")
            for L4 in range(4):
                P.dma("sp", lng[:, L4, :], ln_g[L4].partition_broadcast(128), "c2", writes=[B_cB], group=True)
                P.dma("sp", lnb[:, L4, :], ln_b[L4].partition_broadcast(128), "c2", writes=[B_cB], group=True)
            P.dma("sp", sink_e[:], sinks.partition_broadcast(128), "c2", writes=[B_cB], group=True)
            P.dma("sp", flags[:], swa_flags, "c2", writes=[B_cB], group=True)
            for t2 in range(2):
                P.dma("sp", ETf, biasT[t2], "c2", writes=[B_cB], group=True)
                t_et = P.op("act", lambda e, t2=t2: e.activation(out=ET[:, t2, :], in_=ETf, func=AF.Exp), reads=[B_cB], writes=[B_cB])
            P.op("act", lambda e: e.activation(out=sink_e[:], in_=sink_e[:], func=AF.Exp), reads=[B_cB], writes=[B_cB])

            xs_all = sbb("xs", [128, 2, 2, D], F32)
            B_xs_all = [[Buf("xs00"), Buf("xs01")], [Buf("xs10"), Buf("xs11")]]
            xs = xs_all[:, 0]
            B_xs = B_xs_all[0]
            cur = {"par": 0}
            zb = [sbb("zb%d" % i, [128, D], BF16) for i in range(2)]
            B_zb = [Buf("zb0"), Buf("zb1")]
            gT = sbb("gT", [128, 4, 8], F32)
            bT = sbb("bT", [128, 4, 8], F32)
            P.dma("sp", gT[:], ln_gT.rearrange("p (l c) -> p l c", l=4), "c2", writes=[B_cB], group=True)
            P.dma("sp", bT[:], ln_bT.rearrange("p (l c) -> p l c", l=4), "c2", writes=[B_cB], group=True)
            aT = sbb("aT", [128, 8, 256], BF16)
            B_aTs = [Buf("aT0"), Buf("aT1")]
            hT = [sbb("hT%d" % i, [128, 4, 256], BF16) for i in range(2)]
            B_hT = [Buf("hT0"), Buf("hT1")]
            bst_all = sbb("bst", [128, 2, 2, 2, 6], F32)
            mv_all = sbb("mv", [128, 2, 2, 4], F32)
            B_st_all = [[Buf("bst00"), Buf("bst01")], [Buf("bst10"), Buf("bst11")]]
            bst = bst_all[:, 0]
            mv = mv_all[:, 0]
            B_st = B_st_all[0]
            qT = qT_full[0:64]
            B_qT = Buf("qT")
            B_qT.w = t_et
            kT = sbb("kT", [64, 4, 256], BF16)
            B_kT = Buf("kT")
            Vaug = sbb("Vaug", [128, 2, 4, 65], BF16)
            B_Va = Buf("Vaug")
            kT_p = sbb("kT_p", [64, 4, 128], BF16)
            Va_p = sbb("Va_p", [128, 4, 65], BF16)
            B_prev = Buf("prev")
            kT_e = sbb("kT_e", [64, 4, 128], BF16)
            Va_e = sbb("Va_e", [128, 4, 65], BF16)
            B_ext = Buf("ext")
            eS = [sbb("eS%d" % i, [128, 512], F32) for i in range(2)]
            B_eS = [Buf("eS0"), Buf("eS1")]
            PB = [sbb("PB%d" % i, [128, 512], BF16) for i in range(4)]
            B_PB = [Buf("PB%d" % i) for i in range(4)]
            osb = sbb("osb", [128, 16, 64], BF16)
            B_osb = Buf("osb")
            oT2 = sbb("oT2", [128, 8, 128], BF16)
            B_oT2 = Buf("oT2")
            rd16 = sbb("rd16", [128, 16], F32)
            B_rd16 = Buf("rd16")

            P.op("dve", lambda e: e.memset(Vaug[:, :, :, 64:65], 1.0), writes=[B_Va])
            P.op("dve", lambda e: e.memset(Va_p[:, :, 64:65], 1.0), writes=[B_prev])
            P.op("dve", lambda e: e.memset(Va_e[:, :, 64:65], 1.0), writes=[B_ext])

            state = {"n": 0}

            def mlp_seq(b0, insert=()):
                seq = [b0]
                for j in range(8):
                    if j + 1 < 8:
                        seq.append(b0 + 2 * (j + 1))
                    if j == 7:
                        seq += list(insert)
                    seq.append(b0 + 2 * j + 1)
                return seq

            slab_plan = []
            slab_plan += [0, 1] + mlp_seq(2) + [18]
            slab_plan += [0, 1]
            for bi_ in range(CH):
                slab_plan += mlp_seq(2) + [18, 19, 20, 21, 22] + mlp_seq(23, insert=([0, 1] if bi_ + 1 < CH else ()))
            issued = {"n": 0}
            slab_ticket = {}

            def issue_slabs(upto):
                while issued["n"] < min(upto, len(slab_plan)):
                    g = issued["n"]
                    r = g % RING
                    P.dma("sp", ring[r][:], wscr[slab_plan[g]], "ring%d" % r, reads=[B_scr], writes=[B_ring[r]])
                    issued["n"] += 1

            def next_slab():
                g = state["n"]
                state["n"] += 1
                issue_slabs(g + 1)
                return g % RING

            def release_and_prefetch():
                issue_slabs(state["n"] + RING - 1)

            def ln_part1a(sl, li):
                P.op("dve", lambda e: e.bn_stats(out=bst[:, sl, 0, :], in_=xs[:, sl, 0:512]), reads=[B_xs[sl]], writes=[B_st[sl]])
                P.op("dve", lambda e: e.bn_stats(out=bst[:, sl, 1, :], in_=xs[:, sl, 512:1024]), reads=[B_xs[sl]], writes=[B_st[sl]])
                P.op("dve", lambda e: e.bn_aggr(out=mv[:, sl, 0:2], in_=bst[:, sl].rearrange("p c (t j) -> p (c t) j", j=3)),
                     writes=[B_st[sl]])
                P.op("act", lambda e: e.activation(out=mv[:, sl, 2:3], in_=mv[:, sl, 1:2], func=AF.Ln, bias=LN_EPS), writes=[B_st[sl]])
                P.op("act", lambda e: e.activation(out=mv[:, sl, 2:3], in_=mv[:, sl, 2:3], func=AF.Exp, scale=-0.5), writes=[B_st[sl]])

            def ln_part1b(sl, li):
                P.op("dve", lambda e: e.tensor_scalar(out=zb[sl][:], in0=xs[:, sl, :], scalar1=mv[:, sl, 0:1], scalar2=mv[:, sl, 2:3],
                                                      op0=ALU.subtract, op1=ALU.mult),
                     reads=[B_st[sl], B_xs[sl]], writes=[B_zb[sl]])

            def ln_part1(sl, li, need_aT=True):
                ln_part1a(sl, li)
                if need_aT:
                    ln_part1b(sl, li)

            def ln_part2(sl, li, need_aT=True, do_a=True, do_b=True):
                if need_aT and do_a:
                    bk = 6 + sl
                    transposes([zb[sl][:, c * 128:(c + 1) * 128] for c in range(8)], bk, [B_zb[sl]])
                    for c in range(8):
                        if sl == 0:
                            P.op("act", lambda e, c=c: e.activation(
                                out=aT[:, c, sl * 128:(sl + 1) * 128], in_=bkh(bk)[:, c * 128:(c + 1) * 128], func=AF.Identity,
                                scale=gT[:, li, c:c + 1], bias=bT[:, li, c:c + 1]),
                                reads=BKr(bk) + [B_cB], writes=[B_aTs[sl]])
                        else:
                            P.op("dve", lambda e, c=c: e.tensor_scalar(
                                out=aT[:, c, sl * 128:(sl + 1) * 128], in0=bkh(bk)[:, c * 128:(c + 1) * 128],
                                scalar1=gT[:, li, c:c + 1], scalar2=bT[:, li, c:c + 1], op0=ALU.mult, op1=ALU.add),
                                reads=BKr(bk) + [B_cB], writes=[B_aTs[sl]])
                if not do_b:
                    return
                x_ap = xs[:, sl, :]
                P.op("dve", lambda e: e.scalar_tensor_tensor(out=x_ap, in0=x_ap, scalar=mv[:, sl, 0:1], in1=lng[:, li, :],
                                                             op0=ALU.subtract, op1=ALU.mult),
                     reads=[B_st[sl], B_cB], writes=[B_xs[sl]])
                P.op("dve", lambda e: e.scalar_tensor_tensor(out=x_ap, in0=x_ap, scalar=mv[:, sl, 2:3], in1=lnb[:, li, :],
                                                             op0=ALU.mult, op1=ALU.add),
                     reads=[B_st[sl], B_cB], writes=[B_xs[sl]])

            def ln_stage(nsl, li, need_aT=True):
                for sl in range(nsl):
                    ln_part1a(sl, li)
                if need_aT:
                    for sl in range(nsl):
                        ln_part1b(sl, li)
                for sl in range(nsl):
                    ln_part2(sl, li, need_aT)

            def mlp(nsl, hook=None):
                T = nsl * 128
                accb = [[0, 1], [2, 3]]

                def up(j):
                    ru = next_slab()
                    wu = ring[ru][:].rearrange("p (c n) -> p c n", n=512)
                    hb = hT[j % 2]
                    for fc in range(4):
                        bk = 4 + fc
                        for c in range(8):
                            P.op("pe", lambda e, fc=fc, c=c, bk=bk: e.matmul(
                                bkf(bk)[:, 0:T], wu[:, c, fc * 128:(fc + 1) * 128], aT[:, c, 0:T],
                                start=(c == 0), stop=(c == 7)),
                                reads=[B_ring[ru]] + B_aTs, writes=BKr(bk), sig=(c == 7))
                        src = bkf(bk)[:, 0:T]
                        es = fc % 2
                        P.op("act", lambda e, src=src, es=es: e.activation(out=eS[es][:, 0:T], in_=src, func=AF.Relu),
                             reads=BKr(bk), writes=[B_eS[es]])
                        P.op("dve", lambda e, fc=fc, es=es: e.tensor_tensor(out=hb[:, fc, 0:T], in0=eS[es][:, 0:T], in1=eS[es][:, 0:T], op=ALU.mult),
                             reads=[B_eS[es]], writes=[B_hT[j % 2]])

                def down(j):
                    rd = next_slab()
                    wd = ring[rd][:].rearrange("p (c n) -> p c n", n=1024)
                    hb = hT[j % 2]
                    for sl in range(nsl):
                        for dh in range(2):
                            bk = accb[sl][dh]
                            for fc in range(4):
                                P.op("pe", lambda e, sl=sl, dh=dh, fc=fc, bk=bk: e.matmul(
                                    bkf(bk), hb[:, fc, sl * 128:(sl + 1) * 128], wd[:, fc, dh * 512:(dh + 1) * 512],
                                    start=(j == 0 and fc == 0), stop=(j == 7 and fc == 3)),
                                    reads=[B_hT[j % 2], B_ring[rd]], writes=BKr(bk), sig=(fc == 3))
                    release_and_prefetch()

                up(0)
                for j in range(8):
                    if j + 1 < 8:
                        up(j + 1)
                    if j == 7 and hook is not None:
                        hook()
                    down(j)
                for sl in range(nsl):
                    for dh in range(2):
                        bk = accb[sl][dh]
                        xa = xs[:, sl, dh * 512:(dh + 1) * 512]
                        P.op("dve", lambda e, xa=xa, bk=bk: e.scalar_tensor_tensor(out=xa, in0=xa, scalar=ALPHA, in1=bkf(bk),
                                                                                   op0=ALU.mult, op1=ALU.add),
                             reads=BKr(bk), writes=[B_xs[sl]])

            def swa_kv(nsl, sl_list_global):
                T = nsl * 128
                r = next_slab()
                wk = ring[r][:].rearrange("p (c n) -> p c n", n=512)
                for kh in range(4):
                    for c in range(8):
                        P.op("pe", lambda e, kh=kh, c=c: e.matmul(bkf(4 + kh)[0:64, 0:T],
                                                                   wk[:, c, kh * 64:(kh + 1) * 64], aT[:, c, 0:T],
                                                                   start=(c == 0), stop=(c == 7), skip_group_check=True),
                             reads=[B_ring[r]] + B_aTs, writes=BKr(4 + kh), sig=(c == 7))
                for kh in range(4):
                    P.op("dve", lambda e, kh=kh: e.tensor_copy(out=kT[:, kh, 0:T], in_=bkf(4 + kh)[0:64, 0:T]),
                         reads=BKr(4 + kh), writes=[B_kT])
                for sl in range(nsl):
                    for c in range(8):
                        P.op("pe", lambda e, sl=sl, c=c: e.matmul(bkf(3)[:, sl * 256:(sl + 1) * 256], aT[:, c, sl * 128:(sl + 1) * 128],
                                                                   wk[:, c, 256:512], start=(c == 0), stop=(c == 7), skip_group_check=True),
                             reads=[B_ring[r]] + B_aTs, writes=BKr(3), sig=(c == 7))
                    P.op("act", lambda e, sl=sl: e.copy(out=Vaug[:, sl, :, 0:64],
                                                        in_=bkf(3)[:, sl * 256:(sl + 1) * 256].rearrange("p (k d) -> p k d", k=4)),
                         reads=BKr(3), writes=[B_Va])

            def swa_q(nsl):
                T = nsl * 128
                for half in range(2):
                    r = next_slab()
                    wq = ring[r][:].rearrange("p (c n) -> p c n", n=1024)
                    if half == 0:
                        r0, wq0 = r, wq
                    else:
                        r1, wq1 = r, wq
                for h in range(16):
                    bk = 4 + h % 4
                    colo = 0
                    for c in range(8):
                        wsel = wq0 if c < 4 else wq1
                        rsel = r0 if c < 4 else r1
                        P.op("pe", lambda e, h=h, c=c, wsel=wsel, bk=bk, colo=colo: e.matmul(
                            bkf(bk)[0:64, colo:colo + T], wsel[:, c % 4, h * 64:(h + 1) * 64], aT[:, c, 0:T],
                            start=(c == 0), stop=(c == 7), skip_group_check=True),
                            reads=[B_ring[rsel]] + B_aTs, writes=BKr(bk, colo, colo + T), sig=(c == 7))
                    P.op("act" if h % 2 == 0 else "dve",
                         (lambda e, h=h, bk=bk, colo=colo: e.copy(out=qT[:, h, 0:T], in_=bkf(bk)[0:64, colo:colo + T])) if h % 2 == 0 else
                         (lambda e, h=h, bk=bk, colo=colo: e.tensor_copy(out=qT[:, h, 0:T], in_=bkf(bk)[0:64, colo:colo + T])),
                         reads=BKr(bk, colo, colo + T), writes=[B_qT])
                release_and_prefetch()

            def swa_attn(sl, keys):
                def o_ap(h):
                    return bkf(h // 7)[:, (h % 7) * 65:(h % 7) * 65 + 65]
                first_in_bank = {0: True, 1: True, 2: True}
                nk = len(keys)
                tiles = [(ki, kh) for ki in range(nk) for kh in range(4)]

                def score(t):
                    ki, kh = tiles[t]
                    k_ap, v_ap, et_idx, flag_ap, kbufs = keys[ki]
                    sbk_ = (6, 7, 3)[t % 3]
                    P.op("pe", lambda e: e.matmul(bkf(sbk_), k_ap[:, kh, :], qT[:, kh * 4:kh * 4 + 4, sl * 128:(sl + 1) * 128],
                                                  start=True, stop=True),
                         reads=list(kbufs) + [B_qT], writes=BKr(sbk_))

                def softmax_pv(t):
                    ki, kh = tiles[t]
                    k_ap, v_ap, et_idx, flag_ap, kbufs = keys[ki]
                    sbk_ = (6, 7, 3)[t % 3]
                    es = t % 2
                    pb = t % 4
                    P.op("act", lambda e: e.activation(out=eS[es][:], in_=bkf(sbk_), func=AF.Exp, scale=0.125),
                         reads=BKr(sbk_), writes=[B_eS[es]])
                    et_ap = ET[:, et_idx, kh * 512:(kh + 1) * 512]
                    if flag_ap is None:
                        P.op("dve", lambda e: e.tensor_tensor(out=PB[pb][:], in0=eS[es][:], in1=et_ap, op=ALU.mult),
                             reads=[B_eS[es], B_cB], writes=[B_PB[pb]])
                    else:
                        P.op("dve", lambda e: e.scalar_tensor_tensor(out=PB[pb][:], in0=eS[es][:], scalar=flag_ap, in1=et_ap,
                                                                     op0=ALU.mult, op1=ALU.mult),
                             reads=[B_eS[es], B_cB], writes=[B_PB[pb]])
                    for g in range(4):
                        h = kh * 4 + g
                        bko = h // 7
                        st_ = first_in_bank[bko] and ki == 0
                        if st_:
                            first_in_bank[bko] = False
                        P.op("pe", lambda e, h=h, g=g, st_=st_: e.matmul(
                            o_ap(h), PB[pb][:, g * 128:(g + 1) * 128], v_ap[:, kh, :],
                            start=st_, stop=(ki == nk - 1), skip_group_check=True),
                            reads=list(kbufs) + [B_PB[pb]], writes=BKr(bko), sig=(g == 3))

                nt = len(tiles)
                for t in range(min(3, nt)):
                    score(t)
                for t in range(nt):
                    softmax_pv(t)
                    if t + 3 < nt:
                        score(t + 3)
                for b_ in range(3):
                    nh = 7 if b_ < 2 else 2
                    P.op("dve", lambda e, b_=b_, nh=nh: e.tensor_tensor(
                        out=rd16[:, b_ * 7:b_ * 7 + nh],
                        in0=bkf(b_)[:, 0:nh * 65].rearrange("p (h c) -> p h c", c=65)[:, :, 64],
                        in1=sink_e[:, b_ * 7:b_ * 7 + nh], op=ALU.add),
                        reads=BKr(b_) + [B_cB], writes=[B_rd16])
                P.op("dve", lambda e: e.reciprocal(out=rd16[:], in_=rd16[:]), writes=[B_rd16])
                for b_ in range(3):
                    nh = 7 if b_ < 2 else 2
                    P.op("dve", lambda e, b_=b_, nh=nh: e.tensor_tensor(
                        out=osb[:, b_ * 7:b_ * 7 + nh, :],
                        in0=bkf(b_)[:, 0:nh * 65].rearrange("p (h c) -> p h c", c=65)[:, :, 0:64],
                        in1=rd16[:, b_ * 7:b_ * 7 + nh].unsqueeze(2).to_broadcast([128, nh, 64]), op=ALU.mult),
                        reads=BKr(b_) + [B_rd16], writes=[B_osb])

            def out_proj(lhs_of_chunk, lhs_bufs, r0, r1, banks2):
                for half, r in ((0, r0), (1, r1)):
                    w = ring[r][:].rearrange("p (c n) -> p c n", n=1024)
                    for dh in range(2):
                        for c4 in range(4):
                            c = half * 4 + c4
                            P.op("pe", lambda e, dh=dh, c4=c4, c=c, w=w: e.matmul(
                                bkf(banks2[dh]), lhs_of_chunk(c), w[:, c4, dh * 512:(dh + 1) * 512],
                                start=(c == 0), stop=(c == 7), skip_group_check=True),
                                reads=list(lhs_bufs) + [B_ring[r]], writes=BKr(banks2[dh]), sig=(c4 == 3))

            def residual_from_banks(sl, banks2):
                for dh in range(2):
                    xa = xs[:, sl, dh * 512:(dh + 1) * 512]
                    P.op("dve", lambda e, xa=xa, dh=dh: e.scalar_tensor_tensor(out=xa, in0=xa, scalar=ALPHA, in1=bkf(banks2[dh]),
                                                                               op0=ALU.mult, op1=ALU.add),
                         reads=BKr(banks2[dh]), writes=[B_xs[sl]])

            def set_par(par):
                nonlocal xs, B_xs, mv, bst, B_st
                cur["par"] = par
                xs = xs_all[:, par]
                B_xs = B_xs_all[par]
                mv = mv_all[:, par]
                bst = bst_all[:, par]
                B_st = B_st_all[par]

            def layer0_front(slots):
                nsl = len(slots)
                for sl, s in enumerate(slots):
                    P.dma("sp", xs[:, sl, :], xq[s * 128:(s + 1) * 128, :], "xs%d_%d" % (cur["par"], sl), writes=[B_xs[sl]])
                r0 = next_slab()
                r1 = next_slab()
                pb = [[4, 5], [6, 7]]
                for sl, s in enumerate(slots):
                    out_proj(lambda c, s=s: OT[:, s, c, :], [B_OT[s]], r0, r1, pb[sl])
                release_and_prefetch()
                for sl in range(nsl):
                    residual_from_banks(sl, pb[sl])
                for sl in range(nsl):
                    ln_part1a(sl, 0)
                for sl in range(nsl):
                    ln_part1b(sl, 0)

            def layer0_back(nsl, a_done=False):
                for sl in range(nsl):
                    ln_part2(sl, 0, do_a=not a_done)
                mlp(nsl)
                ln_stage(nsl, 1)

            def save_kv(dst_k, dst_v, sl, bdst):
                P.op("pool", lambda e: e.tensor_copy(out=dst_k[:], in_=kT[:, :, sl * 128:(sl + 1) * 128]), reads=[B_kT], writes=[bdst])
                P.op("pool", lambda e: e.tensor_copy(out=dst_v[:], in_=Vaug[:, sl, :, :]), reads=[B_Va], writes=[bdst])

            set_par(0)
            layer0_front([2 * CH])
            layer0_back(1)
            swa_kv(1, None)
            save_kv(kT_e, Va_e, 0, B_ext)
            release_and_prefetch()

            set_par(1)
            layer0_front([0, 1])
            for bi in range(CH):
                slots = [2 * bi, 2 * bi + 1]
                par_now = (bi + 1) % 2
                set_par(par_now)
                layer0_back(2, a_done=(bi > 0))
                swa_kv(2, slots)
                swa_q(2)
                for sl, s in enumerate(slots):
                    keys = []
                    if s == 0:
                        keys.append((kT_e[:], Va_e[:], 0, flags[:, 0:1], [B_ext]))
                    elif s == CH:
                        keys.append((kT_e[:], Va_e[:], 0, flags[:, 1:2], [B_ext]))
                        keys.append((kT_p[:], Va_p[:], 0, flags[:, 2:3], [B_prev]))
                    elif sl == 0:
                        keys.append((kT_p[:], Va_p[:], 0, None, [B_prev]))
                    else:
                        keys.append((kT[:, :, 0:128], Vaug[:, 0, :, :], 0, None, [B_kT, B_Va]))
                    keys.append((kT[:, :, sl * 128:(sl + 1) * 128], Vaug[:, sl, :, :], 1, None, [B_kT, B_Va]))
                    swa_attn(sl, keys)
                    transposes([osb[:, 2 * c:2 * c + 2, :].rearrange("p h d -> p (h d)") for c in range(8)], 3, [B_osb])
                    P.op("act", lambda e: e.copy(out=oT2[:].rearrange("p c t -> p (c t)"), in_=bkh(3)), reads=BKr(3), writes=[B_oT2])
                    if sl == 0:
                        r0 = next_slab()
                        r1 = next_slab()
                    out_proj(lambda c: oT2[:, c, :], [B_oT2], r0, r1, [4, 5])
                    residual_from_banks(sl, [4, 5])
                    if SPLIT_LN:
                        ln_part1(sl, 2)
                save_kv(kT_p, Va_p, 1, B_prev)
                release_and_prefetch()
                if SPLIT_LN:
                    for sl in range(2):
                        ln_part2(sl, 2)
                else:
                    ln_stage(2, 2)

                def front_next(bi=bi, par_now=par_now):
                    if bi + 1 < CH:
                        set_par((bi + 2) % 2)
                        layer0_front([2 * bi + 2, 2 * bi + 3])
                        set_par(par_now)
                mlp(2, hook=front_next)
                if bi + 1 < CH:
                    set_par((bi + 2) % 2)
                    for sl in range(2):
                        ln_part2(sl, 0, do_b=False)
                    set_par(par_now)
                ln_stage(2, 3, need_aT=False)
                for sl, s in enumerate(slots):
                    P.dma("sp", out[s * 128:(s + 1) * 128, :], xs[:, sl, :], "out%d_%d" % (cur["par"], sl), reads=[B_xs[sl]])
            dbg_dump("xs", xs.rearrange("p s d -> p (s d)"))
            dbg_dump("aT", aT[:].rearrange("p c t -> p (c t)"))
            dbg_dump("osb", osb[:].rearrange("p h d -> p (h d)"))
            dbg_dump("qT", qT[:].rearrange("p h t -> p (h t)"))
            dbg_dump("kT", kT[:].rearrange("p h t -> p (h t)"))
            dbg_dump("ET", ET[:].rearrange("p a b -> p (a b)"))
            for nm in ("out0_0", "out0_1", "out1_0", "out1_1"):
                if nm in P.sem:
                    nc.sync.wait_ge(P.sem[nm], P.cnt[nm])
            P.check()
    return nc


def _t5_bucket_np(dist):
    n = np.maximum(dist, 0)
    max_exact = 16
    nf = np.maximum(n, 1).astype(np.float32)
    large = max_exact + (np.log(nf / np.float32(max_exact)) / np.float32(math.log(128 / max_exact))
                         * np.float32(32 - max_exact)).astype(np.int32)
    large = np.minimum(large, 31)
    return np.where(n < max_exact, n, large)


def _rope_tables(pos):
    half = 32
    inv = (np.float32(10000.0) ** (-(np.arange(half, dtype=np.float32) / np.float32(half)))).astype(np.float32)
    ang = (pos.astype(np.float32)[:, None] * inv[None, :]).astype(np.float32)
    return np.concatenate([np.cos(ang.astype(np.float64)), np.sin(ang.astype(np.float64))], axis=1).astype(np.float32)


_NC_CACHE = {}


def _run(inputs, NB, n_batch):
    CH = NB // 4
    NS = 2 * CH + 1
    S = NB * 128
    bf = ml_dtypes.bfloat16
    f32 = np.float32
    x = np.ascontiguousarray(inputs["x"], dtype=f32)
    assert x.shape == (n_batch, S, D)
    if NB not in _NC_CACHE:
        _NC_CACHE[NB] = build_nc(NB)
    nc = _NC_CACHE[NB]

    ident = np.eye(128, dtype=f32).astype(bf)
    kk = np.arange(128)[:, None]
    qq = np.arange(128)[None, :]
    tri = np.where(kk > qq, NEG, 0.0).astype(f32)
    triT = np.tile(tri, (1, 4)).astype(bf)
    onehot = np.zeros((64, S), dtype=f32)
    for kb in range(NB):
        onehot[kb, kb * 128:(kb + 1) * 128] = 1.0
    onehot = onehot.astype(bf)
    css = _rope_tables(np.arange(S))
    rel_bias = np.asarray(inputs["rel_bias"], dtype=f32)
    ii = np.arange(128)[:, None]
    jj = np.arange(256)[None, :]
    dist = ii + 128 - jj
    bias = rel_bias[_t5_bucket_np(dist)]
    valid = (dist >= 0) & (dist < 128)
    bias = np.where(valid[:, :, None], bias, f32(NEG)).astype(f32)
    biasT = np.ascontiguousarray(bias.transpose(1, 2, 0))
    biasT = biasT.reshape(2, 128, 16 * 128)

    w_uk = np.asarray(inputs["mla_w_uk"], dtype=f32)[0]
    w_ukT = np.ascontiguousarray(w_uk.transpose(2, 1, 0)).reshape(128, H * CKV)
    ln_g = np.stack([inputs["ln_mix_g"][0], inputs["ln_mlp_g"][0], inputs["ln_mix_g"][1], inputs["ln_mlp_g"][1]]).astype(f32)
    ln_b = np.stack([inputs["ln_mix_b"][0], inputs["ln_mlp_b"][0], inputs["ln_mix_b"][1], inputs["ln_mlp_b"][1]]).astype(f32)
    common = {
        "onehot": onehot, "ident": ident, "triT": triT, "css": css,
        "mla_w_in": np.ascontiguousarray(inputs["mla_w_in"][0], dtype=f32),
        "mla_g_q": np.ascontiguousarray(inputs["mla_g_q"][0], dtype=f32),
        "mla_g_kv": np.ascontiguousarray(inputs["mla_g_kv"][0], dtype=f32),
        "mla_w_uq": np.ascontiguousarray(np.concatenate(
            [np.asarray(inputs["mla_w_uq"][0], dtype=f32)[:, :, 0:NOPE].reshape(QR, H * NOPE),
             np.asarray(inputs["mla_w_uq"][0], dtype=f32)[:, :, NOPE:NOPE + RD].reshape(QR, H * RD)], axis=1)),
        "w_ukT": w_ukT,
        "mla_w_uv": np.ascontiguousarray(inputs["mla_w_uv"][0], dtype=f32).reshape(CKV, H * 128),
        "mla_w_o": np.ascontiguousarray(inputs["mla_w_o"][0], dtype=f32),
        "kv_w_shared": np.ascontiguousarray(inputs["kv_w_shared"], dtype=f32),
        "swa_w_q": np.ascontiguousarray(inputs["swa_w_q"][0], dtype=f32),
        "swa_w_o": np.ascontiguousarray(inputs["swa_w_o"][0], dtype=f32),
        "swa_sinks": np.ascontiguousarray(inputs["swa_sinks"][0], dtype=f32),
        "biasT": biasT,
        "mlp_w_up": np.ascontiguousarray(inputs["mlp_w_up"], dtype=f32),
        "mlp_w_down": np.ascontiguousarray(inputs["mlp_w_down"], dtype=f32),
        "ln_g": ln_g, "ln_b": ln_b,
        "ln_gT": np.ascontiguousarray(ln_g.reshape(4, 8, 128).transpose(2, 0, 1)).reshape(128, 32),
        "ln_bT": np.ascontiguousarray(ln_b.reshape(4, 8, 128).transpose(2, 0, 1)).reshape(128, 32),
    }
    xT_all = np.ascontiguousarray(x.reshape(n_batch, NB, 128, 8, 128).transpose(0, 1, 4, 3, 2)).reshape(n_batch, S, D)
    in_maps = []
    for core in range(2 * n_batch):
        b, par = core // 2, core % 2
        blocks = _slot_blocks(par, NB)
        rows = np.concatenate([np.arange(bl * 128, (bl + 1) * 128) for bl in blocks])
        maskrows = np.zeros((NS, 64, 128), dtype=f32)
        for s, bl in enumerate(blocks):
            maskrows[s, bl:, :] = NEG
        flags = np.zeros((128, 4), dtype=f32)
        if par == 0:
            flags[:, 0] = 0.0; flags[:, 1] = 1.0; flags[:, 2] = 0.0
        else:
            flags[:, 0] = 1.0; flags[:, 1] = 0.0; flags[:, 2] = 1.0
        m = dict(common)
        m["xq"] = np.ascontiguousarray(x[b][rows])
        m["xseqT"] = xT_all[b]
        m["xqT"] = np.ascontiguousarray(xT_all[b][rows])
        m["csq"] = np.ascontiguousarray(css[rows])
        m["maskrows"] = maskrows.astype(bf)
        m["swa_flags"] = flags
        in_maps.append(m)
    res = run_bass_kernel_spmd(nc, in_maps, core_ids=list(range(2 * n_batch)))
    if DEBUG:
        global _LAST_RES
        _LAST_RES = res.results
    outp = np.zeros((n_batch, S, D), dtype=f32)
    for core in range(2 * n_batch):
        b, par = core // 2, core % 2
        blocks = _slot_blocks(par, NB)[:2 * CH]
        o = res.results[core]["out"]
        for s, bl in enumerate(blocks):
            outp[b, bl * 128:(bl + 1) * 128] = o[s * 128:(s + 1) * 128]
    return outp


def kernel(**inputs):
    return _run(inputs, 64, 4)
```

```python
import math
import numpy as np
import ml_dtypes
import concourse.bass as bass
import concourse.mybir as mybir
from concourse.bass_utils import run_bass_kernel_spmd

F32 = mybir.dt.float32
BF16 = mybir.dt.bfloat16
AF = mybir.ActivationFunctionType
ALU = mybir.AluOpType
AX = mybir.AxisListType

D = 1024
QR = 384
CKV = 256
RD = 64
H = 8
NOPE = 128
FF = 4096
NEG = -30000.0
ALPHA = 4 ** 0.25
LN_EPS = 1e-5
RMS_EPS = 1e-6
DEBUG = False
DEFER_POST = True
SPLIT_LN = True
NSLAB = 39
SLAB = 4096
RING = 5


class Buf:
    __slots__ = ("w", "r", "name", "excl")

    def __init__(self, name, excl=False):
        self.name = name
        self.w = None
        self.r = {}
        self.excl = excl


class Prog:
    def __init__(self, nc, stack):
        self.nc = nc
        self.eng = {"pe": nc.tensor, "act": nc.scalar, "dve": nc.vector, "pool": nc.gpsimd, "sp": nc.sync}
        self.sem = {}
        self.cnt = {}
        self.pending = {}
        self.waited = {}
        self.stack = stack
        self.log = {e: [] for e in self.eng}
        for e in self.eng:
            self.sem[e] = stack.enter_context(nc.semaphore("pg_" + e))
            self.cnt[e] = 0
            self.pending[e] = False

    def dsem(self, name):
        if name not in self.sem:
            self.sem[name] = self.stack.enter_context(self.nc.semaphore("d_" + name))
            self.cnt[name] = 0
        return name

    def _wait(self, e, t):
        if t is None:
            return
        p, v = t
        if v is None:
            v = self.cnt[p]
        if p == e and e == "pe":
            return
        if self.waited.get((e, p), 0) >= v:
            return
        if p in self.eng:
            assert v <= self.cnt[p], ("deadlock: waiting on unsignalled", e, p, v, self.cnt[p])
        self.eng[e].wait_ge(self.sem[p], v)
        self.log[e].append(("wait", p, v))
        self.waited[(e, p)] = v

    def _deps(self, e, reads, writes):
        for b in reads:
            self._wait(e, b.w)
        for b in writes:
            self._wait(e, b.w)
            for t in b.r.items():
                self._wait(e, t)

    def _mark(self, t, reads, writes):
        for b in reads:
            if t[1] is None:
                b.r[t[0]] = None
            elif b.r.get(t[0], 0) is not None and b.r.get(t[0], 0) < t[1]:
                b.r[t[0]] = t[1]
        for b in writes:
            b.w = t
            b.r = {}

    def op(self, e, fn, reads=(), writes=(), sig=True):
        ex = [b for b in reads if b.excl]
        if ex:
            reads = [b for b in reads if not b.excl]
            writes = list(writes) + ex
        self._deps(e, reads, writes)
        ins = fn(self.eng[e])
        if sig:
            self.cnt[e] += 1
            ins.then_inc(self.sem[e], 1)
            self.log[e].append(("inc", e, 1))
            self.pending[e] = False
            t = (e, self.cnt[e])
        else:
            assert e == "pe"
            self.pending[e] = True
            t = (e, self.cnt[e] + 1)
        self._mark(t, reads, writes)
        return t

    def check(self):
        val = {k: 0 for k in self.sem}
        pos = {e: 0 for e in self.eng}
        progress = True
        while progress:
            progress = False
            for e in self.eng:
                lg = self.log[e]
                while pos[e] < len(lg):
                    kind, p, v = lg[pos[e]]
                    if kind == "wait":
                        if val[p] < v:
                            break
                    else:
                        val[p] += v
                    pos[e] += 1
                    progress = True
        stuck = {e: (pos[e], len(self.log[e]), self.log[e][pos[e]], val[self.log[e][pos[e]][1]])
                 for e in self.eng if pos[e] < len(self.log[e])}
        assert not stuck, ("DEADLOCK", stuck)

    def dma(self, q, out, in_, semname, reads=(), writes=(), group=False):
        self.dsem(semname)
        self._deps(q, reads, writes)
        self.cnt[semname] += 16
        self.eng[q].dma_start(out=out, in_=in_).then_inc(self.sem[semname], 16)
        self.log[q].append(("inc", semname, 16))
        t = (semname, None if group else self.cnt[semname])
        self._mark(t, reads, writes)
        return t


def _slot_blocks(core_parity, NB):
    CH = NB // 4
    if core_parity == 0:
        return list(range(0, CH)) + list(range(3 * CH, 4 * CH)) + [3 * CH - 1]
    return list(range(CH, 2 * CH)) + list(range(2 * CH, 3 * CH)) + [CH - 1]


def _n_off(s, NB):
    CH = NB // 4
    if s < CH:
        return CH + s
    if s < 2 * CH:
        return 3 * CH + (s - CH)
    return 3 * CH - 1


def build_nc(NB):
    CH = NB // 4
    NS = 2 * CH + 1
    S = NB * 128
    nc = bass.Bass("TRN2", target_bir_lowering=False)

    def din(name, shape, dt=F32):
        return nc.dram_tensor(name, list(shape), dt, kind="ExternalInput").ap()

    xq = din("xq", [NS * 128, D])
    xseqT = din("xseqT", [S, D])
    xqT = din("xqT", [NS * 128, D])
    csq = din("csq", [NS * 128, 64])
    css = din("css", [S, 64])
    maskrows = din("maskrows", [NS, 64, 128], BF16)
    onehot = din("onehot", [64, S], BF16)
    ident_d = din("ident", [128, 128], BF16)
    triT_d = din("triT", [128, 512], BF16)
    w_in = din("mla_w_in", [D, QR + CKV + RD])
    g_q = din("mla_g_q", [QR])
    g_kv = din("mla_g_kv", [CKV])
    w_uq = din("mla_w_uq", [QR, H * 192])
    w_ukT = din("w_ukT", [128, H * CKV])
    w_uv = din("mla_w_uv", [CKV, H * 128])
    w_o = din("mla_w_o", [D, D])
    kv_w = din("kv_w_shared", [D, 512])
    swa_wq = din("swa_w_q", [D, D])
    swa_wo = din("swa_w_o", [D, D])
    sinks = din("swa_sinks", [16])
    biasT = din("biasT", [2, 128, 16 * 128])
    swa_flags = din("swa_flags", [128, 4])
    w_up = din("mlp_w_up", [2, D, FF])
    w_down = din("mlp_w_down", [2, FF, D])
    ln_g = din("ln_g", [4, D])
    ln_b = din("ln_b", [4, D])
    ln_gT = din("ln_gT", [128, 32])
    ln_bT = din("ln_bT", [128, 32])
    out = nc.dram_tensor("out", [2 * CH * 128, D], F32, kind="ExternalOutput").ap()
    wscr = nc.dram_tensor("wscr", [NSLAB, 128, SLAB], BF16).ap()
    dbg = {}

    def dbg_dump(name, ap2d):
        if not DEBUG:
            return
        shp = list(ap2d.shape)
        d = nc.dram_tensor("dbg_" + name, shp, ap2d.dtype, kind="ExternalOutput").ap()
        nc.all_engine_barrier()
        P.dma("sp", d, ap2d, "out0_0")
        dbg[name] = d

    from contextlib import ExitStack
    with ExitStack() as st:
        P = Prog(nc, st)

        def sb(name, shape, dt, stack=st):
            return stack.enter_context(nc.sbuf_tensor("s_" + name, list(shape), dt))

        banks = [st.enter_context(nc.psum_tensor("bank%d" % i, [128, 512], F32)) for i in range(8)]
        BQ = [[Buf("bank%d" % i, excl=True)] * 4 for i in range(8)]

        class _BK:
            def __getitem__(self, i):
                return BQ[i]
        BKL = _BK()

        def BKr(i, lo=0, hi=512):
            return [BQ[i][0]]

        def bkf(i):
            return banks[i][:]

        def bkh(i):
            return banks[i][:].bitcast(BF16)

        ident = sb("ident", [128, 128], BF16)
        triT = sb("triT", [128, 512], BF16)
        ones_c = sb("ones_c", [128, 1], BF16)
        OT = sb("OT", [128, NS, 8, 128], BF16)
        B_const = Buf("const")
        B_OT = [Buf("OT%d" % s) for s in range(NS)]

        P.dma("sp", ident[:], ident_d, "c0", writes=[B_const], group=True)
        P.dma("sp", triT[:], triT_d, "c0", writes=[B_const], group=True)
        B_ones = Buf("ones")
        P.op("dve", lambda e: e.memset(ones_c[:], 1.0), writes=[B_ones])

        def transposes(src_aps, bank, reads, n_part_out=128):
            t = None
            hv = bkh(bank)
            for i, a in enumerate(src_aps):
                m = a.shape[-1]
                last = i == len(src_aps) - 1
                t = P.op("pe", lambda e, a=a, i=i, m=m: e.transpose(hv[0:m, i * 128:(i + 1) * 128], a, ident[:]),
                         reads=list(reads) + [B_const], writes=BKr(bank), sig=last)
            return t

        with ExitStack() as sa:
            def sba(name, shape, dt):
                return sb(name, shape, dt, sa)

            KT_c = sba("KT_c", [128, 2, S], BF16)
            KT_r = sba("KT_r", [128, S], BF16)
            Vst = sba("Vst", [128, NB, CKV], BF16)
            w_inq_s = sba("w_inq", [128, 8, QR], BF16)
            w_inkv_s = sba("w_inkv", [128, 8, CKV + RD], BF16)
            w_uq_s = sba("w_uq", [128, 3, H * 192], BF16)
            w_ukT_s = sba("w_ukT", [128, H, CKV], BF16)
            w_uv_s = sba("w_uv", [128, 2, H * 128], BF16)
            gq_rep = sba("gq_rep", [128, QR], F32)
            gkv_rep = sba("gkv_rep", [128, CKV], F32)
            B_wA = Buf("wA")
            B_K = [Buf("K%d" % k) for k in range(NB)]
            B_KTr_const = Buf("ktr_const")

            P.dma("pool", w_inq_s[:], w_in[:, 0:QR].rearrange("(c p) n -> p c n", p=128), "c1", writes=[B_wA], group=True)
            P.dma("pool", w_inkv_s[:], w_in[:, QR:QR + CKV + RD].rearrange("(c p) n -> p c n", p=128), "c1", writes=[B_wA], group=True)
            P.dma("pool", w_uq_s[:], w_uq.rearrange("(c p) n -> p c n", p=128), "c1", writes=[B_wA], group=True)
            P.dma("pool", w_ukT_s[:], w_ukT.rearrange("p (h c) -> p h c", h=H), "c1", writes=[B_wA], group=True)
            P.dma("pool", w_uv_s[:], w_uv.rearrange("(c p) n -> p c n", p=128), "c1", writes=[B_wA], group=True)
            P.dma("sp", gq_rep[:], g_q.partition_broadcast(128), "c0", writes=[B_wA], group=True)
            P.dma("sp", gkv_rep[:], g_kv.partition_broadcast(128), "c0", writes=[B_wA], group=True)
            P.dma("sp", KT_r[64:128, :], onehot, "c0", writes=[B_KTr_const], group=True)

            xb = [sba("xb%d" % i, [128, D], BF16) for i in range(2)]
            xb += [OT[:, 0].rearrange("p h t -> p (h t)"), OT[:, 1].rearrange("p h t -> p (h t)")]
            B_xb = [Buf("xb%d" % i) for i in range(4)]
            cs = [sba("cs%d" % i, [128, 64], F32) for i in range(4)]
            B_cs = [Buf("cs%d" % i) for i in range(4)]
            xT = sba("xT", [128, 8, 128], BF16)
            B_xT = Buf("xT")
            h_sb = sba("h_sb", [128, QR + CKV + RD], F32)
            B_h = Buf("h_sb")
            stat = sba("stat", [128, 8], F32)
            B_stat = Buf("stat")
            rtmp = sba("rtmp", [128, 2, 8, 64], F32)
            B_rtmpA = Buf("rtmpA")
            B_rtmpB = Buf("rtmpB")
            kr_sb = sba("kr_sb", [128, RD], BF16)
            B_kr = Buf("kr")
            cq = sba("cq", [128, QR], BF16)
            B_cq = Buf("cq")
            cqT = sba("cqT", [128, 3, 128], BF16)
            B_cqT = Buf("cqT")
            Vown = sba("Vown", [128, CKV], BF16)
            B_Vown = Buf("Vown")
            KTown_c = sba("KTown_c", [128, 2, 128], BF16)
            KTown_r = sba("KTown_r", [128, 128], BF16)
            B_KTown = Buf("KTown")
            qn = h_sb[:].bitcast(BF16)[:, 0:H * NOPE].rearrange("p (h n) -> p h n", h=H)
            B_qn = B_h
            qr_f = sba("qr_f", [128, H, RD], F32)
            B_qrf = Buf("qr_f")
            junk = qr_f[:].rearrange("p h r -> p (h r)")[:, 0:QR]
            B_junk = B_qrf
            qr = sba("qr", [128, H, RD], BF16)
            B_qr = Buf("qr")
            qnT = xT
            B_qnT = B_xT
            QT_c = sba("QT_c", [128, 2, H, 128], BF16)
            QT_r = sba("QT_r", [128, H, 128], BF16)
            B_QTc = Buf("QT_c")
            B_QTr_lo = Buf("QT_r_lo")
            B_QTr_hi = Buf("QT_r_hi")
            PT = [sba("PT%d" % i, [128, 512], BF16) for i in range(3)]
            B_PT = [Buf("PT%d" % i) for i in range(3)]
            olat = rtmp[:, 0].bitcast(BF16).rearrange("p h f -> p (h f)").rearrange("p (h c) -> p h c", h=4)
            B_olat = B_rtmpA
            olatT = rtmp[:, 1].bitcast(BF16).rearrange("p h f -> p (h f)").rearrange("p (h j t) -> p h j t", h=4, j=2)
            B_olatT = B_rtmpB
            rden = sba("rden", [128, 4], F32)
            B_rden = Buf("rden")

            P.op("dve", lambda e: e.memset(KTown_r[:], 0.0), writes=[B_KTown])

            def rms_scale(src_ap, n, col, eps):
                P.op("dve", lambda e: e.scalar_tensor_tensor(out=junk[:, 0:n], in0=src_ap, scalar=1.0, in1=src_ap,
                                                             op0=ALU.mult, op1=ALU.mult, accum_out=stat[:, col:col + 1]),
                     reads=[B_h], writes=[B_junk, B_stat])
                P.op("act", lambda e: e.activation(out=stat[:, col:col + 1], in_=stat[:, col:col + 1], func=AF.Ln,
                                                   scale=1.0 / n, bias=eps),
                     writes=[B_stat])
                P.op("act", lambda e: e.activation(out=stat[:, col:col + 1], in_=stat[:, col:col + 1], func=AF.Exp,
                                                   scale=-0.5),
                     writes=[B_stat])

            def rope(src3, dst3, nh, csb, bcs, bsrc, bdst):
                s4 = src3.rearrange("p h (t f) -> p h t f", t=2)
                A = rtmp[:, 0, 0:nh, :].rearrange("p h (t f) -> p h t f", t=2)
                Bm = rtmp[:, 1, 0:nh, :].rearrange("p h (t f) -> p h t f", t=2)
                cosb = csb[:, 0:32].unsqueeze(1).unsqueeze(1).to_broadcast([128, nh, 2, 32])
                sinb = csb[:, 32:64].unsqueeze(1).unsqueeze(1).to_broadcast([128, nh, 2, 32])
                P.op("dve", lambda e: e.tensor_tensor(out=A, in0=s4, in1=cosb, op=ALU.mult), reads=[bcs, bsrc], writes=[B_rtmpA])
                P.op("dve", lambda e: e.tensor_tensor(out=Bm, in0=s4, in1=sinb, op=ALU.mult), reads=[bcs, bsrc], writes=[B_rtmpB])
                d4 = dst3.rearrange("p h (t f) -> p h t f", t=2)
                P.op("dve", lambda e: e.tensor_tensor(out=d4[:, :, 0, :], in0=A[:, :, 0, :], in1=Bm[:, :, 1, :], op=ALU.subtract),
                     reads=[B_rtmpA, B_rtmpB], writes=[bdst])
                return P.op("dve", lambda e: e.tensor_tensor(out=d4[:, :, 1, :], in0=Bm[:, :, 0, :], in1=A[:, :, 1, :], op=ALU.add),
                            reads=[B_rtmpA, B_rtmpB], writes=[bdst])

            def load_x_block(src_rows, cs_rows, i):
                P.dma("pool", xb[i][:, :], src_rows, "xb%d" % i, writes=[B_xb[i]])
                P.dma("sp", cs[i][:], cs_rows, "cs%d" % i, writes=[B_cs[i]])

            BX = 0

            def make_xT(i):
                bx = BX
                transposes([xb[i][:, c * 128:(c + 1) * 128] for c in range(8)], bx, [B_xb[i]])
                P.op("act", lambda e: e.copy(out=xT[:].rearrange("p c t -> p (c t)"), in_=bkh(bx)),
                     reads=BKr(bx), writes=[B_xT])

            def kv_from_h(i, v_dst, bv):
                rms_scale(h_sb[:, QR:QR + CKV], CKV, 1, RMS_EPS)
                P.op("dve", lambda e: e.scalar_tensor_tensor(out=v_dst, in0=h_sb[:, QR:QR + CKV], scalar=stat[:, 1:2],
                                                             in1=gkv_rep[:], op0=ALU.mult, op1=ALU.mult),
                     reads=[B_h, B_stat, B_wA], writes=[bv])
                rope(h_sb[:, QR + CKV:QR + CKV + RD].unsqueeze(1), kr_sb[:].unsqueeze(1), 1, cs[i], B_cs[i], B_h, B_kr)

            scr_jobs = []

            def slab_view(idx, inner):
                return wscr[idx].rearrange("p (c n) -> p c n", n=inner)

            k = 0
            for half in range(2):
                scr_jobs.append((slab_view(k, 1024), w_o[half * 512:(half + 1) * 512, :].rearrange("(c p) n -> p c n", p=128)))
                k += 1
            for L in range(2):
                if L == 1:
                    scr_jobs.append((slab_view(k, 512), kv_w.rearrange("(c p) n -> p c n", p=128)))
                    k += 1
                    for wsrc in (swa_wq, swa_wo):
                        for half in range(2):
                            scr_jobs.append((slab_view(k, 1024), wsrc[half * 512:(half + 1) * 512, :].rearrange("(c p) n -> p c n", p=128)))
                            k += 1
                for j in range(8):
                    scr_jobs.append((slab_view(k, 512), w_up[L][:, j * 512:(j + 1) * 512].rearrange("(c p) n -> p c n", p=128)))
                    k += 1
                    scr_jobs.append((slab_view(k, 1024), w_down[L][j * 512:(j + 1) * 512, :].rearrange("(c p) n -> p c n", p=128)))
                    k += 1
            assert k == NSLAB
            B_scr = Buf("scr")

            def issue_scr(n):
                for _ in range(n):
                    if scr_jobs:
                        o, i_ = scr_jobs.pop(0)
                        P.dma("pool", o, i_, "scr", writes=[B_scr])

            NSET = 4
            xTr = [xb[i4][:, :].rearrange("p (c t) -> p c t", c=8) for i4 in range(4)]
            sets = []
            for j in range(NSET):
                if NS >= 2 + 2 * NSET:
                    A_ = OT[:, 2 + 2 * j].rearrange("p h t -> p (h t)").bitcast(F32)
                    B_ = OT[:, 3 + 2 * j].rearrange("p h t -> p (h t)").bitcast(F32)
                else:
                    A_ = sba("p1A%d" % j, [128, 512], F32)[:]
                    B_ = sba("p1B%d" % j, [128, 512], F32)[:]
                sets.append(dict(h=A_[:, 0:320], rA=A_[:, 320:384], rB=A_[:, 384:448], st=A_[:, 448:456],
                                 junk=B_[:, 0:256], kr=B_[:, 256:288].bitcast(BF16),
                                 B_h=Buf("p1h%d" % j), B_st=Buf("p1st%d" % j), B_junk=Buf("p1j%d" % j),
                                 B_r=Buf("p1r%d" % j), B_kr=Buf("p1kr%d" % j), bh=2 * j, bt=2 * j + 1))

            def p1_S0(k):
                i4 = k % 4
                P.dma("pool", xb[i4][:, :], xseqT[k * 128:(k + 1) * 128, :], "xb%d" % i4, writes=[B_xb[i4]])
                P.dma("sp", cs[i4][:], css[k * 128:(k + 1) * 128, :], "cs%d" % i4, writes=[B_cs[i4]])

            def p1_S1(k):
                i4 = k % 4
                W = sets[k % NSET]
                for c in range(8):
                    P.op("pe", lambda e, c=c: e.matmul(bkf(W["bh"])[:, 0:CKV + RD], xTr[i4][:, c, :], w_inkv_s[:, c, :],
                                                       start=(c == 0), stop=(c == 7)),
                         reads=[B_xb[i4], B_wA], writes=BKr(W["bh"]), sig=(c == 7))
                P.op("act", lambda e: e.copy(out=W["h"], in_=bkf(W["bh"])[:, 0:CKV + RD]), reads=BKr(W["bh"]), writes=[W["B_h"]])

            def p1_S2(k):
                W = sets[k % NSET]
                P.op("dve", lambda e: e.scalar_tensor_tensor(out=W["junk"], in0=W["h"][:, 0:CKV], scalar=1.0, in1=W["h"][:, 0:CKV],
                                                             op0=ALU.mult, op1=ALU.mult, accum_out=W["st"][:, 0:1]),
                     reads=[W["B_h"]], writes=[W["B_junk"], W["B_st"]])
                P.op("act", lambda e: e.activation(out=W["st"][:, 0:1], in_=W["st"][:, 0:1], func=AF.Ln, scale=1.0 / CKV, bias=RMS_EPS),
                     writes=[W["B_st"]])
                P.op("act", lambda e: e.activation(out=W["st"][:, 0:1], in_=W["st"][:, 0:1], func=AF.Exp, scale=-0.5), writes=[W["B_st"]])

            def p1_S3(k):
                i4 = k % 4
                W = sets[k % NSET]
                P.op("dve", lambda e: e.scalar_tensor_tensor(out=Vst[:, k, :], in0=W["h"][:, 0:CKV], scalar=W["st"][:, 0:1],
                                                             in1=gkv_rep[:], op0=ALU.mult, op1=ALU.mult),
                     reads=[W["B_h"], W["B_st"], B_wA], writes=[B_K[k]])
                s4 = W["h"][:, CKV:CKV + RD].rearrange("p (t f) -> p t f", t=2)
                A4 = W["rA"].rearrange("p (t f) -> p t f", t=2)
                B4 = W["rB"].rearrange("p (t f) -> p t f", t=2)
                cosb = cs[i4][:, 0:32].unsqueeze(1).to_broadcast([128, 2, 32])
                sinb = cs[i4][:, 32:64].unsqueeze(1).to_broadcast([128, 2, 32])
                P.op("dve", lambda e: e.tensor_tensor(out=A4, in0=s4, in1=cosb, op=ALU.mult), reads=[W["B_h"], B_cs[i4]], writes=[W["B_r"]])
                P.op("dve", lambda e: e.tensor_tensor(out=B4, in0=s4, in1=sinb, op=ALU.mult), reads=[W["B_h"], B_cs[i4]], writes=[W["B_r"]])
                d4 = W["kr"].rearrange("p (t f) -> p t f", t=2)
                P.op("dve", lambda e: e.tensor_tensor(out=d4[:, 0, :], in0=A4[:, 0, :], in1=B4[:, 1, :], op=ALU.subtract),
                     reads=[W["B_r"]], writes=[W["B_kr"]])
                P.op("dve", lambda e: e.tensor_tensor(out=d4[:, 1, :], in0=B4[:, 0, :], in1=A4[:, 1, :], op=ALU.add),
                     reads=[W["B_r"]], writes=[W["B_kr"]])

            def p1_S4(k):
                W = sets[k % NSET]
                transposes([Vst[:, k, 0:128], Vst[:, k, 128:256], W["kr"]], W["bt"], [B_K[k], W["B_kr"]])
                h2 = bkh(W["bt"])
                P.op("act", lambda e: e.copy(out=KT_c[:, :, k * 128:(k + 1) * 128], in_=h2[:, 0:256].rearrange("p (j t) -> p j t", j=2)),
                     reads=BKr(W["bt"]), writes=[B_K[k]])
                P.op("dve", lambda e: e.tensor_copy(out=KT_r[0:64, k * 128:(k + 1) * 128], in_=h2[0:64, 256:384]),
                     reads=BKr(W["bt"]), writes=[B_K[k]])

            for t in range(NB + 3):
                if t < NB:
                    p1_S0(t)
                if 0 <= t - 3 < NB:
                    p1_S4(t - 3)
                if 0 <= t - 2 < NB:
                    p1_S3(t - 2)
                if 0 <= t - 1 < NB:
                    p1_S2(t - 1)
                if t < NB:
                    p1_S1(t)
            nc.all_engine_barrier()
            scr_per_slot = -(-len(scr_jobs) // max(1, min(NS - 1, 20)))

            scale = float((NOPE + RD) ** -0.5)
            for s in range(NS):
                i = s % 2
                n_off = _n_off(s, NB)
                P.dma("pool", xb[i][:, :], xqT[s * 128:(s + 1) * 128, :], "xb%d" % i, writes=[B_xb[i]])
                P.dma("sp", cs[i][:], csq[s * 128:(s + 1) * 128, :], "cs%d" % i, writes=[B_cs[i]])
                xTs = xb[i][:, :].rearrange("p (c t) -> p c t", c=8)
                issue_scr(scr_per_slot)
                P.dma("sp", QT_r[64:128, :, :], maskrows[s].unsqueeze(1).to_broadcast([64, H, 128]), "mrow",
                      writes=[B_QTr_hi])
                for c in range(8):
                    P.op("pe", lambda e, c=c: e.matmul(bkf(1)[:, 0:QR], xTs[:, c, :], w_inq_s[:, c, :],
                                                       start=(c == 0), stop=(c == 7)),
                         reads=[B_xb[i], B_wA], writes=BKr(1), sig=(c == 7))
                for c in range(8):
                    P.op("pe", lambda e, c=c: e.matmul(bkf(2)[:, 0:CKV + RD], xTs[:, c, :], w_inkv_s[:, c, :],
                                                       start=(c == 0), stop=(c == 7)),
                         reads=[B_xb[i], B_wA], writes=BKr(2), sig=(c == 7))
                P.op("act", lambda e: e.copy(out=h_sb[:, 0:QR], in_=bkf(1)[:, 0:QR]), reads=BKr(1), writes=[B_h])
                P.op("act", lambda e: e.copy(out=h_sb[:, QR:QR + CKV + RD], in_=bkf(2)[:, 0:CKV + RD]),
                     reads=BKr(2), writes=[B_h])
                P.op("dve", lambda e: e.scalar_tensor_tensor(out=junk[:, 0:QR], in0=h_sb[:, 0:QR], scalar=1.0 / QR, in1=h_sb[:, 0:QR],
                                                             op0=ALU.mult, op1=ALU.mult, accum_out=stat[:, 0:1]),
                     reads=[B_h], writes=[B_junk, B_stat])
                P.op("dve", lambda e: e.scalar_tensor_tensor(out=junk[:, 0:CKV], in0=h_sb[:, QR:QR + CKV], scalar=1.0 / CKV,
                                                             in1=h_sb[:, QR:QR + CKV], op0=ALU.mult, op1=ALU.mult, accum_out=stat[:, 1:2]),
                     reads=[B_h], writes=[B_junk, B_stat])
                P.op("act", lambda e: e.activation(out=stat[:, 0:2], in_=stat[:, 0:2], func=AF.Ln, bias=RMS_EPS), writes=[B_stat])
                P.op("act", lambda e: e.activation(out=stat[:, 0:2], in_=stat[:, 0:2], func=AF.Exp, scale=-0.5), writes=[B_stat])
                P.op("dve", lambda e: e.scalar_tensor_tensor(out=cq[:], in0=h_sb[:, 0:QR], scalar=stat[:, 0:1],
                                                             in1=gq_rep[:], op0=ALU.mult, op1=ALU.mult),
                     reads=[B_h, B_stat, B_wA], writes=[B_cq])
                P.op("dve", lambda e: e.scalar_tensor_tensor(out=Vown[:], in0=h_sb[:, QR:QR + CKV], scalar=stat[:, 1:2],
                                                             in1=gkv_rep[:], op0=ALU.mult, op1=ALU.mult),
                     reads=[B_h, B_stat, B_wA], writes=[B_Vown])
                rope(h_sb[:, QR + CKV:QR + CKV + RD].unsqueeze(1), kr_sb[:].unsqueeze(1), 1, cs[i], B_cs[i], B_h, B_kr)
                transposes([cq[:, 0:128], cq[:, 128:256], cq[:, 256:384], Vown[:, 0:128], Vown[:, 128:256], kr_sb[:]],
                           0, [B_cq, B_Vown, B_kr])
                h0 = bkh(0)
                P.op("act", lambda e: e.copy(out=cqT[:].rearrange("p c t -> p (c t)"), in_=h0[:, 0:384]),
                     reads=BKr(0), writes=[B_cqT])
                P.op("dve", lambda e: e.tensor_copy(out=KTown_c[:].rearrange("p c t -> p (c t)"), in_=h0[:, 384:640]),
                     reads=BKr(0), writes=[B_KTown])
                P.op("dve", lambda e: e.tensor_copy(out=KTown_r[0:64, :], in_=h0[0:64, 640:768]),
                     reads=BKr(0), writes=[B_KTown])
                for g in range(3):
                    for c in range(3):
                        P.op("pe", lambda e, g=g, c=c: e.matmul(bkf(3 + g), cqT[:, c, :], w_uq_s[:, c, g * 512:(g + 1) * 512],
                                                                start=(c == 0), stop=(c == 2)),
                             reads=[B_cqT, B_wA], writes=BKr(3 + g), sig=(c == 2))
                P.op("act", lambda e: e.copy(out=qn[:, 0:4, :].rearrange("p h n -> p (h n)"), in_=bkf(3)), reads=BKr(3), writes=[B_qn])
                P.op("dve", lambda e: e.tensor_copy(out=qn[:, 4:8, :].rearrange("p h n -> p (h n)"), in_=bkf(4)), reads=BKr(4), writes=[B_qn])
                P.op("act", lambda e: e.copy(out=qr_f[:].rearrange("p h r -> p (h r)"), in_=bkf(5)), reads=BKr(5), writes=[B_qrf])
                rope(qr_f[:], qr[:], H, cs[i], B_cs[i], B_qrf, B_qr)
                transposes([qn[:, h, :] for h in range(H)], 6, [B_qn])
                P.op("act", lambda e: e.copy(out=qnT[:].rearrange("p h t -> p (h t)"), in_=bkh(6)), reads=BKr(6), writes=[B_qnT])
                transposes([qr[:, h, :] for h in range(H)], 7, [B_qr])
                P.op("dve", lambda e: e.tensor_copy(out=QT_r[0:64, :, :].rearrange("p h t -> p (h t)"), in_=bkh(7)[0:64, :]),
                     reads=BKr(7), writes=[B_QTr_lo])
                for j in range(2):
                    for h in range(H):
                        bk = j * 2 + h // 4
                        P.op("pe", lambda e, j=j, h=h, bk=bk: e.matmul(bkf(bk)[:, (h % 4) * 128:(h % 4 + 1) * 128],
                                                                       w_ukT_s[:, h, j * 128:(j + 1) * 128], qnT[:, h, :],
                                                                       start=True, stop=True, skip_group_check=True),
                             reads=[B_qnT, B_wA], writes=BKr(bk), sig=(h % 4 == 3))
                for j in range(2):
                    for hg in range(2):
                        bk = j * 2 + hg
                        eng = "act" if hg == 0 else "dve"
                        dst = QT_c[:, j, hg * 4:(hg + 1) * 4, :].rearrange("p h t -> p (h t)")
                        if eng == "act":
                            P.op("act", lambda e, dst=dst, bk=bk: e.copy(out=dst, in_=bkf(bk)), reads=BKr(bk), writes=[B_QTc])
                        else:
                            P.op("dve", lambda e, dst=dst, bk=bk: e.tensor_copy(out=dst, in_=bkf(bk)), reads=BKr(bk), writes=[B_QTc])

                deferred = []
                pre_b = []
                for hg in range(2):
                    ob = [0, 1] if hg == 0 else [2, 3]
                    db = 4
                    sring = [5, 6, 7]
                    nun = n_off + 1

                    def s_mm(u):
                        r = sring[u % 3]
                        diag = (u == n_off)
                        hs = slice(hg * 4, hg * 4 + 4)
                        if diag:
                            l0, l1, l2 = KTown_c[:, 0, :], KTown_c[:, 1, :], KTown_r[:]
                            rd = [B_KTown]
                        else:
                            l0 = KT_c[:, 0, u * 128:(u + 1) * 128]
                            l1 = KT_c[:, 1, u * 128:(u + 1) * 128]
                            l2 = KT_r[:, u * 128:(u + 1) * 128]
                            rd = [B_K[u], B_KTr_const]
                        P.op("pe", lambda e: e.matmul(bkf(r), l0, QT_c[:, 0, hs, :].rearrange("p h t -> p (h t)"), start=True, stop=False),
                             reads=rd + [B_QTc], writes=BKr(r), sig=False)
                        P.op("pe", lambda e: e.matmul(bkf(r), l1, QT_c[:, 1, hs, :].rearrange("p h t -> p (h t)"), start=False, stop=False),
                             reads=rd + [B_QTc], writes=BKr(r), sig=False)
                        P.op("pe", lambda e: e.matmul(bkf(r), l2, QT_r[:, hs, :].rearrange("p h t -> p (h t)"), start=False, stop=(not diag)),
                             reads=rd + [B_QTr_lo, B_QTr_hi], writes=BKr(r), sig=(not diag))
                        if diag:
                            P.op("pe", lambda e: e.matmul(bkf(r), ident[:], triT[:], start=False, stop=True),
                                 reads=[B_const], writes=BKr(r), sig=True)

                    def exp_u(u):
                        r = sring[u % 3]
                        pt = u % 3
                        P.op("act", lambda e: e.activation(out=PT[pt][:], in_=bkf(r), func=AF.Exp, scale=scale),
                             reads=BKr(r), writes=[B_PT[pt]])

                    def pv_mm(u):
                        pt = u % 3
                        diag = (u == n_off)
                        last = (u == nun - 1)
                        vsrc = Vown[:] if diag else Vst[:, u, :]
                        rd = [B_Vown] if diag else [B_K[u]]
                        for hh in range(4):
                            bk = ob[hh // 2]
                            col = (hh % 2) * 256
                            P.op("pe", lambda e, hh=hh, bk=bk, col=col: e.matmul(
                                bkf(bk)[:, col:col + 256], PT[pt][:, hh * 128:(hh + 1) * 128], vsrc,
                                start=(u == 0 and hh % 2 == 0), stop=last, skip_group_check=True),
                                reads=rd + [B_PT[pt]], writes=BKr(bk), sig=(last and hh % 2 == 1))
                            P.op("pe", lambda e, hh=hh: e.matmul(
                                bkf(db)[:, hg * 4 + hh:hg * 4 + hh + 1], PT[pt][:, hh * 128:(hh + 1) * 128], ones_c[:],
                                start=(u == 0 and hh == 0), stop=last, skip_group_check=True),
                                reads=[B_PT[pt], B_ones], writes=BKr(db), sig=(hh == 3))

                    s_mm(0)
                    if nun > 1:
                        s_mm(1)
                    while pre_b:
                        pre_b.pop(0)()
                    for u in range(nun):
                        exp_u(u)
                        if u + 2 < nun:
                            s_mm(u + 2)
                        pv_mm(u)
                        if deferred and u >= 1:
                            deferred.pop(0)()
                    while deferred:
                        deferred.pop(0)()

                    P.op("dve", lambda e: e.reciprocal(out=rden[:], in_=bkf(db)[:, hg * 4:hg * 4 + 4]),
                         reads=BKr(db), writes=[B_rden])
                    for half in range(2):
                        bk = ob[half]
                        P.op("dve", lambda e, half=half, bk=bk: e.tensor_tensor(
                            out=olat[:, half * 2:half * 2 + 2, :],
                            in0=bkf(bk).rearrange("p (h c) -> p h c", h=2),
                            in1=rden[:, half * 2:half * 2 + 2].unsqueeze(2).to_broadcast([128, 2, CKV]), op=ALU.mult),
                            reads=BKr(bk) + [B_rden], writes=[B_olat])

                    def stage_b(ob=ob):
                        transposes([olat[:, hh, j * 128:(j + 1) * 128] for hh in range(4) for j in range(2)], ob[0], [B_olat])
                        P.op("act", lambda e: e.copy(out=olatT[:].rearrange("p h j t -> p (h j t)"), in_=bkh(ob[0])),
                             reads=BKr(ob[0]), writes=[B_olatT])

                    def stage_c(ob=ob, hg=hg, s=s):
                        for hh in range(4):
                            h = hg * 4 + hh
                            for j in range(2):
                                P.op("pe", lambda e, hh=hh, h=h, j=j: e.matmul(
                                    bkf(ob[1])[:, hh * 128:(hh + 1) * 128], w_uv_s[:, j, h * 128:(h + 1) * 128], olatT[:, hh, j, :],
                                    start=(j == 0), stop=(j == 1), skip_group_check=True),
                                    reads=[B_olatT, B_wA], writes=BKr(ob[1]), sig=(hh == 3 and j == 1))
                        P.op("dve", lambda e: e.tensor_copy(out=OT[:, s, hg * 4:hg * 4 + 4, :].rearrange("p h t -> p (h t)"), in_=bkf(ob[1])),
                             reads=BKr(ob[1]), writes=[B_OT[s]])

                    if hg == 0 and DEFER_POST:
                        pre_b.append(stage_b)
                        deferred.extend([stage_c])
                    else:
                        stage_b()
                        stage_c()

            dbg_dump("V", Vst[:].rearrange("p k c -> p (k c)"))
            dbg_dump("KTc", KT_c[:].rearrange("p j s -> p (j s)"))
            dbg_dump("KTr", KT_r[:])
            dbg_dump("OT", OT[:].rearrange("p s h t -> p (s h t)"))
            dbg_dump("QTc", QT_c[:].rearrange("p j h t -> p (j h t)"))
            dbg_dump("QTr", QT_r[:].rearrange("p h t -> p (h t)"))
            dbg_dump("cq", cq[:])
            dbg_dump("qn", qn[:].rearrange("p h n -> p (h n)"))
            dbg_dump("qr", qr[:].rearrange("p h n -> p (h n)"))
            dbg_dump("olat", olat[:].rearrange("p h n -> p (h n)"))
            dbg_dump("rden", rden[:])
            dbg_dump("hsb", h_sb[:])
            dbg_dump("stat", stat[:])

        nc.all_engine_barrier()
        with ExitStack() as sbk:
            def sbb(name, shape, dt):
                return sb(name, shape, dt, sbk)

            ring = [sbb("ring%d" % i, [128, SLAB], BF16) for i in range(RING)]
            B_ring = [Buf("ring%d" % i) for i in range(RING)]
            lng = sbb("lng", [128, 4, D], F32)
            lnb = sbb("lnb", [128, 4, D], F32)
            ET = sbb("ET", [128, 2, 16 * 128], BF16)
            qT_full = sbb("qT", [128, 16, 256], BF16)
            ETf = qT_full[:].rearrange("p h t -> p (h t)").bitcast(F32)
            sink_e = sbb("sink_e", [128, 16], F32)
            flags = sbb("flags", [128, 4], F32)
            B_cB = Buf("constB")
            for L4 in range(4):
                P.dma("sp", lng[:, L4, :], ln_g[L4].partition_broadcast(128), "c2", writes=[B_cB], group=True)
                P.dma("sp", lnb[:, L4, :], ln_b[L4].partition_broadcast(128), "c2", writes=[B_cB], group=True)
            P.dma("sp", sink_e[:], sinks.partition_broadcast(128), "c2", writes=[B_cB], group=True)
            P.dma("sp", flags[:], swa_flags, "c2", writes=[B_cB], group=True)
            for t2 in range(2):
                P.dma("sp", ETf, biasT[t2], "c2", writes=[B_cB], group=True)
                t_et = P.op("act", lambda e, t2=t2: e.activation(out=ET[:, t2, :], in_=ETf, func=AF.Exp), reads=[B_cB], writes=[B_cB])
            P.op("act", lambda e: e.activation(out=sink_e[:], in_=sink_e[:], func=AF.Exp), reads=[B_cB], writes=[B_cB])

            xs_all = sbb("xs", [128, 2, 2, D], F32)
            B_xs_all = [[Buf("xs00"), Buf("xs01")], [Buf("xs10"), Buf("xs11")]]
            xs = xs_all[:, 0]
            B_xs = B_xs_all[0]
            cur = {"par": 0}
            zb = [sbb("zb%d" % i, [128, D], BF16) for i in range(2)]
            B_zb = [Buf("zb0"), Buf("zb1")]
            gT = sbb("gT", [128, 4, 8], F32)
            bT = sbb("bT", [128, 4, 8], F32)
            P.dma("sp", gT[:], ln_gT.rearrange("p (l c) -> p l c", l=4), "c2", writes=[B_cB], group=True)
            P.dma("sp", bT[:], ln_bT.rearrange("p (l c) -> p l c", l=4), "c2", writes=[B_cB], group=True)
            aT = sbb("aT", [128, 8, 256], BF16)
            B_aTs = [Buf("aT0"), Buf("aT1")]
            hT = [sbb("hT%d" % i, [128, 4, 256], BF16) for i in range(2)]
            B_hT = [Buf("hT0"), Buf("hT1")]
            bst_all = sbb("bst", [128, 2, 2, 2, 6], F32)
            mv_all = sbb("mv", [128, 2, 2, 4], F32)
            B_st_all = [[Buf("bst00"), Buf("bst01")], [Buf("bst10"), Buf("bst11")]]
            bst = bst_all[:, 0]
            mv = mv_all[:, 0]
            B_st = B_st_all[0]
            qT = qT_full[0:64]
            B_qT = Buf("qT")
            B_qT.w = t_et
            kT = sbb("kT", [64, 4, 256], BF16)
            B_kT = Buf("kT")
            Vaug = sbb("Vaug", [128, 2, 4, 65], BF16)
            B_Va = Buf("Vaug")
            kT_p = sbb("kT_p", [64, 4, 128], BF16)
            Va_p = sbb("Va_p", [128, 4, 65], BF16)
            B_prev = Buf("prev")
            kT_e = sbb("kT_e", [64, 4, 128], BF16)
            Va_e = sbb("Va_e", [128, 4, 65], BF16)
            B_ext = Buf("ext")
            eS = [sbb("eS%d" % i, [128, 512], F32) for i in range(2)]
            B_eS = [Buf("eS0"), Buf("eS1")]
            PB = [sbb("PB%d" % i, [128, 512], BF16) for i in range(4)]
            B_PB = [Buf("PB%d" % i) for i in range(4)]
            osb = sbb("osb", [128, 16, 64], BF16)
            B_osb = Buf("osb")
            oT2 = sbb("oT2", [128, 8, 128], BF16)
            B_oT2 = Buf("oT2")
            rd16 = sbb("rd16", [128, 16], F32)
            B_rd16 = Buf("rd16")

            P.op("dve", lambda e: e.memset(Vaug[:, :, :, 64:65], 1.0), writes=[B_Va])
            P.op("dve", lambda e: e.memset(Va_p[:, :, 64:65], 1.0), writes=[B_prev])
            P.op("dve", lambda e: e.memset(Va_e[:, :, 64:65], 1.0), writes=[B_ext])

            state = {"n": 0}

            def mlp_seq(b0, insert=()):
                seq = [b0]
                for j in range(8):
                    if j + 1 < 8:
                        seq.append(b0 + 2 * (j + 1))
                    if j == 7:
                        seq += list(insert)
                    seq.append(b0 + 2 * j + 1)
                return seq

            slab_plan = []
            slab_plan += [0, 1] + mlp_seq(2) + [18]
            slab_plan += [0, 1]
            for bi_ in range(CH):
                slab_plan += mlp_seq(2) + [18, 19, 20, 21, 22] + mlp_seq(23, insert=([0, 1] if bi_ + 1 < CH else ()))
            issued = {"n": 0}
            slab_ticket = {}

            def issue_slabs(upto):
                while issued["n"] < min(upto, len(slab_plan)):
                    g = issued["n"]
                    r = g % RING
                    P.dma("sp", ring[r][:], wscr[slab_plan[g]], "ring%d" % r, reads=[B_scr], writes=[B_ring[r]])
                    issued["n"] += 1

            def next_slab():
                g = state["n"]
                state["n"] += 1
                issue_slabs(g + 1)
                return g % RING

            def release_and_prefetch():
                issue_slabs(state["n"] + RING - 1)

            def ln_part1a(sl, li):
                P.op("dve", lambda e: e.bn_stats(out=bst[:, sl, 0, :], in_=xs[:, sl, 0:512]), reads=[B_xs[sl]], writes=[B_st[sl]])
                P.op("dve", lambda e: e.bn_stats(out=bst[:, sl, 1, :], in_=xs[:, sl, 512:1024]), reads=[B_xs[sl]], writes=[B_st[sl]])
                P.op("dve", lambda e: e.bn_aggr(out=mv[:, sl, 0:2], in_=bst[:, sl].rearrange("p c (t j) -> p (c t) j", j=3)),
                     writes=[B_st[sl]])
                P.op("act", lambda e: e.activation(out=mv[:, sl, 2:3], in_=mv[:, sl, 1:2], func=AF.Ln, bias=LN_EPS), writes=[B_st[sl]])
                P.op("act", lambda e: e.activation(out=mv[:, sl, 2:3], in_=mv[:, sl, 2:3], func=AF.Exp, scale=-0.5), writes=[B_st[sl]])

            def ln_part1b(sl, li):
                P.op("dve", lambda e: e.tensor_scalar(out=zb[sl][:], in0=xs[:, sl, :], scalar1=mv[:, sl, 0:1], scalar2=mv[:, sl, 2:3],
                                                      op0=ALU.subtract, op1=ALU.mult),
                     reads=[B_st[sl], B_xs[sl]], writes=[B_zb[sl]])

            def ln_part1(sl, li, need_aT=True):
                ln_part1a(sl, li)
                if need_aT:
                    ln_part1b(sl, li)

            def ln_part2(sl, li, need_aT=True, do_a=True, do_b=True):
                if need_aT and do_a:
                    bk = 6 + sl
                    transposes([zb[sl][:, c * 128:(c + 1) * 128] for c in range(8)], bk, [B_zb[sl]])
                    for c in range(8):
                        if sl == 0:
                            P.op("act", lambda e, c=c: e.activation(
                                out=aT[:, c, sl * 128:(sl + 1) * 128], in_=bkh(bk)[:, c * 128:(c + 1) * 128], func=AF.Identity,
                                scale=gT[:, li, c:c + 1], bias=bT[:, li, c:c + 1]),
                                reads=BKr(bk) + [B_cB], writes=[B_aTs[sl]])
                        else:
                            P.op("dve", lambda e, c=c: e.tensor_scalar(
                                out=aT[:, c, sl * 128:(sl + 1) * 128], in0=bkh(bk)[:, c * 128:(c + 1) * 128],
                                scalar1=gT[:, li, c:c + 1], scalar2=bT[:, li, c:c + 1], op0=ALU.mult, op1=ALU.add),
                                reads=BKr(bk) + [B_cB], writes=[B_aTs[sl]])
                if not do_b:
                    return
                x_ap = xs[:, sl, :]
                P.op("dve", lambda e: e.scalar_tensor_tensor(out=x_ap, in0=x_ap, scalar=mv[:, sl, 0:1], in1=lng[:, li, :],
                                                             op0=ALU.subtract, op1=ALU.mult),
                     reads=[B_st[sl], B_cB], writes=[B_xs[sl]])
                P.op("dve", lambda e: e.scalar_tensor_tensor(out=x_ap, in0=x_ap, scalar=mv[:, sl, 2:3], in1=lnb[:, li, :],
                                                             op0=ALU.mult, op1=ALU.add),
                     reads=[B_st[sl], B_cB], writes=[B_xs[sl]])

            def ln_stage(nsl, li, need_aT=True):
                for sl in range(nsl):
                    ln_part1a(sl, li)
                if need_aT:
                    for sl in range(nsl):
                        ln_part1b(sl, li)
                for sl in range(nsl):
                    ln_part2(sl, li, need_aT)

            def mlp(nsl, hook=None):
                T = nsl * 128
                accb = [[0, 1], [2, 3]]

                def up(j):
                    ru = next_slab()
                    wu = ring[ru][:].rearrange("p (c n) -> p c n", n=512)
                    hb = hT[j % 2]
                    for fc in range(4):
                        bk = 4 + fc
                        for c in range(8):
                            P.op("pe", lambda e, fc=fc, c=c, bk=bk: e.matmul(
                                bkf(bk)[:, 0:T], wu[:, c, fc * 128:(fc + 1) * 128], aT[:, c, 0:T],
                                start=(c == 0), stop=(c == 7)),
                                reads=[B_ring[ru]] + B_aTs, writes=BKr(bk), sig=(c == 7))
                        src = bkf(bk)[:, 0:T]
                        es = fc % 2
                        P.op("act", lambda e, src=src, es=es: e.activation(out=eS[es][:, 0:T], in_=src, func=AF.Relu),
                             reads=BKr(bk), writes=[B_eS[es]])
                        P.op("dve", lambda e, fc=fc, es=es: e.tensor_tensor(out=hb[:, fc, 0:T], in0=eS[es][:, 0:T], in1=eS[es][:, 0:T], op=ALU.mult),
                             reads=[B_eS[es]], writes=[B_hT[j % 2]])

                def down(j):
                    rd = next_slab()
                    wd = ring[rd][:].rearrange("p (c n) -> p c n", n=1024)
                    hb = hT[j % 2]
                    for sl in range(nsl):
                        for dh in range(2):
                            bk = accb[sl][dh]
                            for fc in range(4):
                                P.op("pe", lambda e, sl=sl, dh=dh, fc=fc, bk=bk: e.matmul(
                                    bkf(bk), hb[:, fc, sl * 128:(sl + 1) * 128], wd[:, fc, dh * 512:(dh + 1) * 512],
                                    start=(j == 0 and fc == 0), stop=(j == 7 and fc == 3)),
                                    reads=[B_hT[j % 2], B_ring[rd]], writes=BKr(bk), sig=(fc == 3))
                    release_and_prefetch()

                up(0)
                for j in range(8):
                    if j + 1 < 8:
                        up(j + 1)
                    if j == 7 and hook is not None:
                        hook()
                    down(j)
                for sl in range(nsl):
                    for dh in range(2):
                        bk = accb[sl][dh]
                        xa = xs[:, sl, dh * 512:(dh + 1) * 512]
                        P.op("dve", lambda e, xa=xa, bk=bk: e.scalar_tensor_tensor(out=xa, in0=xa, scalar=ALPHA, in1=bkf(bk),
                                                                                   op0=ALU.mult, op1=ALU.add),
                             reads=BKr(bk), writes=[B_xs[sl]])

            def swa_kv(nsl, sl_list_global):
                T = nsl * 128
                r = next_slab()
                wk = ring[r][:].rearrange("p (c n) -> p c n", n=512)
                for kh in range(4):
                    for c in range(8):
                        P.op("pe", lambda e, kh=kh, c=c: e.matmul(bkf(4 + kh)[0:64, 0:T],
                                                                   wk[:, c, kh * 64:(kh + 1) * 64], aT[:, c, 0:T],
                                                                   start=(c == 0), stop=(c == 7), skip_group_check=True),
                             reads=[B_ring[r]] + B_aTs, writes=BKr(4 + kh), sig=(c == 7))
                for kh in range(4):
                    P.op("dve", lambda e, kh=kh: e.tensor_copy(out=kT[:, kh, 0:T], in_=bkf(4 + kh)[0:64, 0:T]),
                         reads=BKr(4 + kh), writes=[B_kT])
                for sl in range(nsl):
                    for c in range(8):
                        P.op("pe", lambda e, sl=sl, c=c: e.matmul(bkf(3)[:, sl * 256:(sl + 1) * 256], aT[:, c, sl * 128:(sl + 1) * 128],
                                                                   wk[:, c, 256:512], start=(c == 0), stop=(c == 7), skip_group_check=True),
                             reads=[B_ring[r]] + B_aTs, writes=BKr(3), sig=(c == 7))
                    P.op("act", lambda e, sl=sl: e.copy(out=Vaug[:, sl, :, 0:64],
                                                        in_=bkf(3)[:, sl * 256:(sl + 1) * 256].rearrange("p (k d) -> p k d", k=4)),
                         reads=BKr(3), writes=[B_Va])

            def swa_q(nsl):
                T = nsl * 128
                for half in range(2):
                    r = next_slab()
                    wq = ring[r][:].rearrange("p (c n) -> p c n", n=1024)
                    if half == 0:
                        r0, wq0 = r, wq
                    else:
                        r1, wq1 = r, wq
                for h in range(16):
                    bk = 4 + h % 4
                    colo = 0
                    for c in range(8):
                        wsel = wq0 if c < 4 else wq1
                        rsel = r0 if c < 4 else r1
                        P.op("pe", lambda e, h=h, c=c, wsel=wsel, bk=bk, colo=colo: e.matmul(
                            bkf(bk)[0:64, colo:colo + T], wsel[:, c % 4, h * 64:(h + 1) * 64], aT[:, c, 0:T],
                            start=(c == 0), stop=(c == 7), skip_group_check=True),
                            reads=[B_ring[rsel]] + B_aTs, writes=BKr(bk, colo, colo + T), sig=(c == 7))
                    P.op("act" if h % 2 == 0 else "dve",
                         (lambda e, h=h, bk=bk, colo=colo: e.copy(out=qT[:, h, 0:T], in_=bkf(bk)[0:64, colo:colo + T])) if h % 2 == 0 else
                         (lambda e, h=h, bk=bk, colo=colo: e.tensor_copy(out=qT[:, h, 0:T], in_=bkf(bk)[0:64, colo:colo + T])),
                         reads=BKr(bk, colo, colo + T), writes=[B_qT])
                release_and_prefetch()

            def swa_attn(sl, keys):
                def o_ap(h):
                    return bkf(h // 7)[:, (h % 7) * 65:(h % 7) * 65 + 65]
                first_in_bank = {0: True, 1: True, 2: True}
                nk = len(keys)
                tiles = [(ki, kh) for ki in range(nk) for kh in range(4)]

                def score(t):
                    ki, kh = tiles[t]
                    k_ap, v_ap, et_idx, flag_ap, kbufs = keys[ki]
                    sbk_ = (6, 7, 3)[t % 3]
                    P.op("pe", lambda e: e.matmul(bkf(sbk_), k_ap[:, kh, :], qT[:, kh * 4:kh * 4 + 4, sl * 128:(sl + 1) * 128],
                                                  start=True, stop=True),
                         reads=list(kbufs) + [B_qT], writes=BKr(sbk_))

                def softmax_pv(t):
                    ki, kh = tiles[t]
                    k_ap, v_ap, et_idx, flag_ap, kbufs = keys[ki]
                    sbk_ = (6, 7, 3)[t % 3]
                    es = t % 2
                    pb = t % 4
                    P.op("act", lambda e: e.activation(out=eS[es][:], in_=bkf(sbk_), func=AF.Exp, scale=0.125),
                         reads=BKr(sbk_), writes=[B_eS[es]])
                    et_ap = ET[:, et_idx, kh * 512:(kh + 1) * 512]
                    if flag_ap is None:
                        P.op("dve", lambda e: e.tensor_tensor(out=PB[pb][:], in0=eS[es][:], in1=et_ap, op=ALU.mult),
                             reads=[B_eS[es], B_cB], writes=[B_PB[pb]])
                    else:
                        P.op("dve", lambda e: e.scalar_tensor_tensor(out=PB[pb][:], in0=eS[es][:], scalar=flag_ap, in1=et_ap,
                                                                     op0=ALU.mult, op1=ALU.mult),
                             reads=[B_eS[es], B_cB], writes=[B_PB[pb]])
                    for g in range(4):
                        h = kh * 4 + g
                        bko = h // 7
                        st_ = first_in_bank[bko] and ki == 0
                        if st_:
                            first_in_bank[bko] = False
                        P.op("pe", lambda e, h=h, g=g, st_=st_: e.matmul(
                            o_ap(h), PB[pb][:, g * 128:(g + 1) * 128], v_ap[:, kh, :],
                            start=st_, stop=(ki == nk - 1), skip_group_check=True),
                            reads=list(kbufs) + [B_PB[pb]], writes=BKr(bko), sig=(g == 3))

                nt = len(tiles)
                for t in range(min(3, nt)):
                    score(t)
                for t in range(nt):
                    softmax_pv(t)
                    if t + 3 < nt:
                        score(t + 3)
                for b_ in range(3):
                    nh = 7 if b_ < 2 else 2
                    P.op("dve", lambda e, b_=b_, nh=nh: e.tensor_tensor(
                        out=rd16[:, b_ * 7:b_ * 7 + nh],
                        in0=bkf(b_)[:, 0:nh * 65].rearrange("p (h c) -> p h c", c=65)[:, :, 64],
                        in1=sink_e[:, b_ * 7:b_ * 7 + nh], op=ALU.add),
                        reads=BKr(b_) + [B_cB], writes=[B_rd16])
                P.op("dve", lambda e: e.reciprocal(out=rd16[:], in_=rd16[:]), writes=[B_rd16])
                for b_ in range(3):
                    nh = 7 if b_ < 2 else 2
                    P.op("dve", lambda e, b_=b_, nh=nh: e.tensor_tensor(
                        out=osb[:, b_ * 7:b_ * 7 + nh, :],
                        in0=bkf(b_)[:, 0:nh * 65].rearrange("p (h c) -> p h c", c=65)[:, :, 0:64],
                        in1=rd16[:, b_ * 7:b_ * 7 + nh].unsqueeze(2).to_broadcast([128, nh, 64]), op=ALU.mult),
                        reads=BKr(b_) + [B_rd16], writes=[B_osb])

            def out_proj(lhs_of_chunk, lhs_bufs, r0, r1, banks2):
                for half, r in ((0, r0), (1, r1)):
                    w = ring[r][:].rearrange("p (c n) -> p c n", n=1024)
                    for dh in range(2):
                        for c4 in range(4):
                            c = half * 4 + c4
                            P.op("pe", lambda e, dh=dh, c4=c4, c=c, w=w: e.matmul(
                                bkf(banks2[dh]), lhs_of_chunk(c), w[:, c4, dh * 512:(dh + 1) * 512],
                                start=(c == 0), stop=(c == 7), skip_group_check=True),
                                reads=list(lhs_bufs) + [B_ring[r]], writes=BKr(banks2[dh]), sig=(c4 == 3))

            def residual_from_banks(sl, banks2):
                for dh in range(2):
                    xa = xs[:, sl, dh * 512:(dh + 1) * 512]
                    P.op("dve", lambda e, xa=xa, dh=dh: e.scalar_tensor_tensor(out=xa, in0=xa, scalar=ALPHA, in1=bkf(banks2[dh]),
                                                                               op0=ALU.mult, op1=ALU.add),
                         reads=BKr(banks2[dh]), writes=[B_xs[sl]])

            def set_par(par):
                nonlocal xs, B_xs, mv, bst, B_st
                cur["par"] = par
                xs = xs_all[:, par]
                B_xs = B_xs_all[par]
                mv = mv_all[:, par]
                bst = bst_all[:, par]
                B_st = B_st_all[par]

            def layer0_front(slots):
                nsl = len(slots)
                for sl, s in enumerate(slots):
                    P.dma("sp", xs[:, sl, :], xq[s * 128:(s + 1) * 128, :], "xs%d_%d" % (cur["par"], sl), writes=[B_xs[sl]])
                r0 = next_slab()
                r1 = next_slab()
                pb = [[4, 5], [6, 7]]
                for sl, s in enumerate(slots):
                    out_proj(lambda c, s=s: OT[:, s, c, :], [B_OT[s]], r0, r1, pb[sl])
                release_and_prefetch()
                for sl in range(nsl):
                    residual_from_banks(sl, pb[sl])
                for sl in range(nsl):
                    ln_part1a(sl, 0)
                for sl in range(nsl):
                    ln_part1b(sl, 0)

            def layer0_back(nsl, a_done=False):
                for sl in range(nsl):
                    ln_part2(sl, 0, do_a=not a_done)
                mlp(nsl)
                ln_stage(nsl, 1)

            def save_kv(dst_k, dst_v, sl, bdst):
                P.op("pool", lambda e: e.tensor_copy(out=dst_k[:], in_=kT[:, :, sl * 128:(sl + 1) * 128]), reads=[B_kT], writes=[bdst])
                P.op("pool", lambda e: e.tensor_copy(out=dst_v[:], in_=Vaug[:, sl, :, :]), reads=[B_Va], writes=[bdst])

            set_par(0)
            layer0_front([2 * CH])
            layer0_back(1)
            swa_kv(1, None)
            save_kv(kT_e, Va_e, 0, B_ext)
            release_and_prefetch()

            set_par(1)
            layer0_front([0, 1])
            for bi in range(CH):
                slots = [2 * bi, 2 * bi + 1]
                par_now = (bi + 1) % 2
                set_par(par_now)
                layer0_back(2, a_done=(bi > 0))
                swa_kv(2, slots)
                swa_q(2)
                for sl, s in enumerate(slots):
                    keys = []
                    if s == 0:
                        keys.append((kT_e[:], Va_e[:], 0, flags[:, 0:1], [B_ext]))
                    elif s == CH:
                        keys.append((kT_e[:], Va_e[:], 0, flags[:, 1:2], [B_ext]))
                        keys.append((kT_p[:], Va_p[:], 0, flags[:, 2:3], [B_prev]))
                    elif sl == 0:
                        keys.append((kT_p[:], Va_p[:], 0, None, [B_prev]))
                    else:
                        keys.append((kT[:, :, 0:128], Vaug[:, 0, :, :], 0, None, [B_kT, B_Va]))
                    keys.append((kT[:, :, sl * 128:(sl + 1) * 128], Vaug[:, sl, :, :], 1, None, [B_kT, B_Va]))
                    swa_attn(sl, keys)
                    transposes([osb[:, 2 * c:2 * c + 2, :].rearrange("p h d -> p (h d)") for c in range(8)], 3, [B_osb])
                    P.op("act", lambda e: e.copy(out=oT2[:].rearrange("p c t -> p (c t)"), in_=bkh(3)), reads=BKr(3), writes=[B_oT2])
                    if sl == 0:
                        r0 = next_slab()
                        r1 = next_slab()
                    out_proj(lambda c: oT2[:, c, :], [B_oT2], r0, r1, [4, 5])
                    residual_from_banks(sl, [4, 5])
                    if SPLIT_LN:
                        ln_part1(sl, 2)
                save_kv(kT_p, Va_p, 1, B_prev)
                release_and_prefetch()
                if SPLIT_LN:
                    for sl in range(2):
                        ln_part2(sl, 2)
                else:
                    ln_stage(2, 2)

                def front_next(bi=bi, par_now=par_now):
                    if bi + 1 < CH:
                        set_par((bi + 2) % 2)
                        layer0_front([2 * bi + 2, 2 * bi + 3])
                        set_par(par_now)
                mlp(2, hook=front_next)
                if bi + 1 < CH:
                    set_par((bi + 2) % 2)
                    for sl in range(2):
                        ln_part2(sl, 0, do_b=False)
                    set_par(par_now)
                ln_stage(2, 3, need_aT=False)
                for sl, s in enumerate(slots):
                    P.dma("sp", out[s * 128:(s + 1) * 128, :], xs[:, sl, :], "out%d_%d" % (cur["par"], sl), reads=[B_xs[sl]])
            dbg_dump("xs", xs.rearrange("p s d -> p (s d)"))
            dbg_dump("aT", aT[:].rearrange("p c t -> p (c t)"))
            dbg_dump("osb", osb[:].rearrange("p h d -> p (h d)"))
            dbg_dump("qT", qT[:].rearrange("p h t -> p (h t)"))
            dbg_dump("kT", kT[:].rearrange("p h t -> p (h t)"))
            dbg_dump("ET", ET[:].rearrange("p a b -> p (a b)"))
            for nm in ("out0_0", "out0_1", "out1_0", "out1_1"):
                if nm in P.sem:
                    nc.sync.wait_ge(P.sem[nm], P.cnt[nm])
            P.check()
    return nc


def _t5_bucket_np(dist):
    n = np.maximum(dist, 0)
    max_exact = 16
    nf = np.maximum(n, 1).astype(np.float32)
    large = max_exact + (np.log(nf / np.float32(max_exact)) / np.float32(math.log(128 / max_exact))
                         * np.float32(32 - max_exact)).astype(np.int32)
    large = np.minimum(large, 31)
    return np.where(n < max_exact, n, large)


def _rope_tables(pos):
    half = 32
    inv = (np.float32(10000.0) ** (-(np.arange(half, dtype=np.float32) / np.float32(half)))).astype(np.float32)
    ang = (pos.astype(np.float32)[:, None] * inv[None, :]).astype(np.float32)
    return np.concatenate([np.cos(ang.astype(np.float64)), np.sin(ang.astype(np.float64))], axis=1).astype(np.float32)


_NC_CACHE = {}


def _run(inputs, NB, n_batch):
    CH = NB // 4
    NS = 2 * CH + 1
    S = NB * 128
    bf = ml_dtypes.bfloat16
    f32 = np.float32
    x = np.ascontiguousarray(inputs["x"], dtype=f32)
    assert x.shape == (n_batch, S, D)
    if NB not in _NC_CACHE:
        _NC_CACHE[NB] = build_nc(NB)
    nc = _NC_CACHE[NB]

    ident = np.eye(128, dtype=f32).astype(bf)
    kk = np.arange(128)[:, None]
    qq = np.arange(128)[None, :]
    tri = np.where(kk > qq, NEG, 0.0).astype(f32)
    triT = np.tile(tri, (1, 4)).astype(bf)
    onehot = np.zeros((64, S), dtype=f32)
    for kb in range(NB):
        onehot[kb, kb * 128:(kb + 1) * 128] = 1.0
    onehot = onehot.astype(bf)
    css = _rope_tables(np.arange(S))
    rel_bias = np.asarray(inputs["rel_bias"], dtype=f32)
    ii = np.arange(128)[:, None]
    jj = np.arange(256)[None, :]
    dist = ii + 128 - jj
    bias = rel_bias[_t5_bucket_np(dist)]
    valid = (dist >= 0) & (dist < 128)
    bias = np.where(valid[:, :, None], bias, f32(NEG)).astype(f32)
    biasT = np.ascontiguousarray(bias.transpose(1, 2, 0))
    biasT = biasT.reshape(2, 128, 16 * 128)

    w_uk = np.asarray(inputs["mla_w_uk"], dtype=f32)[0]
    w_ukT = np.ascontiguousarray(w_uk.transpose(2, 1, 0)).reshape(128, H * CKV)
    ln_g = np.stack([inputs["ln_mix_g"][0], inputs["ln_mlp_g"][0], inputs["ln_mix_g"][1], inputs["ln_mlp_g"][1]]).astype(f32)
    ln_b = np.stack([inputs["ln_mix_b"][0], inputs["ln_mlp_b"][0], inputs["ln_mix_b"][1], inputs["ln_mlp_b"][1]]).astype(f32)
    common = {
        "onehot": onehot, "ident": ident, "triT": triT, "css": css,
        "mla_w_in": np.ascontiguousarray(inputs["mla_w_in"][0], dtype=f32),
        "mla_g_q": np.ascontiguousarray(inputs["mla_g_q"][0], dtype=f32),
        "mla_g_kv": np.ascontiguousarray(inputs["mla_g_kv"][0], dtype=f32),
        "mla_w_uq": np.ascontiguousarray(np.concatenate(
            [np.asarray(inputs["mla_w_uq"][0], dtype=f32)[:, :, 0:NOPE].reshape(QR, H * NOPE),
             np.asarray(inputs["mla_w_uq"][0], dtype=f32)[:, :, NOPE:NOPE + RD].reshape(QR, H * RD)], axis=1)),
        "w_ukT": w_ukT,
        "mla_w_uv": np.ascontiguousarray(inputs["mla_w_uv"][0], dtype=f32).reshape(CKV, H * 128),
        "mla_w_o": np.ascontiguousarray(inputs["mla_w_o"][0], dtype=f32),
        "kv_w_shared": np.ascontiguousarray(inputs["kv_w_shared"], dtype=f32),
        "swa_w_q": np.ascontiguousarray(inputs["swa_w_q"][0], dtype=f32),
        "swa_w_o": np.ascontiguousarray(inputs["swa_w_o"][0], dtype=f32),
        "swa_sinks": np.ascontiguousarray(inputs["swa_sinks"][0], dtype=f32),
        "biasT": biasT,
        "mlp_w_up": np.ascontiguousarray(inputs["mlp_w_up"], dtype=f32),
        "mlp_w_down": np.ascontiguousarray(inputs["mlp_w_down"], dtype=f32),
        "ln_g": ln_g, "ln_b": ln_b,
        "ln_gT": np.ascontiguousarray(ln_g.reshape(4, 8, 128).transpose(2, 0, 1)).reshape(128, 32),
        "ln_bT": np.ascontiguousarray(ln_b.reshape(4, 8, 128).transpose(2, 0, 1)).reshape(128, 32),
    }
    xT_all = np.ascontiguousarray(x.reshape(n_batch, NB, 128, 8, 128).transpose(0, 1, 4, 3, 2)).reshape(n_batch, S, D)
    in_maps = []
    for core in range(2 * n_batch):
        b, par = core // 2, core % 2
        blocks = _slot_blocks(par, NB)
        rows = np.concatenate([np.arange(bl * 128, (bl + 1) * 128) for bl in blocks])
        maskrows = np.zeros((NS, 64, 128), dtype=f32)
        for s, bl in enumerate(blocks):
            maskrows[s, bl:, :] = NEG
        flags = np.zeros((128, 4), dtype=f32)
        if par == 0:
            flags[:, 0] = 0.0; flags[:, 1] = 1.0; flags[:, 2] = 0.0
        else:
            flags[:, 0] = 1.0; flags[:, 1] = 0.0; flags[:, 2] = 1.0
        m = dict(common)
        m["xq"] = np.ascontiguousarray(x[b][rows])
        m["xseqT"] = xT_all[b]
        m["xqT"] = np.ascontiguousarray(xT_all[b][rows])
        m["csq"] = np.ascontiguousarray(css[rows])
        m["maskrows"] = maskrows.astype(bf)
        m["swa_flags"] = flags
        in_maps.append(m)
    res = run_bass_kernel_spmd(nc, in_maps, core_ids=list(range(2 * n_batch)))
    if DEBUG:
        global _LAST_RES
        _LAST_RES = res.results
    outp = np.zeros((n_batch, S, D), dtype=f32)
    for core in range(2 * n_batch):
        b, par = core // 2, core % 2
        blocks = _slot_blocks(par, NB)[:2 * CH]
        o = res.results[core]["out"]
        for s, bl in enumerate(blocks):
            outp[b, bl * 128:(bl + 1) * 128] = o[s * 128:(s + 1) * 128]
    return outp


def kernel(**inputs):
    return _run(inputs, 64, 4)
```

```python
import math
import numpy as np
import ml_dtypes
import concourse.bass as bass
import concourse.mybir as mybir
from concourse.bass_utils import run_bass_kernel_spmd

F32 = mybir.dt.float32
BF16 = mybir.dt.bfloat16
AF = mybir.ActivationFunctionType
ALU = mybir.AluOpType
AX = mybir.AxisListType

D = 1024
QR = 384
CKV = 256
RD = 64
H = 8
NOPE = 128
FF = 4096
NEG = -30000.0
ALPHA = 4 ** 0.25
LN_EPS = 1e-5
RMS_EPS = 1e-6
DEBUG = False
DEFER_POST = True
SPLIT_LN = True
NSLAB = 39
SLAB = 4096
RING = 5


class Buf:
    __slots__ = ("w", "r", "name", "excl")

    def __init__(self, name, excl=False):
        self.name = name
        self.w = None
        self.r = {}
        self.excl = excl


class Prog:
    def __init__(self, nc, stack):
        self.nc = nc
        self.eng = {"pe": nc.tensor, "act": nc.scalar, "dve": nc.vector, "pool": nc.gpsimd, "sp": nc.sync}
        self.sem = {}
        self.cnt = {}
        self.pending = {}
        self.waited = {}
        self.stack = stack
        self.log = {e: [] for e in self.eng}
        for e in self.eng:
            self.sem[e] = stack.enter_context(nc.semaphore("pg_" + e))
            self.cnt[e] = 0
            self.pending[e] = False

    def dsem(self, name):
        if name not in self.sem:
            self.sem[name] = self.stack.enter_context(self.nc.semaphore("d_" + name))
            self.cnt[name] = 0
        return name

    def _wait(self, e, t):
        if t is None:
            return
        p, v = t
        if v is None:
            v = self.cnt[p]
        if p == e and e == "pe":
            return
        if self.waited.get((e, p), 0) >= v:
            return
        if p in self.eng:
            assert v <= self.cnt[p], ("deadlock: waiting on unsignalled", e, p, v, self.cnt[p])
        self.eng[e].wait_ge(self.sem[p], v)
        self.log[e].append(("wait", p, v))
        self.waited[(e, p)] = v

    def _deps(self, e, reads, writes):
        for b in reads:
            self._wait(e, b.w)
        for b in writes:
            self._wait(e, b.w)
            for t in b.r.items():
                self._wait(e, t)

    def _mark(self, t, reads, writes):
        for b in reads:
            if t[1] is None:
                b.r[t[0]] = None
            elif b.r.get(t[0], 0) is not None and b.r.get(t[0], 0) < t[1]:
                b.r[t[0]] = t[1]
        for b in writes:
            b.w = t
            b.r = {}

    def op(self, e, fn, reads=(), writes=(), sig=True):
        ex = [b for b in reads if b.excl]
        if ex:
            reads = [b for b in reads if not b.excl]
            writes = list(writes) + ex
        self._deps(e, reads, writes)
        ins = fn(self.eng[e])
        if sig:
            self.cnt[e] += 1
            ins.then_inc(self.sem[e], 1)
            self.log[e].append(("inc", e, 1))
            self.pending[e] = False
            t = (e, self.cnt[e])
        else:
            assert e == "pe"
            self.pending[e] = True
            t = (e, self.cnt[e] + 1)
        self._mark(t, reads, writes)
        return t

    def check(self):
        val = {k: 0 for k in self.sem}
        pos = {e: 0 for e in self.eng}
        progress = True
        while progress:
            progress = False
            for e in self.eng:
                lg = self.log[e]
                while pos[e] < len(lg):
                    kind, p, v = lg[pos[e]]
                    if kind == "wait":
                        if val[p] < v:
                            break
                    else:
                        val[p] += v
                    pos[e] += 1
                    progress = True
        stuck = {e: (pos[e], len(self.log[e]), self.log[e][pos[e]], val[self.log[e][pos[e]][1]])
                 for e in self.eng if pos[e] < len(self.log[e])}
        assert not stuck, ("DEADLOCK", stuck)

    def dma(self, q, out, in_, semname, reads=(), writes=(), group=False):
        self.dsem(semname)
        self._deps(q, reads, writes)
        self.cnt[semname] += 16
        self.eng[q].dma_start(out=out, in_=in_).then_inc(self.sem[semname], 16)
        self.log[q].append(("inc", semname, 16))
        t = (semname, None if group else self.cnt[semname])
        self._mark(t, reads, writes)
        return t


def _slot_blocks(core_parity, NB):
    CH = NB // 4
    if core_parity == 0:
        return list(range(0, CH)) + list(range(3 * CH, 4 * CH)) + [3 * CH - 1]
    return list(range(CH, 2 * CH)) + list(range(2 * CH, 3 * CH)) + [CH - 1]


def _n_off(s, NB):
    CH = NB // 4
    if s < CH:
        return CH + s
    if s < 2 * CH:
        return 3 * CH + (s - CH)
    return 3 * CH - 1


def build_nc(NB):
    CH = NB // 4
    NS = 2 * CH + 1
    S = NB * 128
    nc = bass.Bass("TRN2", target_bir_lowering=False)

    def din(name, shape, dt=F32):
        return nc.dram_tensor(name, list(shape), dt, kind="ExternalInput").ap()

    xq = din("xq", [NS * 128, D])
    xseqT = din("xseqT", [S, D])
    xqT = din("xqT", [NS * 128, D])
    csq = din("csq", [NS * 128, 64])
    css = din("css", [S, 64])
    maskrows = din("maskrows", [NS, 64, 128], BF16)
    onehot = din("onehot", [64, S], BF16)
    ident_d = din("ident", [128, 128], BF16)
    triT_d = din("triT", [128, 512], BF16)
    w_in = din("mla_w_in", [D, QR + CKV + RD])
    g_q = din("mla_g_q", [QR])
    g_kv = din("mla_g_kv", [CKV])
    w_uq = din("mla_w_uq", [QR, H * 192])
    w_ukT = din("w_ukT", [128, H * CKV])
    w_uv = din("mla_w_uv", [CKV, H * 128])
    w_o = din("mla_w_o", [D, D])
    kv_w = din("kv_w_shared", [D, 512])
    swa_wq = din("swa_w_q", [D, D])
    swa_wo = din("swa_w_o", [D, D])
    sinks = din("swa_sinks", [16])
    biasT = din("biasT", [2, 128, 16 * 128])
    swa_flags = din("swa_flags", [128, 4])
    w_up = din("mlp_w_up", [2, D, FF])
    w_down = din("mlp_w_down", [2, FF, D])
    ln_g = din("ln_g", [4, D])
    ln_b = din("ln_b", [4, D])
    ln_gT = din("ln_gT", [128, 32])
    ln_bT = din("ln_bT", [128, 32])
    out = nc.dram_tensor("out", [2 * CH * 128, D], F32, kind="ExternalOutput").ap()
    wscr = nc.dram_tensor("wscr", [NSLAB, 128, SLAB], BF16).ap()
    dbg = {}

    def dbg_dump(name, ap2d):
        if not DEBUG:
            return
        shp = list(ap2d.shape)
        d = nc.dram_tensor("dbg_" + name, shp, ap2d.dtype, kind="ExternalOutput").ap()
        nc.all_engine_barrier()
        P.dma("sp", d, ap2d, "out0_0")
        dbg[name] = d

    from contextlib import ExitStack
    with ExitStack() as st:
        P = Prog(nc, st)

        def sb(name, shape, dt, stack=st):
            return stack.enter_context(nc.sbuf_tensor("s_" + name, list(shape), dt))

        banks = [st.enter_context(nc.psum_tensor("bank%d" % i, [128, 512], F32)) for i in range(8)]
        BQ = [[Buf("bank%d" % i, excl=True)] * 4 for i in range(8)]

        class _BK:
            def __getitem__(self, i):
                return BQ[i]
        BKL = _BK()

        def BKr(i, lo=0, hi=512):
            return [BQ[i][0]]

        def bkf(i):
            return banks[i][:]

        def bkh(i):
            return banks[i][:].bitcast(BF16)

        ident = sb("ident", [128, 128], BF16)
        triT = sb("triT", [128, 512], BF16)
        ones_c = sb("ones_c", [128, 1], BF16)
        OT = sb("OT", [128, NS, 8, 128], BF16)
        B_const = Buf("const")
        B_OT = [Buf("OT%d" % s) for s in range(NS)]

        P.dma("sp", ident[:], ident_d, "c0", writes=[B_const], group=True)
        P.dma("sp", triT[:], triT_d, "c0", writes=[B_const], group=True)
        B_ones = Buf("ones")
        P.op("dve", lambda e: e.memset(ones_c[:], 1.0), writes=[B_ones])

        def transposes(src_aps, bank, reads, n_part_out=128):
            t = None
            hv = bkh(bank)
            for i, a in enumerate(src_aps):
                m = a.shape[-1]
                last = i == len(src_aps) - 1
                t = P.op("pe", lambda e, a=a, i=i, m=m: e.transpose(hv[0:m, i * 128:(i + 1) * 128], a, ident[:]),
                         reads=list(reads) + [B_const], writes=BKr(bank), sig=last)
            return t

        with ExitStack() as sa:
            def sba(name, shape, dt):
                return sb(name, shape, dt, sa)

            KT_c = sba("KT_c", [128, 2, S], BF16)
            KT_r = sba("KT_r", [128, S], BF16)
            Vst = sba("Vst", [128, NB, CKV], BF16)
            w_inq_s = sba("w_inq", [128, 8, QR], BF16)
            w_inkv_s = sba("w_inkv", [128, 8, CKV + RD], BF16)
            w_uq_s = sba("w_uq", [128, 3, H * 192], BF16)
            w_ukT_s = sba("w_ukT", [128, H, CKV], BF16)
            w_uv_s = sba("w_uv", [128, 2, H * 128], BF16)
            gq_rep = sba("gq_rep", [128, QR], F32)
            gkv_rep = sba("gkv_rep", [128, CKV], F32)
            B_wA = Buf("wA")
            B_K = [Buf("K%d" % k) for k in range(NB)]
            B_KTr_const = Buf("ktr_const")

            P.dma("pool", w_inq_s[:], w_in[:, 0:QR].rearrange("(c p) n -> p c n", p=128), "c1", writes=[B_wA], group=True)
            P.dma("pool", w_inkv_s[:], w_in[:, QR:QR + CKV + RD].rearrange("(c p) n -> p c n", p=128), "c1", writes=[B_wA], group=True)
            P.dma("pool", w_uq_s[:], w_uq.rearrange("(c p) n -> p c n", p=128), "c1", writes=[B_wA], group=True)
            P.dma("pool", w_ukT_s[:], w_ukT.rearrange("p (h c) -> p h c", h=H), "c1", writes=[B_wA], group=True)
            P.dma("pool", w_uv_s[:], w_uv.rearrange("(c p) n -> p c n", p=128), "c1", writes=[B_wA], group=True)
            P.dma("sp", gq_rep[:], g_q.partition_broadcast(128), "c0", writes=[B_wA], group=True)
            P.dma("sp", gkv_rep[:], g_kv.partition_broadcast(128), "c0", writes=[B_wA], group=True)
            P.dma("sp", KT_r[64:128, :], onehot, "c0", writes=[B_KTr_const], group=True)

            xb = [sba("xb%d" % i, [128, D], BF16) for i in range(2)]
            xb += [OT[:, 0].rearrange("p h t -> p (h t)"), OT[:, 1].rearrange("p h t -> p (h t)")]
            B_xb = [Buf("xb%d" % i) for i in range(4)]
            cs = [sba("cs%d" % i, [128, 64], F32) for i in range(4)]
            B_cs = [Buf("cs%d" % i) for i in range(4)]
            xT = sba("xT", [128, 8, 128], BF16)
            B_xT = Buf("xT")
            h_sb = sba("h_sb", [128, QR + CKV + RD], F32)
            B_h = Buf("h_sb")
            stat = sba("stat", [128, 8], F32)
            B_stat = Buf("stat")
            rtmp = sba("rtmp", [128, 2, 8, 64], F32)
            B_rtmpA = Buf("rtmpA")
            B_rtmpB = Buf("rtmpB")
            kr_sb = sba("kr_sb", [128, RD], BF16)
            B_kr = Buf("kr")
            cq = sba("cq", [128, QR], BF16)
            B_cq = Buf("cq")
            cqT = sba("cqT", [128, 3, 128], BF16)
            B_cqT = Buf("cqT")
            Vown = sba("Vown", [128, CKV], BF16)
            B_Vown = Buf("Vown")
            KTown_c = sba("KTown_c", [128, 2, 128], BF16)
            KTown_r = sba("KTown_r", [128, 128], BF16)
            B_KTown = Buf("KTown")
            qn = h_sb[:].bitcast(BF16)[:, 0:H * NOPE].rearrange("p (h n) -> p h n", h=H)
            B_qn = B_h
            qr_f = sba("qr_f", [128, H, RD], F32)
            B_qrf = Buf("qr_f")
            junk = qr_f[:].rearrange("p h r -> p (h r)")[:, 0:QR]
            B_junk = B_qrf
            qr = sba("qr", [128, H, RD], BF16)
            B_qr = Buf("qr")
            qnT = xT
            B_qnT = B_xT
            QT_c = sba("QT_c", [128, 2, H, 128], BF16)
            QT_r = sba("QT_r", [128, H, 128], BF16)
            B_QTc = Buf("QT_c")
            B_QTr_lo = Buf("QT_r_lo")
            B_QTr_hi = Buf("QT_r_hi")
            PT = [sba("PT%d" % i, [128, 512], BF16) for i in range(3)]
            B_PT = [Buf("PT%d" % i) for i in range(3)]
            olat = rtmp[:, 0].bitcast(BF16).rearrange("p h f -> p (h f)").rearrange("p (h c) -> p h c", h=4)
            B_olat = B_rtmpA
            olatT = rtmp[:, 1].bitcast(BF16).rearrange("p h f -> p (h f)").rearrange("p (h j t) -> p h j t", h=4, j=2)
            B_olatT = B_rtmpB
            rden = sba("rden", [128, 4], F32)
            B_rden = Buf("rden")

            P.op("dve", lambda e: e.memset(KTown_r[:], 0.0), writes=[B_KTown])

            def rms_scale(src_ap, n, col, eps):
                P.op("dve", lambda e: e.scalar_tensor_tensor(out=junk[:, 0:n], in0=src_ap, scalar=1.0, in1=src_ap,
                                                             op0=ALU.mult, op1=ALU.mult, accum_out=stat[:, col:col + 1]),
                     reads=[B_h], writes=[B_junk, B_stat])
                P.op("act", lambda e: e.activation(out=stat[:, col:col + 1], in_=stat[:, col:col + 1], func=AF.Ln,
                                                   scale=1.0 / n, bias=eps),
                     writes=[B_stat])
                P.op("act", lambda e: e.activation(out=stat[:, col:col + 1], in_=stat[:, col:col + 1], func=AF.Exp,
                                                   scale=-0.5),
                     writes=[B_stat])

            def rope(src3, dst3, nh, csb, bcs, bsrc, bdst, tmp=None, btmp=None):
                s4 = src3.rearrange("p h (t f) -> p h t f", t=2)
                tsrc = rtmp if tmp is None else tmp
                bA, bB = (B_rtmpA, B_rtmpB) if btmp is None else (btmp, btmp)
                A = tsrc[:, 0, 0:nh, :].rearrange("p h (t f) -> p h t f", t=2)
                Bm = tsrc[:, 1, 0:nh, :].rearrange("p h (t f) -> p h t f", t=2)
                cosb = csb[:, 0:32].unsqueeze(1).unsqueeze(1).to_broadcast([128, nh, 2, 32])
                sinb = csb[:, 32:64].unsqueeze(1).unsqueeze(1).to_broadcast([128, nh, 2, 32])
                P.op("dve", lambda e: e.tensor_tensor(out=A, in0=s4, in1=cosb, op=ALU.mult), reads=[bcs, bsrc], writes=[bA])
                P.op("dve", lambda e: e.tensor_tensor(out=Bm, in0=s4, in1=sinb, op=ALU.mult), reads=[bcs, bsrc], writes=[bB])
                d4 = dst3.rearrange("p h (t f) -> p h t f", t=2)
                P.op("dve", lambda e: e.tensor_tensor(out=d4[:, :, 0, :], in0=A[:, :, 0, :], in1=Bm[:, :, 1, :], op=ALU.subtract),
                     reads=[bA, bB], writes=[bdst])
                return P.op("dve", lambda e: e.tensor_tensor(out=d4[:, :, 1, :], in0=Bm[:, :, 0, :], in1=A[:, :, 1, :], op=ALU.add),
                            reads=[bA, bB], writes=[bdst])

            def load_x_block(src_rows, cs_rows, i):
                P.dma("pool", xb[i][:, :], src_rows, "xb%d" % i, writes=[B_xb[i]])
                P.dma("sp", cs[i][:], cs_rows, "cs%d" % i, writes=[B_cs[i]])

            BX = 0

            def make_xT(i):
                bx = BX
                transposes([xb[i][:, c * 128:(c + 1) * 128] for c in range(8)], bx, [B_xb[i]])
                P.op("act", lambda e: e.copy(out=xT[:].rearrange("p c t -> p (c t)"), in_=bkh(bx)),
                     reads=BKr(bx), writes=[B_xT])

            def kv_from_h(i, v_dst, bv):
                rms_scale(h_sb[:, QR:QR + CKV], CKV, 1, RMS_EPS)
                P.op("dve", lambda e: e.scalar_tensor_tensor(out=v_dst, in0=h_sb[:, QR:QR + CKV], scalar=stat[:, 1:2],
                                                             in1=gkv_rep[:], op0=ALU.mult, op1=ALU.mult),
                     reads=[B_h, B_stat, B_wA], writes=[bv])
                rope(h_sb[:, QR + CKV:QR + CKV + RD].unsqueeze(1), kr_sb[:].unsqueeze(1), 1, cs[i], B_cs[i], B_h, B_kr)

            scr_jobs = []

            def slab_view(idx, inner):
                return wscr[idx].rearrange("p (c n) -> p c n", n=inner)

            k = 0
            for half in range(2):
                scr_jobs.append((slab_view(k, 1024), w_o[half * 512:(half + 1) * 512, :].rearrange("(c p) n -> p c n", p=128)))
                k += 1
            for L in range(2):
                if L == 1:
                    scr_jobs.append((slab_view(k, 512), kv_w.rearrange("(c p) n -> p c n", p=128)))
                    k += 1
                    for wsrc in (swa_wq, swa_wo):
                        for half in range(2):
                            scr_jobs.append((slab_view(k, 1024), wsrc[half * 512:(half + 1) * 512, :].rearrange("(c p) n -> p c n", p=128)))
                            k += 1
                for j in range(8):
                    scr_jobs.append((slab_view(k, 512), w_up[L][:, j * 512:(j + 1) * 512].rearrange("(c p) n -> p c n", p=128)))
                    k += 1
                    scr_jobs.append((slab_view(k, 1024), w_down[L][j * 512:(j + 1) * 512, :].rearrange("(c p) n -> p c n", p=128)))
                    k += 1
            assert k == NSLAB
            B_scr = Buf("scr")

            def issue_scr(n):
                for _ in range(n):
                    if scr_jobs:
                        o, i_ = scr_jobs.pop(0)
                        P.dma("pool", o, i_, "scr", writes=[B_scr])

            NSET = 4
            xTr = [xb[i4][:, :].rearrange("p (c t) -> p c t", c=8) for i4 in range(4)]
            sets = []
            for j in range(NSET):
                if NS >= 2 + 2 * NSET:
                    A_ = OT[:, 2 + 2 * j].rearrange("p h t -> p (h t)").bitcast(F32)
                    B_ = OT[:, 3 + 2 * j].rearrange("p h t -> p (h t)").bitcast(F32)
                else:
                    A_ = sba("p1A%d" % j, [128, 512], F32)[:]
                    B_ = sba("p1B%d" % j, [128, 512], F32)[:]
                sets.append(dict(h=A_[:, 0:320], rA=A_[:, 320:384], rB=A_[:, 384:448], st=A_[:, 448:456],
                                 junk=B_[:, 0:256], kr=B_[:, 256:288].bitcast(BF16),
                                 B_h=Buf("p1h%d" % j), B_st=Buf("p1st%d" % j), B_junk=Buf("p1j%d" % j),
                                 B_r=Buf("p1r%d" % j), B_kr=Buf("p1kr%d" % j), bh=2 * j, bt=2 * j + 1))

            def p1_S0(k):
                i4 = k % 4
                P.dma("pool", xb[i4][:, :], xseqT[k * 128:(k + 1) * 128, :], "xb%d" % i4, writes=[B_xb[i4]])
                P.dma("sp", cs[i4][:], css[k * 128:(k + 1) * 128, :], "cs%d" % i4, writes=[B_cs[i4]])

            def p1_S1(k):
                i4 = k % 4
                W = sets[k % NSET]
                for c in range(8):
                    P.op("pe", lambda e, c=c: e.matmul(bkf(W["bh"])[:, 0:CKV + RD], xTr[i4][:, c, :], w_inkv_s[:, c, :],
                                                       start=(c == 0), stop=(c == 7)),
                         reads=[B_xb[i4], B_wA], writes=BKr(W["bh"]), sig=(c == 7))
                P.op("act", lambda e: e.copy(out=W["h"], in_=bkf(W["bh"])[:, 0:CKV + RD]), reads=BKr(W["bh"]), writes=[W["B_h"]])

            def p1_S2(k):
                W = sets[k % NSET]
                P.op("dve", lambda e: e.scalar_tensor_tensor(out=W["junk"], in0=W["h"][:, 0:CKV], scalar=1.0, in1=W["h"][:, 0:CKV],
                                                             op0=ALU.mult, op1=ALU.mult, accum_out=W["st"][:, 0:1]),
                     reads=[W["B_h"]], writes=[W["B_junk"], W["B_st"]])
                P.op("act", lambda e: e.activation(out=W["st"][:, 0:1], in_=W["st"][:, 0:1], func=AF.Ln, scale=1.0 / CKV, bias=RMS_EPS),
                     writes=[W["B_st"]])
                P.op("act", lambda e: e.activation(out=W["st"][:, 0:1], in_=W["st"][:, 0:1], func=AF.Exp, scale=-0.5), writes=[W["B_st"]])

            def p1_S3(k):
                i4 = k % 4
                W = sets[k % NSET]
                P.op("dve", lambda e: e.scalar_tensor_tensor(out=Vst[:, k, :], in0=W["h"][:, 0:CKV], scalar=W["st"][:, 0:1],
                                                             in1=gkv_rep[:], op0=ALU.mult, op1=ALU.mult),
                     reads=[W["B_h"], W["B_st"], B_wA], writes=[B_K[k]])
                s4 = W["h"][:, CKV:CKV + RD].rearrange("p (t f) -> p t f", t=2)
                A4 = W["rA"].rearrange("p (t f) -> p t f", t=2)
                B4 = W["rB"].rearrange("p (t f) -> p t f", t=2)
                cosb = cs[i4][:, 0:32].unsqueeze(1).to_broadcast([128, 2, 32])
                sinb = cs[i4][:, 32:64].unsqueeze(1).to_broadcast([128, 2, 32])
                P.op("dve", lambda e: e.tensor_tensor(out=A4, in0=s4, in1=cosb, op=ALU.mult), reads=[W["B_h"], B_cs[i4]], writes=[W["B_r"]])
                P.op("dve", lambda e: e.tensor_tensor(out=B4, in0=s4, in1=sinb, op=ALU.mult), reads=[W["B_h"], B_cs[i4]], writes=[W["B_r"]])
                d4 = W["kr"].rearrange("p (t f) -> p t f", t=2)
                P.op("dve", lambda e: e.tensor_tensor(out=d4[:, 0, :], in0=A4[:, 0, :], in1=B4[:, 1, :], op=ALU.subtract),
                     reads=[W["B_r"]], writes=[W["B_kr"]])
                P.op("dve", lambda e: e.tensor_tensor(out=d4[:, 1, :], in0=B4[:, 0, :], in1=A4[:, 1, :], op=ALU.add),
                     reads=[W["B_r"]], writes=[W["B_kr"]])

            def p1_S4(k):
                W = sets[k % NSET]
                transposes([Vst[:, k, 0:128], Vst[:, k, 128:256], W["kr"]], W["bt"], [B_K[k], W["B_kr"]])
                h2 = bkh(W["bt"])
                P.op("act", lambda e: e.copy(out=KT_c[:, :, k * 128:(k + 1) * 128], in_=h2[:, 0:256].rearrange("p (j t) -> p j t", j=2)),
                     reads=BKr(W["bt"]), writes=[B_K[k]])
                P.op("dve", lambda e: e.tensor_copy(out=KT_r[0:64, k * 128:(k + 1) * 128], in_=h2[0:64, 256:384]),
                     reads=BKr(W["bt"]), writes=[B_K[k]])

            for t in range(NB + 3):
                if t < NB:
                    p1_S0(t)
                if 0 <= t - 3 < NB:
                    p1_S4(t - 3)
                if 0 <= t - 2 < NB:
                    p1_S3(t - 2)
                if 0 <= t - 1 < NB:
                    p1_S2(t - 1)
                if t < NB:
                    p1_S1(t)
            nc.all_engine_barrier()
            scr_per_slot = -(-len(scr_jobs) // max(1, min(NS - 1, 20)))

            scale = float((NOPE + RD) ** -0.5)
            rtk = sba("rtk", [128, 2, 1, RD], F32)
            B_rtk = Buf("rtk")

            def pre1(s):
                i = s % 2
                P.dma("pool", xb[i][:, :], xqT[s * 128:(s + 1) * 128, :], "xb%d" % i, writes=[B_xb[i]])
                P.dma("sp", cs[i][:], csq[s * 128:(s + 1) * 128, :], "cs%d" % i, writes=[B_cs[i]])
                xTs = xb[i][:, :].rearrange("p (c t) -> p c t", c=8)
                issue_scr(scr_per_slot)
                P.dma("sp", QT_r[64:128, :, :], maskrows[s].unsqueeze(1).to_broadcast([64, H, 128]), "mrow",
                      writes=[B_QTr_hi])
                for c in range(8):
                    P.op("pe", lambda e, c=c: e.matmul(bkf(0)[:, 0:QR], xTs[:, c, :], w_inq_s[:, c, :],
                                                       start=(c == 0), stop=(c == 7)),
                         reads=[B_xb[i], B_wA], writes=BKr(0), sig=(c == 7))
                for c in range(8):
                    P.op("pe", lambda e, c=c: e.matmul(bkf(1)[:, 0:CKV + RD], xTs[:, c, :], w_inkv_s[:, c, :],
                                                       start=(c == 0), stop=(c == 7)),
                         reads=[B_xb[i], B_wA], writes=BKr(1), sig=(c == 7))
                P.op("act", lambda e: e.copy(out=h_sb[:, 0:QR], in_=bkf(0)[:, 0:QR]), reads=BKr(0), writes=[B_h])
                P.op("act", lambda e: e.copy(out=h_sb[:, QR:QR + CKV + RD], in_=bkf(1)[:, 0:CKV + RD]),
                     reads=BKr(1), writes=[B_h])
                P.op("dve", lambda e: e.scalar_tensor_tensor(out=junk[:, 0:QR], in0=h_sb[:, 0:QR], scalar=1.0 / QR, in1=h_sb[:, 0:QR],
                                                             op0=ALU.mult, op1=ALU.mult, accum_out=stat[:, 0:1]),
                     reads=[B_h], writes=[B_junk, B_stat])
                P.op("dve", lambda e: e.scalar_tensor_tensor(out=junk[:, 0:CKV], in0=h_sb[:, QR:QR + CKV], scalar=1.0 / CKV,
                                                             in1=h_sb[:, QR:QR + CKV], op0=ALU.mult, op1=ALU.mult, accum_out=stat[:, 1:2]),
                     reads=[B_h], writes=[B_junk, B_stat])
                P.op("act", lambda e: e.activation(out=stat[:, 0:2], in_=stat[:, 0:2], func=AF.Ln, bias=RMS_EPS), writes=[B_stat])
                P.op("act", lambda e: e.activation(out=stat[:, 0:2], in_=stat[:, 0:2], func=AF.Exp, scale=-0.5), writes=[B_stat])
                P.op("dve", lambda e: e.scalar_tensor_tensor(out=cq[:], in0=h_sb[:, 0:QR], scalar=stat[:, 0:1],
                                                             in1=gq_rep[:], op0=ALU.mult, op1=ALU.mult),
                     reads=[B_h, B_stat, B_wA], writes=[B_cq])
                P.op("dve", lambda e: e.scalar_tensor_tensor(out=Vown[:], in0=h_sb[:, QR:QR + CKV], scalar=stat[:, 1:2],
                                                             in1=gkv_rep[:], op0=ALU.mult, op1=ALU.mult),
                     reads=[B_h, B_stat, B_wA], writes=[B_Vown])
                rope(h_sb[:, QR + CKV:QR + CKV + RD].unsqueeze(1), kr_sb[:].unsqueeze(1), 1, cs[i], B_cs[i], B_h, B_kr, tmp=rtk, btmp=B_rtk)

            pre1(0)
            for s in range(NS):
                i = s % 2
                n_off = _n_off(s, NB)
                transposes([cq[:, 0:128], cq[:, 128:256], cq[:, 256:384], Vown[:, 0:128], Vown[:, 128:256], kr_sb[:]],
                           5, [B_cq, B_Vown, B_kr])
                h0 = bkh(5)
                P.op("act", lambda e: e.copy(out=cqT[:].rearrange("p c t -> p (c t)"), in_=h0[:, 0:384]),
                     reads=BKr(5), writes=[B_cqT])
                P.op("dve", lambda e: e.tensor_copy(out=KTown_c[:].rearrange("p c t -> p (c t)"), in_=h0[:, 384:640]),
                     reads=BKr(5), writes=[B_KTown])
                P.op("dve", lambda e: e.tensor_copy(out=KTown_r[0:64, :], in_=h0[0:64, 640:768]),
                     reads=BKr(5), writes=[B_KTown])
                for g in range(3):
                    for c in range(3):
                        P.op("pe", lambda e, g=g, c=c: e.matmul(bkf(3 + g), cqT[:, c, :], w_uq_s[:, c, g * 512:(g + 1) * 512],
                                                                start=(c == 0), stop=(c == 2)),
                             reads=[B_cqT, B_wA], writes=BKr(3 + g), sig=(c == 2))
                P.op("act", lambda e: e.copy(out=qn[:, 0:4, :].rearrange("p h n -> p (h n)"), in_=bkf(3)), reads=BKr(3), writes=[B_qn])
                P.op("dve", lambda e: e.tensor_copy(out=qn[:, 4:8, :].rearrange("p h n -> p (h n)"), in_=bkf(4)), reads=BKr(4), writes=[B_qn])
                P.op("act", lambda e: e.copy(out=qr_f[:].rearrange("p h r -> p (h r)"), in_=bkf(5)), reads=BKr(5), writes=[B_qrf])
                rope(qr_f[:], qr[:], H, cs[i], B_cs[i], B_qrf, B_qr)
                transposes([qn[:, h, :] for h in range(H)], 6, [B_qn])
                P.op("act", lambda e: e.copy(out=qnT[:].rearrange("p h t -> p (h t)"), in_=bkh(6)), reads=BKr(6), writes=[B_qnT])
                transposes([qr[:, h, :] for h in range(H)], 7, [B_qr])
                P.op("dve", lambda e: e.tensor_copy(out=QT_r[0:64, :, :].rearrange("p h t -> p (h t)"), in_=bkh(7)[0:64, :]),
                     reads=BKr(7), writes=[B_QTr_lo])
                for j in range(2):
                    for h in range(H):
                        bk = j * 2 + h // 4
                        P.op("pe", lambda e, j=j, h=h, bk=bk: e.matmul(bkf(bk)[:, (h % 4) * 128:(h % 4 + 1) * 128],
                                                                       w_ukT_s[:, h, j * 128:(j + 1) * 128], qnT[:, h, :],
                                                                       start=True, stop=True, skip_group_check=True),
                             reads=[B_qnT, B_wA], writes=BKr(bk), sig=(h % 4 == 3))
                for j in range(2):
                    for hg in range(2):
                        bk = j * 2 + hg
                        eng = "act" if hg == 0 else "dve"
                        dst = QT_c[:, j, hg * 4:(hg + 1) * 4, :].rearrange("p h t -> p (h t)")
                        if eng == "act":
                            P.op("act", lambda e, dst=dst, bk=bk: e.copy(out=dst, in_=bkf(bk)), reads=BKr(bk), writes=[B_QTc])
                        else:
                            P.op("dve", lambda e, dst=dst, bk=bk: e.tensor_copy(out=dst, in_=bkf(bk)), reads=BKr(bk), writes=[B_QTc])

                deferred = []
                pre_b = []
                for hg in range(2):
                    ob = [0, 1] if hg == 0 else [2, 3]
                    db = 4
                    sring = [5, 6, 7]
                    nun = n_off + 1

                    def s_mm(u):
                        r = sring[u % 3]
                        diag = (u == n_off)
                        hs = slice(hg * 4, hg * 4 + 4)
                        if diag:
                            l0, l1, l2 = KTown_c[:, 0, :], KTown_c[:, 1, :], KTown_r[:]
                            rd = [B_KTown]
                        else:
                            l0 = KT_c[:, 0, u * 128:(u + 1) * 128]
                            l1 = KT_c[:, 1, u * 128:(u + 1) * 128]
                            l2 = KT_r[:, u * 128:(u + 1) * 128]
                            rd = [B_K[u], B_KTr_const]
                        P.op("pe", lambda e: e.matmul(bkf(r), l0, QT_c[:, 0, hs, :].rearrange("p h t -> p (h t)"), start=True, stop=False),
                             reads=rd + [B_QTc], writes=BKr(r), sig=False)
                        P.op("pe", lambda e: e.matmul(bkf(r), l1, QT_c[:, 1, hs, :].rearrange("p h t -> p (h t)"), start=False, stop=False),
                             reads=rd + [B_QTc], writes=BKr(r), sig=False)
                        P.op("pe", lambda e: e.matmul(bkf(r), l2, QT_r[:, hs, :].rearrange("p h t -> p (h t)"), start=False, stop=(not diag)),
                             reads=rd + [B_QTr_lo, B_QTr_hi], writes=BKr(r), sig=(not diag))
                        if diag:
                            P.op("pe", lambda e: e.matmul(bkf(r), ident[:], triT[:], start=False, stop=True),
                                 reads=[B_const], writes=BKr(r), sig=True)

                    def exp_u(u):
                        r = sring[u % 3]
                        pt = u % 3
                        P.op("act", lambda e: e.activation(out=PT[pt][:], in_=bkf(r), func=AF.Exp, scale=scale),
                             reads=BKr(r), writes=[B_PT[pt]])

                    def pv_mm(u):
                        pt = u % 3
                        diag = (u == n_off)
                        last = (u == nun - 1)
                        vsrc = Vown[:] if diag else Vst[:, u, :]
                        rd = [B_Vown] if diag else [B_K[u]]
                        for hh in range(4):
                            bk = ob[hh // 2]
                            col = (hh % 2) * 256
                            P.op("pe", lambda e, hh=hh, bk=bk, col=col: e.matmul(
                                bkf(bk)[:, col:col + 256], PT[pt][:, hh * 128:(hh + 1) * 128], vsrc,
                                start=(u == 0 and hh % 2 == 0), stop=last, skip_group_check=True),
                                reads=rd + [B_PT[pt]], writes=BKr(bk), sig=(last and hh % 2 == 1))
                            P.op("pe", lambda e, hh=hh: e.matmul(
                                bkf(db)[:, hg * 4 + hh:hg * 4 + hh + 1], PT[pt][:, hh * 128:(hh + 1) * 128], ones_c[:],
                                start=(u == 0 and hh == 0), stop=last, skip_group_check=True),
                                reads=[B_PT[pt], B_ones], writes=BKr(db), sig=(hh == 3))

                    s_mm(0)
                    if nun > 1:
                        s_mm(1)
                    while pre_b:
                        pre_b.pop(0)()
                    for u in range(nun):
                        exp_u(u)
                        if u + 2 < nun:
                            s_mm(u + 2)
                        pv_mm(u)
                        if deferred and u >= 1:
                            deferred.pop(0)()
                    while deferred:
                        deferred.pop(0)()

                    P.op("dve", lambda e: e.reciprocal(out=rden[:], in_=bkf(db)[:, hg * 4:hg * 4 + 4]),
                         reads=BKr(db), writes=[B_rden])
                    for half in range(2):
                        bk = ob[half]
                        P.op("dve", lambda e, half=half, bk=bk: e.tensor_tensor(
                            out=olat[:, half * 2:half * 2 + 2, :],
                            in0=bkf(bk).rearrange("p (h c) -> p h c", h=2),
                            in1=rden[:, half * 2:half * 2 + 2].unsqueeze(2).to_broadcast([128, 2, CKV]), op=ALU.mult),
                            reads=BKr(bk) + [B_rden], writes=[B_olat])

                    def stage_b(ob=ob):
                        transposes([olat[:, hh, j * 128:(j + 1) * 128] for hh in range(4) for j in range(2)], ob[0], [B_olat])
                        P.op("act", lambda e: e.copy(out=olatT[:].rearrange("p h j t -> p (h j t)"), in_=bkh(ob[0])),
                             reads=BKr(ob[0]), writes=[B_olatT])

                    def stage_c(ob=ob, hg=hg, s=s):
                        for hh in range(4):
                            h = hg * 4 + hh
                            for j in range(2):
                                P.op("pe", lambda e, hh=hh, h=h, j=j: e.matmul(
                                    bkf(ob[1])[:, hh * 128:(hh + 1) * 128], w_uv_s[:, j, h * 128:(h + 1) * 128], olatT[:, hh, j, :],
                                    start=(j == 0), stop=(j == 1), skip_group_check=True),
                                    reads=[B_olatT, B_wA], writes=BKr(ob[1]), sig=(hh == 3 and j == 1))
                        P.op("dve", lambda e: e.tensor_copy(out=OT[:, s, hg * 4:hg * 4 + 4, :].rearrange("p h t -> p (h t)"), in_=bkf(ob[1])),
                             reads=BKr(ob[1]), writes=[B_OT[s]])

                    if hg == 0 and DEFER_POST:
                        pre_b.append(stage_b)
                        deferred.extend([stage_c])
                    else:
                        if hg == 1 and s + 1 < NS:
                            pre1(s + 1)
                        stage_b()
                        stage_c()

            dbg_dump("V", Vst[:].rearrange("p k c -> p (k c)"))
            dbg_dump("KTc", KT_c[:].rearrange("p j s -> p (j s)"))
            dbg_dump("KTr", KT_r[:])
            dbg_dump("OT", OT[:].rearrange("p s h t -> p (s h t)"))
            dbg_dump("QTc", QT_c[:].rearrange("p j h t -> p (j h t)"))
            dbg_dump("QTr", QT_r[:].rearrange("p h t -> p (h t)"))
            dbg_dump("cq", cq[:])
            dbg_dump("qn", qn[:].rearrange("p h n -> p (h n)"))
            dbg_dump("qr", qr[:].rearrange("p h n -> p (h n)"))
            dbg_dump("olat", olat[:].rearrange("p h n -> p (h n)"))
            dbg_dump("rden", rden[:])
            dbg_dump("hsb", h_sb[:])
            dbg_dump("stat", stat[:])

        nc.all_engine_barrier()
        with ExitStack() as sbk:
            def sbb(name, shape, dt):
                return sb(name, shape, dt, sbk)

            ring = [sbb("ring%d" % i, [128, SLAB], BF16) for i in range(RING)]
            B_ring = [Buf("ring%d" % i) for i in range(RING)]
            lng = sbb("lng", [128, 4, D], F32)
            lnb = sbb("lnb", [128, 4, D], F32)
            ET = sbb("ET", [128, 2, 16 * 128], BF16)
            qT_full = sbb("qT", [128, 16, 256], BF16)
            ETf = qT_full[:].rearrange("p h t -> p (h t)").bitcast(F32)
            sink_e = sbb("sink_e", [128, 16], F32)
            flags = sbb("flags", [128, 4], F32)
            B_cB = Buf("constB")
            for L4 in range(4):
                P.dma("sp", lng[:, L4, :], ln_g[L4].partition_broadcast(128), "c2", writes=[B_cB], group=True)
                P.dma("sp", lnb[:, L4, :], ln_b[L4].partition_broadcast(128), "c2", writes=[B_cB], group=True)
            P.dma("sp", sink_e[:], sinks.partition_broadcast(128), "c2", writes=[B_cB], group=True)
            P.dma("sp", flags[:], swa_flags, "c2", writes=[B_cB], group=True)
            for t2 in range(2):
                P.dma("sp", ETf, biasT[t2], "c2", writes=[B_cB], group=True)
                t_et = P.op("act", lambda e, t2=t2: e.activation(out=ET[:, t2, :], in_=ETf, func=AF.Exp), reads=[B_cB], writes=[B_cB])
            P.op("act", lambda e: e.activation(out=sink_e[:], in_=sink_e[:], func=AF.Exp), reads=[B_cB], writes=[B_cB])

            xs_all = sbb("xs", [128, 2, 2, D], F32)
            B_xs_all = [[Buf("xs00"), Buf("xs01")], [Buf("xs10"), Buf("xs11")]]
            xs = xs_all[:, 0]
            B_xs = B_xs_all[0]
            cur = {"par": 0}
            zb = [sbb("zb%d" % i, [128, D], BF16) for i in range(2)]
            B_zb = [Buf("zb0"), Buf("zb1")]
            gT = sbb("gT", [128, 4, 8], F32)
            bT = sbb("bT", [128, 4, 8], F32)
            P.dma("sp", gT[:], ln_gT.rearrange("p (l c) -> p l c", l=4), "c2", writes=[B_cB], group=True)
            P.dma("sp", bT[:], ln_bT.rearrange("p (l c) -> p l c", l=4), "c2", writes=[B_cB], group=True)
            aT = sbb("aT", [128, 8, 256], BF16)
            B_aTs = [Buf("aT0"), Buf("aT1")]
            hT = [sbb("hT%d" % i, [128, 4, 256], BF16) for i in range(2)]
            B_hT = [Buf("hT0"), Buf("hT1")]
            bst_all = sbb("bst", [128, 2, 2, 2, 6], F32)
            mv_all = sbb("mv", [128, 2, 2, 4], F32)
            B_st_all = [[Buf("bst00"), Buf("bst01")], [Buf("bst10"), Buf("bst11")]]
            bst = bst_all[:, 0]
            mv = mv_all[:, 0]
            B_st = B_st_all[0]
            qT = qT_full[0:64]
            B_qT = Buf("qT")
            B_qT.w = t_et
            kT = sbb("kT", [64, 4, 256], BF16)
            B_kT = Buf("kT")
            Vaug = sbb("Vaug", [128, 2, 4, 65], BF16)
            B_Va = Buf("Vaug")
            kT_p = sbb("kT_p", [64, 4, 128], BF16)
            Va_p = sbb("Va_p", [128, 4, 65], BF16)
            B_prev = Buf("prev")
            kT_e = sbb("kT_e", [64, 4, 128], BF16)
            Va_e = sbb("Va_e", [128, 4, 65], BF16)
            B_ext = Buf("ext")
            eS = [sbb("eS%d" % i, [128, 512], F32) for i in range(2)]
            B_eS = [Buf("eS0"), Buf("eS1")]
            PB = [sbb("PB%d" % i, [128, 512], BF16) for i in range(4)]
            B_PB = [Buf("PB%d" % i) for i in range(4)]
            osb = sbb("osb", [128, 16, 64], BF16)
            B_osb = Buf("osb")
            oT2 = sbb("oT2", [128, 8, 128], BF16)
            B_oT2 = Buf("oT2")
            rd16 = sbb("rd16", [128, 16], F32)
            B_rd16 = Buf("rd16")

            P.op("dve", lambda e: e.memset(Vaug[:, :, :, 64:65], 1.0), writes=[B_Va])
            P.op("dve", lambda e: e.memset(Va_p[:, :, 64:65], 1.0), writes=[B_prev])
            P.op("dve", lambda e: e.memset(Va_e[:, :, 64:65], 1.0), writes=[B_ext])

            state = {"n": 0}

            def mlp_seq(b0, insert=()):
                seq = [b0]
                for j in range(8):
                    if j + 1 < 8:
                        seq.append(b0 + 2 * (j + 1))
                    if j == 7:
                        seq += list(insert)
                    seq.append(b0 + 2 * j + 1)
                return seq

            slab_plan = []
            slab_plan += [0, 1] + mlp_seq(2) + [18]
            slab_plan += [0, 1]
            for bi_ in range(CH):
                slab_plan += mlp_seq(2) + [18, 19, 20, 21, 22] + mlp_seq(23, insert=([0, 1] if bi_ + 1 < CH else ()))
            issued = {"n": 0}
            slab_ticket = {}

            def issue_slabs(upto):
                while issued["n"] < min(upto, len(slab_plan)):
                    g = issued["n"]
                    r = g % RING
                    P.dma("sp", ring[r][:], wscr[slab_plan[g]], "ring%d" % r, reads=[B_scr], writes=[B_ring[r]])
                    issued["n"] += 1

            def next_slab():
                g = state["n"]
                state["n"] += 1
                issue_slabs(g + 1)
                return g % RING

            def release_and_prefetch():
                issue_slabs(state["n"] + RING - 1)

            def ln_part1a(sl, li):
                P.op("dve", lambda e: e.bn_stats(out=bst[:, sl, 0, :], in_=xs[:, sl, 0:512]), reads=[B_xs[sl]], writes=[B_st[sl]])
                P.op("dve", lambda e: e.bn_stats(out=bst[:, sl, 1, :], in_=xs[:, sl, 512:1024]), reads=[B_xs[sl]], writes=[B_st[sl]])
                P.op("dve", lambda e: e.bn_aggr(out=mv[:, sl, 0:2], in_=bst[:, sl].rearrange("p c (t j) -> p (c t) j", j=3)),
                     writes=[B_st[sl]])
                P.op("act", lambda e: e.activation(out=mv[:, sl, 2:3], in_=mv[:, sl, 1:2], func=AF.Ln, bias=LN_EPS), writes=[B_st[sl]])
                P.op("act", lambda e: e.activation(out=mv[:, sl, 2:3], in_=mv[:, sl, 2:3], func=AF.Exp, scale=-0.5), writes=[B_st[sl]])

            def ln_part1b(sl, li):
                P.op("dve", lambda e: e.tensor_scalar(out=zb[sl][:], in0=xs[:, sl, :], scalar1=mv[:, sl, 0:1], scalar2=mv[:, sl, 2:3],
                                                      op0=ALU.subtract, op1=ALU.mult),
                     reads=[B_st[sl], B_xs[sl]], writes=[B_zb[sl]])

            def ln_part1(sl, li, need_aT=True):
                ln_part1a(sl, li)
                if need_aT:
                    ln_part1b(sl, li)

            def ln_part2(sl, li, need_aT=True, do_a=True, do_b=True):
                if need_aT and do_a:
                    bk = 6 + sl
                    transposes([zb[sl][:, c * 128:(c + 1) * 128] for c in range(8)], bk, [B_zb[sl]])
                    for c in range(8):
                        if sl == 0:
                            P.op("act", lambda e, c=c: e.activation(
                                out=aT[:, c, sl * 128:(sl + 1) * 128], in_=bkh(bk)[:, c * 128:(c + 1) * 128], func=AF.Identity,
                                scale=gT[:, li, c:c + 1], bias=bT[:, li, c:c + 1]),
                                reads=BKr(bk) + [B_cB], writes=[B_aTs[sl]])
                        else:
                            P.op("dve", lambda e, c=c: e.tensor_scalar(
                                out=aT[:, c, sl * 128:(sl + 1) * 128], in0=bkh(bk)[:, c * 128:(c + 1) * 128],
                                scalar1=gT[:, li, c:c + 1], scalar2=bT[:, li, c:c + 1], op0=ALU.mult, op1=ALU.add),
                                reads=BKr(bk) + [B_cB], writes=[B_aTs[sl]])
                if not do_b:
                    return
                x_ap = xs[:, sl, :]
                P.op("dve", lambda e: e.scalar_tensor_tensor(out=x_ap, in0=x_ap, scalar=mv[:, sl, 0:1], in1=lng[:, li, :],
                                                             op0=ALU.subtract, op1=ALU.mult),
                     reads=[B_st[sl], B_cB], writes=[B_xs[sl]])
                P.op("dve", lambda e: e.scalar_tensor_tensor(out=x_ap, in0=x_ap, scalar=mv[:, sl, 2:3], in1=lnb[:, li, :],
                                                             op0=ALU.mult, op1=ALU.add),
                     reads=[B_st[sl], B_cB], writes=[B_xs[sl]])

            def ln_stage(nsl, li, need_aT=True):
                for sl in range(nsl):
                    ln_part1a(sl, li)
                if need_aT:
                    for sl in range(nsl):
                        ln_part1b(sl, li)
                for sl in range(nsl):
                    ln_part2(sl, li, need_aT)

            def mlp(nsl, hook=None):
                T = nsl * 128
                accb = [[0, 1], [2, 3]]

                def up(j):
                    ru = next_slab()
                    wu = ring[ru][:].rearrange("p (c n) -> p c n", n=512)
                    hb = hT[j % 2]
                    for fc in range(4):
                        bk = 4 + fc
                        for c in range(8):
                            P.op("pe", lambda e, fc=fc, c=c, bk=bk: e.matmul(
                                bkf(bk)[:, 0:T], wu[:, c, fc * 128:(fc + 1) * 128], aT[:, c, 0:T],
                                start=(c == 0), stop=(c == 7)),
                                reads=[B_ring[ru]] + B_aTs, writes=BKr(bk), sig=(c == 7))
                        src = bkf(bk)[:, 0:T]
                        es = fc % 2
                        P.op("act", lambda e, src=src, es=es: e.activation(out=eS[es][:, 0:T], in_=src, func=AF.Relu),
                             reads=BKr(bk), writes=[B_eS[es]])
                        P.op("dve", lambda e, fc=fc, es=es: e.tensor_tensor(out=hb[:, fc, 0:T], in0=eS[es][:, 0:T], in1=eS[es][:, 0:T], op=ALU.mult),
                             reads=[B_eS[es]], writes=[B_hT[j % 2]])

                def down(j):
                    rd = next_slab()
                    wd = ring[rd][:].rearrange("p (c n) -> p c n", n=1024)
                    hb = hT[j % 2]
                    for sl in range(nsl):
                        for dh in range(2):
                            bk = accb[sl][dh]
                            for fc in range(4):
                                P.op("pe", lambda e, sl=sl, dh=dh, fc=fc, bk=bk: e.matmul(
                                    bkf(bk), hb[:, fc, sl * 128:(sl + 1) * 128], wd[:, fc, dh * 512:(dh + 1) * 512],
                                    start=(j == 0 and fc == 0), stop=(j == 7 and fc == 3)),
                                    reads=[B_hT[j % 2], B_ring[rd]], writes=BKr(bk), sig=(fc == 3))
                    release_and_prefetch()

                up(0)
                for j in range(8):
                    if j + 1 < 8:
                        up(j + 1)
                    if j == 7 and hook is not None:
                        hook()
                    down(j)
                for sl in range(nsl):
                    for dh in range(2):
                        bk = accb[sl][dh]
                        xa = xs[:, sl, dh * 512:(dh + 1) * 512]
                        P.op("dve", lambda e, xa=xa, bk=bk: e.scalar_tensor_tensor(out=xa, in0=xa, scalar=ALPHA, in1=bkf(bk),
                                                                                   op0=ALU.mult, op1=ALU.add),
                             reads=BKr(bk), writes=[B_xs[sl]])

            def swa_kv(nsl, sl_list_global):
                T = nsl * 128
                r = next_slab()
                wk = ring[r][:].rearrange("p (c n) -> p c n", n=512)
                for kh in range(4):
                    for c in range(8):
                        P.op("pe", lambda e, kh=kh, c=c: e.matmul(bkf(4 + kh)[0:64, 0:T],
                                                                   wk[:, c, kh * 64:(kh + 1) * 64], aT[:, c, 0:T],
                                                                   start=(c == 0), stop=(c == 7), skip_group_check=True),
                             reads=[B_ring[r]] + B_aTs, writes=BKr(4 + kh), sig=(c == 7))
                for kh in range(4):
                    P.op("dve", lambda e, kh=kh: e.tensor_copy(out=kT[:, kh, 0:T], in_=bkf(4 + kh)[0:64, 0:T]),
                         reads=BKr(4 + kh), writes=[B_kT])
                for sl in range(nsl):
                    for c in range(8):
                        P.op("pe", lambda e, sl=sl, c=c: e.matmul(bkf(3)[:, sl * 256:(sl + 1) * 256], aT[:, c, sl * 128:(sl + 1) * 128],
                                                                   wk[:, c, 256:512], start=(c == 0), stop=(c == 7), skip_group_check=True),
                             reads=[B_ring[r]] + B_aTs, writes=BKr(3), sig=(c == 7))
                    P.op("act", lambda e, sl=sl: e.copy(out=Vaug[:, sl, :, 0:64],
                                                        in_=bkf(3)[:, sl * 256:(sl + 1) * 256].rearrange("p (k d) -> p k d", k=4)),
                         reads=BKr(3), writes=[B_Va])

            def swa_q(nsl):
                T = nsl * 128
                for half in range(2):
                    r = next_slab()
                    wq = ring[r][:].rearrange("p (c n) -> p c n", n=1024)
                    if half == 0:
                        r0, wq0 = r, wq
                    else:
                        r1, wq1 = r, wq
                for h in range(16):
                    bk = 4 + h % 4
                    colo = 0
                    for c in range(8):
                        wsel = wq0 if c < 4 else wq1
                        rsel = r0 if c < 4 else r1
                        P.op("pe", lambda e, h=h, c=c, wsel=wsel, bk=bk, colo=colo: e.matmul(
                            bkf(bk)[0:64, colo:colo + T], wsel[:, c % 4, h * 64:(h + 1) * 64], aT[:, c, 0:T],
                            start=(c == 0), stop=(c == 7), skip_group_check=True),
                            reads=[B_ring[rsel]] + B_aTs, writes=BKr(bk, colo, colo + T), sig=(c == 7))
                    P.op("act" if h % 2 == 0 else "dve",
                         (lambda e, h=h, bk=bk, colo=colo: e.copy(out=qT[:, h, 0:T], in_=bkf(bk)[0:64, colo:colo + T])) if h % 2 == 0 else
                         (lambda e, h=h, bk=bk, colo=colo: e.tensor_copy(out=qT[:, h, 0:T], in_=bkf(bk)[0:64, colo:colo + T])),
                         reads=BKr(bk, colo, colo + T), writes=[B_qT])
                release_and_prefetch()

            def swa_attn(sl, keys):
                def o_ap(h):
                    return bkf(h // 7)[:, (h % 7) * 65:(h % 7) * 65 + 65]
                first_in_bank = {0: True, 1: True, 2: True}
                nk = len(keys)
                tiles = [(ki, kh) for ki in range(nk) for kh in range(4)]

                def score(t):
                    ki, kh = tiles[t]
                    k_ap, v_ap, et_idx, flag_ap, kbufs = keys[ki]
                    sbk_ = (6, 7, 3)[t % 3]
                    P.op("pe", lambda e: e.matmul(bkf(sbk_), k_ap[:, kh, :], qT[:, kh * 4:kh * 4 + 4, sl * 128:(sl + 1) * 128],
                                                  start=True, stop=True),
                         reads=list(kbufs) + [B_qT], writes=BKr(sbk_))

                def softmax_pv(t):
                    ki, kh = tiles[t]
                    k_ap, v_ap, et_idx, flag_ap, kbufs = keys[ki]
                    sbk_ = (6, 7, 3)[t % 3]
                    es = t % 2
                    pb = t % 4
                    P.op("act", lambda e: e.activation(out=eS[es][:], in_=bkf(sbk_), func=AF.Exp, scale=0.125),
                         reads=BKr(sbk_), writes=[B_eS[es]])
                    et_ap = ET[:, et_idx, kh * 512:(kh + 1) * 512]
                    if flag_ap is None:
                        P.op("dve", lambda e: e.tensor_tensor(out=PB[pb][:], in0=eS[es][:], in1=et_ap, op=ALU.mult),
                             reads=[B_eS[es], B_cB], writes=[B_PB[pb]])
                    else:
                        P.op("dve", lambda e: e.scalar_tensor_tensor(out=PB[pb][:], in0=eS[es][:], scalar=flag_ap, in1=et_ap,
                                                                     op0=ALU.mult, op1=ALU.mult),
                             reads=[B_eS[es], B_cB], writes=[B_PB[pb]])
                    for g in range(4):
                        h = kh * 4 + g
                        bko = h // 7
                        st_ = first_in_bank[bko] and ki == 0
                        if st_:
                            first_in_bank[bko] = False
                        P.op("pe", lambda e, h=h, g=g, st_=st_: e.matmul(
                            o_ap(h), PB[pb][:, g * 128:(g + 1) * 128], v_ap[:, kh, :],
                            start=st_, stop=(ki == nk - 1), skip_group_check=True),
                            reads=list(kbufs) + [B_PB[pb]], writes=BKr(bko), sig=(g == 3))

                nt = len(tiles)
                for t in range(min(3, nt)):
                    score(t)
                for t in range(nt):
                    softmax_pv(t)
                    if t + 3 < nt:
                        score(t + 3)
                for b_ in range(3):
                    nh = 7 if b_ < 2 else 2
                    P.op("dve", lambda e, b_=b_, nh=nh: e.tensor_tensor(
                        out=rd16[:, b_ * 7:b_ * 7 + nh],
                        in0=bkf(b_)[:, 0:nh * 65].rearrange("p (h c) -> p h c", c=65)[:, :, 64],
                        in1=sink_e[:, b_ * 7:b_ * 7 + nh], op=ALU.add),
                        reads=BKr(b_) + [B_cB], writes=[B_rd16])
                P.op("dve", lambda e: e.reciprocal(out=rd16[:], in_=rd16[:]), writes=[B_rd16])
                for b_ in range(3):
                    nh = 7 if b_ < 2 else 2
                    P.op("dve", lambda e, b_=b_, nh=nh: e.tensor_tensor(
                        out=osb[:, b_ * 7:b_ * 7 + nh, :],
                        in0=bkf(b_)[:, 0:nh * 65].rearrange("p (h c) -> p h c", c=65)[:, :, 0:64],
                        in1=rd16[:, b_ * 7:b_ * 7 + nh].unsqueeze(2).to_broadcast([128, nh, 64]), op=ALU.mult),
                        reads=BKr(b_) + [B_rd16], writes=[B_osb])

            def out_proj(lhs_of_chunk, lhs_bufs, r0, r1, banks2):
                for half, r in ((0, r0), (1, r1)):
                    w = ring[r][:].rearrange("p (c n) -> p c n", n=1024)
                    for dh in range(2):
                        for c4 in range(4):
                            c = half * 4 + c4
                            P.op("pe", lambda e, dh=dh, c4=c4, c=c, w=w: e.matmul(
                                bkf(banks2[dh]), lhs_of_chunk(c), w[:, c4, dh * 512:(dh + 1) * 512],
                                start=(c == 0), stop=(c == 7), skip_group_check=True),
                                reads=list(lhs_bufs) + [B_ring[r]], writes=BKr(banks2[dh]), sig=(c4 == 3))

            def residual_from_banks(sl, banks2):
                for dh in range(2):
                    xa = xs[:, sl, dh * 512:(dh + 1) * 512]
                    P.op("dve", lambda e, xa=xa, dh=dh: e.scalar_tensor_tensor(out=xa, in0=xa, scalar=ALPHA, in1=bkf(banks2[dh]),
                                                                               op0=ALU.mult, op1=ALU.add),
                         reads=BKr(banks2[dh]), writes=[B_xs[sl]])

            def set_par(par):
                nonlocal xs, B_xs, mv, bst, B_st
                cur["par"] = par
                xs = xs_all[:, par]
                B_xs = B_xs_all[par]
                mv = mv_all[:, par]
                bst = bst_all[:, par]
                B_st = B_st_all[par]

            def layer0_front(slots):
                nsl = len(slots)
                for sl, s in enumerate(slots):
                    P.dma("sp", xs[:, sl, :], xq[s * 128:(s + 1) * 128, :], "xs%d_%d" % (cur["par"], sl), writes=[B_xs[sl]])
                r0 = next_slab()
                r1 = next_slab()
                pb = [[4, 5], [6, 7]]
                for sl, s in enumerate(slots):
                    out_proj(lambda c, s=s: OT[:, s, c, :], [B_OT[s]], r0, r1, pb[sl])
                release_and_prefetch()
                for sl in range(nsl):
                    residual_from_banks(sl, pb[sl])
                for sl in range(nsl):
                    ln_part1a(sl, 0)
                for sl in range(nsl):
                    ln_part1b(sl, 0)

            def layer0_back(nsl, a_done=False):
                for sl in range(nsl):
                    ln_part2(sl, 0, do_a=not a_done)
                mlp(nsl)
                ln_stage(nsl, 1)

            def save_kv(dst_k, dst_v, sl, bdst):
                P.op("pool", lambda e: e.tensor_copy(out=dst_k[:], in_=kT[:, :, sl * 128:(sl + 1) * 128]), reads=[B_kT], writes=[bdst])
                P.op("pool", lambda e: e.tensor_copy(out=dst_v[:], in_=Vaug[:, sl, :, :]), reads=[B_Va], writes=[bdst])

            set_par(0)
            layer0_front([2 * CH])
            layer0_back(1)
            swa_kv(1, None)
            save_kv(kT_e, Va_e, 0, B_ext)
            release_and_prefetch()

            set_par(1)
            layer0_front([0, 1])
            for bi in range(CH):
                slots = [2 * bi, 2 * bi + 1]
                par_now = (bi + 1) % 2
                set_par(par_now)
                layer0_back(2, a_done=(bi > 0))
                swa_kv(2, slots)
                swa_q(2)
                for sl, s in enumerate(slots):
                    keys = []
                    if s == 0:
                        keys.append((kT_e[:], Va_e[:], 0, flags[:, 0:1], [B_ext]))
                    elif s == CH:
                        keys.append((kT_e[:], Va_e[:], 0, flags[:, 1:2], [B_ext]))
                        keys.append((kT_p[:], Va_p[:], 0, flags[:, 2:3], [B_prev]))
                    elif sl == 0:
                        keys.append((kT_p[:], Va_p[:], 0, None, [B_prev]))
                    else:
                        keys.append((kT[:, :, 0:128], Vaug[:, 0, :, :], 0, None, [B_kT, B_Va]))
                    keys.append((kT[:, :, sl * 128:(sl + 1) * 128], Vaug[:, sl, :, :], 1, None, [B_kT, B_Va]))
                    swa_attn(sl, keys)
                    transposes([osb[:, 2 * c:2 * c + 2, :].rearrange("p h d -> p (h d)") for c in range(8)], 3, [B_osb])
                    P.op("act", lambda e: e.copy(out=oT2[:].rearrange("p c t -> p (c t)"), in_=bkh(3)), reads=BKr(3), writes=[B_oT2])
                    if sl == 0:
                        r0 = next_slab()
                        r1 = next_slab()
                    out_proj(lambda c: oT2[:, c, :], [B_oT2], r0, r1, [4, 5])
                    residual_from_banks(sl, [4, 5])
                    if SPLIT_LN:
                        ln_part1(sl, 2)
                save_kv(kT_p, Va_p, 1, B_prev)
                release_and_prefetch()
                if SPLIT_LN:
                    for sl in range(2):
                        ln_part2(sl, 2)
                else:
                    ln_stage(2, 2)

                def front_next(bi=bi, par_now=par_now):
                    if bi + 1 < CH:
                        set_par((bi + 2) % 2)
                        layer0_front([2 * bi + 2, 2 * bi + 3])
                        set_par(par_now)
                mlp(2, hook=front_next)
                if bi + 1 < CH:
                    set_par((bi + 2) % 2)
                    for sl in range(2):
                        ln_part2(sl, 0, do_b=False)
                    set_par(par_now)
                ln_stage(2, 3, need_aT=False)
                for sl, s in enumerate(slots):
                    P.dma("sp", out[s * 128:(s + 1) * 128, :], xs[:, sl, :], "out%d_%d" % (cur["par"], sl), reads=[B_xs[sl]])
            dbg_dump("xs", xs.rearrange("p s d -> p (s d)"))
            dbg_dump("aT", aT[:].rearrange("p c t -> p (c t)"))
            dbg_dump("osb", osb[:].rearrange("p h d -> p (h d)"))
            dbg_dump("qT", qT[:].rearrange("p h t -> p (h t)"))
            dbg_dump("kT", kT[:].rearrange("p h t -> p (h t)"))
            dbg_dump("ET", ET[:].rearrange("p a b -> p (a b)"))
            for nm in ("out0_0", "out0_1", "out1_0", "out1_1"):
                if nm in P.sem:
                    nc.sync.wait_ge(P.sem[nm], P.cnt[nm])
            P.check()
    return nc


def _t5_bucket_np(dist):
    n = np.maximum(dist, 0)
    max_exact = 16
    nf = np.maximum(n, 1).astype(np.float32)
    large = max_exact + (np.log(nf / np.float32(max_exact)) / np.float32(math.log(128 / max_exact))
                         * np.float32(32 - max_exact)).astype(np.int32)
    large = np.minimum(large, 31)
    return np.where(n < max_exact, n, large)


def _rope_tables(pos):
    half = 32
    inv = (np.float32(10000.0) ** (-(np.arange(half, dtype=np.float32) / np.float32(half)))).astype(np.float32)
    ang = (pos.astype(np.float32)[:, None] * inv[None, :]).astype(np.float32)
    return np.concatenate([np.cos(ang.astype(np.float64)), np.sin(ang.astype(np.float64))], axis=1).astype(np.float32)


_NC_CACHE = {}


def _run(inputs, NB, n_batch):
    CH = NB // 4
    NS = 2 * CH + 1
    S = NB * 128
    bf = ml_dtypes.bfloat16
    f32 = np.float32
    x = np.ascontiguousarray(inputs["x"], dtype=f32)
    assert x.shape == (n_batch, S, D)
    if NB not in _NC_CACHE:
        _NC_CACHE[NB] = build_nc(NB)
    nc = _NC_CACHE[NB]

    ident = np.eye(128, dtype=f32).astype(bf)
    kk = np.arange(128)[:, None]
    qq = np.arange(128)[None, :]
    tri = np.where(kk > qq, NEG, 0.0).astype(f32)
    triT = np.tile(tri, (1, 4)).astype(bf)
    onehot = np.zeros((64, S), dtype=f32)
    for kb in range(NB):
        onehot[kb, kb * 128:(kb + 1) * 128] = 1.0
    onehot = onehot.astype(bf)
    css = _rope_tables(np.arange(S))
    rel_bias = np.asarray(inputs["rel_bias"], dtype=f32)
    ii = np.arange(128)[:, None]
    jj = np.arange(256)[None, :]
    dist = ii + 128 - jj
    bias = rel_bias[_t5_bucket_np(dist)]
    valid = (dist >= 0) & (dist < 128)
    bias = np.where(valid[:, :, None], bias, f32(NEG)).astype(f32)
    biasT = np.ascontiguousarray(bias.transpose(1, 2, 0))
    biasT = biasT.reshape(2, 128, 16 * 128)

    w_uk = np.asarray(inputs["mla_w_uk"], dtype=f32)[0]
    w_ukT = np.ascontiguousarray(w_uk.transpose(2, 1, 0)).reshape(128, H * CKV)
    ln_g = np.stack([inputs["ln_mix_g"][0], inputs["ln_mlp_g"][0], inputs["ln_mix_g"][1], inputs["ln_mlp_g"][1]]).astype(f32)
    ln_b = np.stack([inputs["ln_mix_b"][0], inputs["ln_mlp_b"][0], inputs["ln_mix_b"][1], inputs["ln_mlp_b"][1]]).astype(f32)
    common = {
        "onehot": onehot, "ident": ident, "triT": triT, "css": css,
        "mla_w_in": np.ascontiguousarray(inputs["mla_w_in"][0], dtype=f32),
        "mla_g_q": np.ascontiguousarray(inputs["mla_g_q"][0], dtype=f32),
        "mla_g_kv": np.ascontiguousarray(inputs["mla_g_kv"][0], dtype=f32),
        "mla_w_uq": np.ascontiguousarray(np.concatenate(
            [np.asarray(inputs["mla_w_uq"][0], dtype=f32)[:, :, 0:NOPE].reshape(QR, H * NOPE),
             np.asarray(inputs["mla_w_uq"][0], dtype=f32)[:, :, NOPE:NOPE + RD].reshape(QR, H * RD)], axis=1)),
        "w_ukT": w_ukT,
        "mla_w_uv": np.ascontiguousarray(inputs["mla_w_uv"][0], dtype=f32).reshape(CKV, H * 128),
        "mla_w_o": np.ascontiguousarray(inputs["mla_w_o"][0], dtype=f32),
        "kv_w_shared": np.ascontiguousarray(inputs["kv_w_shared"], dtype=f32),
        "swa_w_q": np.ascontiguousarray(inputs["swa_w_q"][0], dtype=f32),
        "swa_w_o": np.ascontiguousarray(inputs["swa_w_o"][0], dtype=f32),
        "swa_sinks": np.ascontiguousarray(inputs["swa_sinks"][0], dtype=f32),
        "biasT": biasT,
        "mlp_w_up": np.ascontiguousarray(inputs["mlp_w_up"], dtype=f32),
        "mlp_w_down": np.ascontiguousarray(inputs["mlp_w_down"], dtype=f32),
        "ln_g": ln_g, "ln_b": ln_b,
        "ln_gT": np.ascontiguousarray(ln_g.reshape(4, 8, 128).transpose(2, 0, 1)).reshape(128, 32),
        "ln_bT": np.ascontiguousarray(ln_b.reshape(4, 8, 128).transpose(2, 0, 1)).reshape(128, 32),
    }
    xT_all = np.ascontiguousarray(x.reshape(n_batch, NB, 128, 8, 128).transpose(0, 1, 4, 3, 2)).reshape(n_batch, S, D)
    in_maps = []
    for core in range(2 * n_batch):
        b, par = core // 2, core % 2
        blocks = _slot_blocks(par, NB)
        rows = np.concatenate([np.arange(bl * 128, (bl + 1) * 128) for bl in blocks])
        maskrows = np.zeros((NS, 64, 128), dtype=f32)
        for s, bl in enumerate(blocks):
            maskrows[s, bl:, :] = NEG
        flags = np.zeros((128, 4), dtype=f32)
        if par == 0:
            flags[:, 0] = 0.0; flags[:, 1] = 1.0; flags[:, 2] = 0.0
        else:
            flags[:, 0] = 1.0; flags[:, 1] = 0.0; flags[:, 2] = 1.0
        m = dict(common)
        m["xq"] = np.ascontiguousarray(x[b][rows])
        m["xseqT"] = xT_all[b]
        m["xqT"] = np.ascontiguousarray(xT_all[b][rows])
        m["csq"] = np.ascontiguousarray(css[rows])
        m["maskrows"] = maskrows.astype(bf)
        m["swa_flags"] = flags
        in_maps.append(m)
    res = run_bass_kernel_spmd(nc, in_maps, core_ids=list(range(2 * n_batch)))
    if DEBUG:
        global _LAST_RES
        _LAST_RES = res.results
    outp = np.zeros((n_batch, S, D), dtype=f32)
    for core in range(2 * n_batch):
        b, par = core // 2, core % 2
        blocks = _slot_blocks(par, NB)[:2 * CH]
        o = res.results[core]["out"]
        for s, bl in enumerate(blocks):
            outp[b, bl * 128:(bl + 1) * 128] = o[s * 128:(s + 1) * 128]
    return outp


def kernel(**inputs):
    return _run(inputs, 64, 4)
```

```python
import math
import numpy as np
import ml_dtypes
import concourse.bass as bass
import concourse.mybir as mybir
from concourse.bass_utils import run_bass_kernel_spmd

F32 = mybir.dt.float32
BF16 = mybir.dt.bfloat16
AF = mybir.ActivationFunctionType
ALU = mybir.AluOpType
AX = mybir.AxisListType

D = 1024
QR = 384
CKV = 256
RD = 64
H = 8
NOPE = 128
FF = 4096
NEG = -30000.0
ALPHA = 4 ** 0.25
LN_EPS = 1e-5
RMS_EPS = 1e-6
DEBUG = False
DEFER_POST = True
SPLIT_LN = True
NSLAB = 39
SLAB = 4096
RING = 5


class Buf:
    __slots__ = ("w", "r", "name", "excl")

    def __init__(self, name, excl=False):
        self.name = name
        self.w = None
        self.r = {}
        self.excl = excl


class Prog:
    def __init__(self, nc, stack):
        self.nc = nc
        self.eng = {"pe": nc.tensor, "act": nc.scalar, "dve": nc.vector, "pool": nc.gpsimd, "sp": nc.sync}
        self.sem = {}
        self.cnt = {}
        self.pending = {}
        self.waited = {}
        self.stack = stack
        self.log = {e: [] for e in self.eng}
        for e in self.eng:
            self.sem[e] = stack.enter_context(nc.semaphore("pg_" + e))
            self.cnt[e] = 0
            self.pending[e] = False

    def dsem(self, name):
        if name not in self.sem:
            self.sem[name] = self.stack.enter_context(self.nc.semaphore("d_" + name))
            self.cnt[name] = 0
        return name

    def _wait(self, e, t):
        if t is None:
            return
        p, v = t
        if v is None:
            v = self.cnt[p]
        if p == e and e == "pe":
            return
        if self.waited.get((e, p), 0) >= v:
            return
        if p in self.eng:
            assert v <= self.cnt[p], ("deadlock: waiting on unsignalled", e, p, v, self.cnt[p])
        self.eng[e].wait_ge(self.sem[p], v)
        self.log[e].append(("wait", p, v))
        self.waited[(e, p)] = v

    def _deps(self, e, reads, writes):
        for b in reads:
            self._wait(e, b.w)
        for b in writes:
            self._wait(e, b.w)
            for t in b.r.items():
                self._wait(e, t)

    def _mark(self, t, reads, writes):
        for b in reads:
            if t[1] is None:
                b.r[t[0]] = None
            elif b.r.get(t[0], 0) is not None and b.r.get(t[0], 0) < t[1]:
                b.r[t[0]] = t[1]
        for b in writes:
            b.w = t
            b.r = {}

    def op(self, e, fn, reads=(), writes=(), sig=True):
        ex = [b for b in reads if b.excl]
        if ex:
            reads = [b for b in reads if not b.excl]
            writes = list(writes) + ex
        self._deps(e, reads, writes)
        ins = fn(self.eng[e])
        if sig:
            self.cnt[e] += 1
            ins.then_inc(self.sem[e], 1)
            self.log[e].append(("inc", e, 1))
            self.pending[e] = False
            t = (e, self.cnt[e])
        else:
            assert e == "pe"
            self.pending[e] = True
            t = (e, self.cnt[e] + 1)
        self._mark(t, reads, writes)
        return t

    def check(self):
        val = {k: 0 for k in self.sem}
        pos = {e: 0 for e in self.eng}
        progress = True
        while progress:
            progress = False
            for e in self.eng:
                lg = self.log[e]
                while pos[e] < len(lg):
                    kind, p, v = lg[pos[e]]
                    if kind == "wait":
                        if val[p] < v:
                            break
                    else:
                        val[p] += v
                    pos[e] += 1
                    progress = True
        stuck = {e: (pos[e], len(self.log[e]), self.log[e][pos[e]], val[self.log[e][pos[e]][1]])
                 for e in self.eng if pos[e] < len(self.log[e])}
        assert not stuck, ("DEADLOCK", stuck)

    def dma(self, q, out, in_, semname, reads=(), writes=(), group=False):
        self.dsem(semname)
        self._deps(q, reads, writes)
        self.cnt[semname] += 16
        self.eng[q].dma_start(out=out, in_=in_).then_inc(self.sem[semname], 16)
        self.log[q].append(("inc", semname, 16))
        t = (semname, None if group else self.cnt[semname])
        self._mark(t, reads, writes)
        return t


def _slot_blocks(core_parity, NB):
    CH = NB // 4
    if core_parity == 0:
        return list(range(0, CH)) + list(range(3 * CH, 4 * CH)) + [3 * CH - 1]
    return list(range(CH, 2 * CH)) + list(range(2 * CH, 3 * CH)) + [CH - 1]


def _n_off(s, NB):
    CH = NB // 4
    if s < CH:
        return CH + s
    if s < 2 * CH:
        return 3 * CH + (s - CH)
    return 3 * CH - 1


def build_nc(NB):
    CH = NB // 4
    NS = 2 * CH + 1
    S = NB * 128
    nc = bass.Bass("TRN2", target_bir_lowering=False)

    def din(name, shape, dt=F32):
        return nc.dram_tensor(name, list(shape), dt, kind="ExternalInput").ap()

    xq = din("xq", [NS * 128, D])
    xseqT = din("xseqT", [S, D])
    xqT = din("xqT", [NS * 128, D])
    csq = din("csq", [NS * 128, 64])
    css = din("css", [S, 64])
    maskrows = din("maskrows", [NS, 64, 128], BF16)
    onehot = din("onehot", [64, S], BF16)
    ident_d = din("ident", [128, 128], BF16)
    triT_d = din("triT", [128, 512], BF16)
    w_in = din("mla_w_in", [D, QR + CKV + RD])
    g_q = din("mla_g_q", [QR])
    g_kv = din("mla_g_kv", [CKV])
    w_uq = din("mla_w_uq", [QR, H * 192])
    w_ukT = din("w_ukT", [128, H * CKV])
    w_uv = din("mla_w_uv", [CKV, H * 128])
    w_o = din("mla_w_o", [D, D])
    kv_w = din("kv_w_shared", [D, 512])
    swa_wq = din("swa_w_q", [D, D])
    swa_wo = din("swa_w_o", [D, D])
    sinks = din("swa_sinks", [16])
    biasT = din("biasT", [2, 128, 16 * 128])
    swa_flags = din("swa_flags", [128, 4])
    w_up = din("mlp_w_up", [2, D, FF])
    w_down = din("mlp_w_down", [2, FF, D])
    ln_g = din("ln_g", [4, D])
    ln_b = din("ln_b", [4, D])
    ln_gT = din("ln_gT", [128, 32])
    ln_bT = din("ln_bT", [128, 32])
    out = nc.dram_tensor("out", [2 * CH * 128, D], F32, kind="ExternalOutput").ap()
    wscr = nc.dram_tensor("wscr", [NSLAB, 128, SLAB], BF16).ap()
    dbg = {}

    def dbg_dump(name, ap2d):
        if not DEBUG:
            return
        shp = list(ap2d.shape)
        d = nc.dram_tensor("dbg_" + name, shp, ap2d.dtype, kind="ExternalOutput").ap()
        nc.all_engine_barrier()
        P.dma("sp", d, ap2d, "out0_0")
        dbg[name] = d

    from contextlib import ExitStack
    with ExitStack() as st:
        P = Prog(nc, st)

        def sb(name, shape, dt, stack=st):
            return stack.enter_context(nc.sbuf_tensor("s_" + name, list(shape), dt))

        banks = [st.enter_context(nc.psum_tensor("bank%d" % i, [128, 512], F32)) for i in range(8)]
        BQ = [[Buf("bank%d" % i, excl=True)] * 4 for i in range(8)]

        class _BK:
            def __getitem__(self, i):
                return BQ[i]
        BKL = _BK()

        def BKr(i, lo=0, hi=512):
            return [BQ[i][0]]

        def bkf(i):
            return banks[i][:]

        def bkh(i):
            return banks[i][:].bitcast(BF16)

        ident = sb("ident", [128, 128], BF16)
        triT = sb("triT", [128, 512], BF16)
        ones_c = sb("ones_c", [128, 1], BF16)
        OT = sb("OT", [128, NS, 8, 128], BF16)
        B_const = Buf("const")
        B_OT = [Buf("OT%d" % s) for s in range(NS)]

        P.dma("sp", ident[:], ident_d, "c0", writes=[B_const], group=True)
        P.dma("sp", triT[:], triT_d, "c0", writes=[B_const], group=True)
        B_ones = Buf("ones")
        P.op("dve", lambda e: e.memset(ones_c[:], 1.0), writes=[B_ones])

        def transposes(src_aps, bank, reads, n_part_out=128):
            t = None
            hv = bkh(bank)
            for i, a in enumerate(src_aps):
                m = a.shape[-1]
                last = i == len(src_aps) - 1
                t = P.op("pe", lambda e, a=a, i=i, m=m: e.transpose(hv[0:m, i * 128:(i + 1) * 128], a, ident[:]),
                         reads=list(reads) + [B_const], writes=BKr(bank), sig=last)
            return t

        with ExitStack() as sa:
            def sba(name, shape, dt):
                return sb(name, shape, dt, sa)

            KT_c = sba("KT_c", [128, 2, S], BF16)
            KT_r = sba("KT_r", [128, S], BF16)
            Vst = sba("Vst", [128, NB, CKV], BF16)
            w_inq_s = sba("w_inq", [128, 8, QR], BF16)
            w_inkv_s = sba("w_inkv", [128, 8, CKV + RD], BF16)
            w_uq_s = sba("w_uq", [128, 3, H * 192], BF16)
            w_ukT_s = sba("w_ukT", [128, H, CKV], BF16)
            w_uv_s = sba("w_uv", [128, 2, H * 128], BF16)
            gq_rep = sba("gq_rep", [128, QR], F32)
            gkv_rep = sba("gkv_rep", [128, CKV], F32)
            B_wA = Buf("wA")
            B_K = [Buf("K%d" % k) for k in range(NB)]
            B_KTr_const = Buf("ktr_const")

            P.dma("pool", w_inq_s[:], w_in[:, 0:QR].rearrange("(c p) n -> p c n", p=128), "c1", writes=[B_wA], group=True)
            P.dma("pool", w_inkv_s[:], w_in[:, QR:QR + CKV + RD].rearrange("(c p) n -> p c n", p=128), "c1", writes=[B_wA], group=True)
            P.dma("pool", w_uq_s[:], w_uq.rearrange("(c p) n -> p c n", p=128), "c1", writes=[B_wA], group=True)
            P.dma("pool", w_ukT_s[:], w_ukT.rearrange("p (h c) -> p h c", h=H), "c1", writes=[B_wA], group=True)
            P.dma("pool", w_uv_s[:], w_uv.rearrange("(c p) n -> p c n", p=128), "c1", writes=[B_wA], group=True)
            P.dma("sp", gq_rep[:], g_q.partition_broadcast(128), "c0", writes=[B_wA], group=True)
            P.dma("sp", gkv_rep[:], g_kv.partition_broadcast(128), "c0", writes=[B_wA], group=True)
            P.dma("sp", KT_r[64:128, :], onehot, "c0", writes=[B_KTr_const], group=True)

            xb = [sba("xb%d" % i, [128, D], BF16) for i in range(2)]
            xb += [OT[:, 0].rearrange("p h t -> p (h t)"), OT[:, 1].rearrange("p h t -> p (h t)")]
            B_xb = [Buf("xb%d" % i) for i in range(4)]
            cs = [sba("cs%d" % i, [128, 64], F32) for i in range(4)]
            B_cs = [Buf("cs%d" % i) for i in range(4)]
            xT = sba("xT", [128, 8, 128], BF16)
            B_xT = Buf("xT")
            h_sb = sba("h_sb", [128, QR + CKV + RD], F32)
            B_h = Buf("h_sb")
            stat = sba("stat", [128, 8], F32)
            B_stat = Buf("stat")
            rtmp = sba("rtmp", [128, 2, 8, 64], F32)
            B_rtmpA = Buf("rtmpA")
            B_rtmpB = Buf("rtmpB")
            kr_sb = sba("kr_sb", [128, RD], BF16)
            B_kr = Buf("kr")
            cq = sba("cq", [128, QR], BF16)
            B_cq = Buf("cq")
            cqT = sba("cqT", [128, 3, 128], BF16)
            B_cqT = Buf("cqT")
            Vown = sba("Vown", [128, CKV], BF16)
            B_Vown = Buf("Vown")
            KTown_c = sba("KTown_c", [128, 2, 128], BF16)
            KTown_r = sba("KTown_r", [128, 128], BF16)
            B_KTown = Buf("KTown")
            qn = h_sb[:].bitcast(BF16)[:, 0:H * NOPE].rearrange("p (h n) -> p h n", h=H)
            B_qn = B_h
            qr_f = sba("qr_f", [128, H, RD], F32)
            B_qrf = Buf("qr_f")
            junk = qr_f[:].rearrange("p h r -> p (h r)")[:, 0:QR]
            B_junk = B_qrf
            qr = sba("qr", [128, H, RD], BF16)
            B_qr = Buf("qr")
            qnT = xT
            B_qnT = B_xT
            QT_c = sba("QT_c", [128, 2, H, 128], BF16)
            QT_r = sba("QT_r", [128, H, 128], BF16)
            B_QTc = Buf("QT_c")
            B_QTr_lo = Buf("QT_r_lo")
            B_QTr_hi = Buf("QT_r_hi")
            PT = [sba("PT%d" % i, [128, 512], BF16) for i in range(3)]
            B_PT = [Buf("PT%d" % i) for i in range(3)]
            olat = rtmp[:, 0].bitcast(BF16).rearrange("p h f -> p (h f)").rearrange("p (h c) -> p h c", h=4)
            B_olat = B_rtmpA
            olatT = rtmp[:, 1].bitcast(BF16).rearrange("p h f -> p (h f)").rearrange("p (h j t) -> p h j t", h=4, j=2)
            B_olatT = B_rtmpB
            rden = sba("rden", [128, 4], F32)
            B_rden = Buf("rden")

            P.op("dve", lambda e: e.memset(KTown_r[:], 0.0), writes=[B_KTown])

            def rms_scale(src_ap, n, col, eps):
                P.op("dve", lambda e: e.scalar_tensor_tensor(out=junk[:, 0:n], in0=src_ap, scalar=1.0, in1=src_ap,
                                                             op0=ALU.mult, op1=ALU.mult, accum_out=stat[:, col:col + 1]),
                     reads=[B_h], writes=[B_junk, B_stat])
                P.op("act", lambda e: e.activation(out=stat[:, col:col + 1], in_=stat[:, col:col + 1], func=AF.Ln,
                                                   scale=1.0 / n, bias=eps),
                     writes=[B_stat])
                P.op("act", lambda e: e.activation(out=stat[:, col:col + 1], in_=stat[:, col:col + 1], func=AF.Exp,
                                                   scale=-0.5),
                     writes=[B_stat])

            def rope(src3, dst3, nh, csb, bcs, bsrc, bdst, tmp=None, btmp=None):
                s4 = src3.rearrange("p h (t f) -> p h t f", t=2)
                tsrc = rtmp if tmp is None else tmp
                bA, bB = (B_rtmpA, B_rtmpB) if btmp is None else (btmp, btmp)
                A = tsrc[:, 0, 0:nh, :].rearrange("p h (t f) -> p h t f", t=2)
                Bm = tsrc[:, 1, 0:nh, :].rearrange("p h (t f) -> p h t f", t=2)
                cosb = csb[:, 0:32].unsqueeze(1).unsqueeze(1).to_broadcast([128, nh, 2, 32])
                sinb = csb[:, 32:64].unsqueeze(1).unsqueeze(1).to_broadcast([128, nh, 2, 32])
                P.op("dve", lambda e: e.tensor_tensor(out=A, in0=s4, in1=cosb, op=ALU.mult), reads=[bcs, bsrc], writes=[bA])
                P.op("dve", lambda e: e.tensor_tensor(out=Bm, in0=s4, in1=sinb, op=ALU.mult), reads=[bcs, bsrc], writes=[bB])
                d4 = dst3.rearrange("p h (t f) -> p h t f", t=2)
                P.op("dve", lambda e: e.tensor_tensor(out=d4[:, :, 0, :], in0=A[:, :, 0, :], in1=Bm[:, :, 1, :], op=ALU.subtract),
                     reads=[bA, bB], writes=[bdst])
                return P.op("dve", lambda e: e.tensor_tensor(out=d4[:, :, 1, :], in0=Bm[:, :, 0, :], in1=A[:, :, 1, :], op=ALU.add),
                            reads=[bA, bB], writes=[bdst])

            def load_x_block(src_rows, cs_rows, i):
                P.dma("pool", xb[i][:, :], src_rows, "xb%d" % i, writes=[B_xb[i]])
                P.dma("sp", cs[i][:], cs_rows, "cs%d" % i, writes=[B_cs[i]])

            BX = 0

            def make_xT(i):
                bx = BX
                transposes([xb[i][:, c * 128:(c + 1) * 128] for c in range(8)], bx, [B_xb[i]])
                P.op("act", lambda e: e.copy(out=xT[:].rearrange("p c t -> p (c t)"), in_=bkh(bx)),
                     reads=BKr(bx), writes=[B_xT])

            def kv_from_h(i, v_dst, bv):
                rms_scale(h_sb[:, QR:QR + CKV], CKV, 1, RMS_EPS)
                P.op("dve", lambda e: e.scalar_tensor_tensor(out=v_dst, in0=h_sb[:, QR:QR + CKV], scalar=stat[:, 1:2],
                                                             in1=gkv_rep[:], op0=ALU.mult, op1=ALU.mult),
                     reads=[B_h, B_stat, B_wA], writes=[bv])
                rope(h_sb[:, QR + CKV:QR + CKV + RD].unsqueeze(1), kr_sb[:].unsqueeze(1), 1, cs[i], B_cs[i], B_h, B_kr)

            scr_jobs = []

            def slab_view(idx, inner):
                return wscr[idx].rearrange("p (c n) -> p c n", n=inner)

            k = 0
            for half in range(2):
                scr_jobs.append((slab_view(k, 1024), w_o[half * 512:(half + 1) * 512, :].rearrange("(c p) n -> p c n", p=128)))
                k += 1
            for L in range(2):
                if L == 1:
                    scr_jobs.append((slab_view(k, 512), kv_w.rearrange("(c p) n -> p c n", p=128)))
                    k += 1
                    for wsrc in (swa_wq, swa_wo):
                        for half in range(2):
                            scr_jobs.append((slab_view(k, 1024), wsrc[half * 512:(half + 1) * 512, :].rearrange("(c p) n -> p c n", p=128)))
                            k += 1
                for j in range(8):
                    scr_jobs.append((slab_view(k, 512), w_up[L][:, j * 512:(j + 1) * 512].rearrange("(c p) n -> p c n", p=128)))
                    k += 1
                    scr_jobs.append((slab_view(k, 1024), w_down[L][j * 512:(j + 1) * 512, :].rearrange("(c p) n -> p c n", p=128)))
                    k += 1
            assert k == NSLAB
            B_scr = Buf("scr")

            def issue_scr(n):
                for _ in range(n):
                    if scr_jobs:
                        o, i_ = scr_jobs.pop(0)
                        P.dma("pool", o, i_, "scr", writes=[B_scr])

            NSET = 4
            xTr = [xb[i4][:, :].rearrange("p (c t) -> p c t", c=8) for i4 in range(4)]
            sets = []
            for j in range(NSET):
                if NS >= 2 + 2 * NSET:
                    A_ = OT[:, 2 + 2 * j].rearrange("p h t -> p (h t)").bitcast(F32)
                    B_ = OT[:, 3 + 2 * j].rearrange("p h t -> p (h t)").bitcast(F32)
                else:
                    A_ = sba("p1A%d" % j, [128, 512], F32)[:]
                    B_ = sba("p1B%d" % j, [128, 512], F32)[:]
                sets.append(dict(h=A_[:, 0:320], rA=A_[:, 320:384], rB=A_[:, 384:448], st=A_[:, 448:456],
                                 junk=B_[:, 0:256], kr=B_[:, 256:288].bitcast(BF16),
                                 B_h=Buf("p1h%d" % j), B_st=Buf("p1st%d" % j), B_junk=Buf("p1j%d" % j),
                                 B_r=Buf("p1r%d" % j), B_kr=Buf("p1kr%d" % j), bh=2 * j, bt=2 * j + 1))

            def p1_S0(k):
                i4 = k % 4
                P.dma("pool", xb[i4][:, :], xseqT[k * 128:(k + 1) * 128, :], "xb%d" % i4, writes=[B_xb[i4]])
                P.dma("sp", cs[i4][:], css[k * 128:(k + 1) * 128, :], "cs%d" % i4, writes=[B_cs[i4]])

            def p1_S1(k):
                i4 = k % 4
                W = sets[k % NSET]
                for c in range(8):
                    P.op("pe", lambda e, c=c: e.matmul(bkf(W["bh"])[:, 0:CKV + RD], xTr[i4][:, c, :], w_inkv_s[:, c, :],
                                                       start=(c == 0), stop=(c == 7)),
                         reads=[B_xb[i4], B_wA], writes=BKr(W["bh"]), sig=(c == 7))
                P.op("act", lambda e: e.copy(out=W["h"], in_=bkf(W["bh"])[:, 0:CKV + RD]), reads=BKr(W["bh"]), writes=[W["B_h"]])

            def p1_S2(k):
                W = sets[k % NSET]
                P.op("dve", lambda e: e.scalar_tensor_tensor(out=W["junk"], in0=W["h"][:, 0:CKV], scalar=1.0, in1=W["h"][:, 0:CKV],
                                                             op0=ALU.mult, op1=ALU.mult, accum_out=W["st"][:, 0:1]),
                     reads=[W["B_h"]], writes=[W["B_junk"], W["B_st"]])
                P.op("act", lambda e: e.activation(out=W["st"][:, 0:1], in_=W["st"][:, 0:1], func=AF.Ln, scale=1.0 / CKV, bias=RMS_EPS),
                     writes=[W["B_st"]])
                P.op("act", lambda e: e.activation(out=W["st"][:, 0:1], in_=W["st"][:, 0:1], func=AF.Exp, scale=-0.5), writes=[W["B_st"]])

            def p1_S3(k):
                i4 = k % 4
                W = sets[k % NSET]
                P.op("dve", lambda e: e.scalar_tensor_tensor(out=Vst[:, k, :], in0=W["h"][:, 0:CKV], scalar=W["st"][:, 0:1],
                                                             in1=gkv_rep[:], op0=ALU.mult, op1=ALU.mult),
                     reads=[W["B_h"], W["B_st"], B_wA], writes=[B_K[k]])
                s4 = W["h"][:, CKV:CKV + RD].rearrange("p (t f) -> p t f", t=2)
                A4 = W["rA"].rearrange("p (t f) -> p t f", t=2)
                B4 = W["rB"].rearrange("p (t f) -> p t f", t=2)
                cosb = cs[i4][:, 0:32].unsqueeze(1).to_broadcast([128, 2, 32])
                sinb = cs[i4][:, 32:64].unsqueeze(1).to_broadcast([128, 2, 32])
                P.op("dve", lambda e: e.tensor_tensor(out=A4, in0=s4, in1=cosb, op=ALU.mult), reads=[W["B_h"], B_cs[i4]], writes=[W["B_r"]])
                P.op("dve", lambda e: e.tensor_tensor(out=B4, in0=s4, in1=sinb, op=ALU.mult), reads=[W["B_h"], B_cs[i4]], writes=[W["B_r"]])
                d4 = W["kr"].rearrange("p (t f) -> p t f", t=2)
                P.op("dve", lambda e: e.tensor_tensor(out=d4[:, 0, :], in0=A4[:, 0, :], in1=B4[:, 1, :], op=ALU.subtract),
                     reads=[W["B_r"]], writes=[W["B_kr"]])
                P.op("dve", lambda e: e.tensor_tensor(out=d4[:, 1, :], in0=B4[:, 0, :], in1=A4[:, 1, :], op=ALU.add),
                     reads=[W["B_r"]], writes=[W["B_kr"]])

            def p1_S4(k):
                W = sets[k % NSET]
                transposes([Vst[:, k, 0:128], Vst[:, k, 128:256], W["kr"]], W["bt"], [B_K[k], W["B_kr"]])
                h2 = bkh(W["bt"])
                P.op("act", lambda e: e.copy(out=KT_c[:, :, k * 128:(k + 1) * 128], in_=h2[:, 0:256].rearrange("p (j t) -> p j t", j=2)),
                     reads=BKr(W["bt"]), writes=[B_K[k]])
                P.op("dve", lambda e: e.tensor_copy(out=KT_r[0:64, k * 128:(k + 1) * 128], in_=h2[0:64, 256:384]),
                     reads=BKr(W["bt"]), writes=[B_K[k]])

            for t in range(NB + 3):
                if t < NB:
                    p1_S0(t)
                if 0 <= t - 3 < NB:
                    p1_S4(t - 3)
                if 0 <= t - 2 < NB:
                    p1_S3(t - 2)
                if 0 <= t - 1 < NB:
                    p1_S2(t - 1)
                if t < NB:
                    p1_S1(t)
            nc.all_engine_barrier()
            scr_per_slot = -(-len(scr_jobs) // max(1, min(NS - 1, 20)))

            scale = float((NOPE + RD) ** -0.5)
            rtk = sba("rtk", [128, 2, 1, RD], F32)
            B_rtk = Buf("rtk")

            def pre1(s):
                i = s % 2
                P.dma("pool", xb[i][:, :], xqT[s * 128:(s + 1) * 128, :], "xb%d" % i, writes=[B_xb[i]])
                P.dma("sp", cs[i][:], csq[s * 128:(s + 1) * 128, :], "cs%d" % i, writes=[B_cs[i]])
                xTs = xb[i][:, :].rearrange("p (c t) -> p c t", c=8)
                issue_scr(scr_per_slot)
                P.dma("sp", QT_r[64:128, :, :], maskrows[s].unsqueeze(1).to_broadcast([64, H, 128]), "mrow",
                      writes=[B_QTr_hi])
                for c in range(8):
                    P.op("pe", lambda e, c=c: e.matmul(bkf(0)[:, 0:QR], xTs[:, c, :], w_inq_s[:, c, :],
                                                       start=(c == 0), stop=(c == 7)),
                         reads=[B_xb[i], B_wA], writes=BKr(0), sig=(c == 7))
                for c in range(8):
                    P.op("pe", lambda e, c=c: e.matmul(bkf(1)[:, 0:CKV + RD], xTs[:, c, :], w_inkv_s[:, c, :],
                                                       start=(c == 0), stop=(c == 7)),
                         reads=[B_xb[i], B_wA], writes=BKr(1), sig=(c == 7))
                P.op("act", lambda e: e.copy(out=h_sb[:, 0:QR], in_=bkf(0)[:, 0:QR]), reads=BKr(0), writes=[B_h])
                P.op("act", lambda e: e.copy(out=h_sb[:, QR:QR + CKV + RD], in_=bkf(1)[:, 0:CKV + RD]),
                     reads=BKr(1), writes=[B_h])
                P.op("dve", lambda e: e.scalar_tensor_tensor(out=junk[:, 0:QR], in0=h_sb[:, 0:QR], scalar=1.0 / QR, in1=h_sb[:, 0:QR],
                                                             op0=ALU.mult, op1=ALU.mult, accum_out=stat[:, 0:1]),
                     reads=[B_h], writes=[B_junk, B_stat])
                P.op("dve", lambda e: e.scalar_tensor_tensor(out=junk[:, 0:CKV], in0=h_sb[:, QR:QR + CKV], scalar=1.0 / CKV,
                                                             in1=h_sb[:, QR:QR + CKV], op0=ALU.mult, op1=ALU.mult, accum_out=stat[:, 1:2]),
                     reads=[B_h], writes=[B_junk, B_stat])
                P.op("act", lambda e: e.activation(out=stat[:, 0:2], in_=stat[:, 0:2], func=AF.Ln, bias=RMS_EPS), writes=[B_stat])
                P.op("act", lambda e: e.activation(out=stat[:, 0:2], in_=stat[:, 0:2], func=AF.Exp, scale=-0.5), writes=[B_stat])
                P.op("dve", lambda e: e.scalar_tensor_tensor(out=cq[:], in0=h_sb[:, 0:QR], scalar=stat[:, 0:1],
                                                             in1=gq_rep[:], op0=ALU.mult, op1=ALU.mult),
                     reads=[B_h, B_stat, B_wA], writes=[B_cq])
                P.op("dve", lambda e: e.scalar_tensor_tensor(out=Vown[:], in0=h_sb[:, QR:QR + CKV], scalar=stat[:, 1:2],
                                                             in1=gkv_rep[:], op0=ALU.mult, op1=ALU.mult),
                     reads=[B_h, B_stat, B_wA], writes=[B_Vown])
                rope(h_sb[:, QR + CKV:QR + CKV + RD].unsqueeze(1), kr_sb[:].unsqueeze(1), 1, cs[i], B_cs[i], B_h, B_kr, tmp=rtk, btmp=B_rtk)

            pre1(0)
            for s in range(NS):
                i = s % 2
                n_off = _n_off(s, NB)
                transposes([cq[:, 0:128], cq[:, 128:256], cq[:, 256:384], Vown[:, 0:128], Vown[:, 128:256], kr_sb[:]],
                           5, [B_cq, B_Vown, B_kr])
                h0 = bkh(5)
                P.op("act", lambda e: e.copy(out=cqT[:].rearrange("p c t -> p (c t)"), in_=h0[:, 0:384]),
                     reads=BKr(5), writes=[B_cqT])
                P.op("dve", lambda e: e.tensor_copy(out=KTown_c[:].rearrange("p c t -> p (c t)"), in_=h0[:, 384:640]),
                     reads=BKr(5), writes=[B_KTown])
                P.op("dve", lambda e: e.tensor_copy(out=KTown_r[0:64, :], in_=h0[0:64, 640:768]),
                     reads=BKr(5), writes=[B_KTown])
                for g in range(3):
                    for c in range(3):
                        P.op("pe", lambda e, g=g, c=c: e.matmul(bkf(3 + g), cqT[:, c, :], w_uq_s[:, c, g * 512:(g + 1) * 512],
                                                                start=(c == 0), stop=(c == 2)),
                             reads=[B_cqT, B_wA], writes=BKr(3 + g), sig=(c == 2))
                P.op("act", lambda e: e.copy(out=qn[:, 0:4, :].rearrange("p h n -> p (h n)"), in_=bkf(3)), reads=BKr(3), writes=[B_qn])
                P.op("dve", lambda e: e.tensor_copy(out=qn[:, 4:8, :].rearrange("p h n -> p (h n)"), in_=bkf(4)), reads=BKr(4), writes=[B_qn])
                P.op("act", lambda e: e.copy(out=qr_f[:].rearrange("p h r -> p (h r)"), in_=bkf(5)), reads=BKr(5), writes=[B_qrf])
                rope(qr_f[:], qr[:], H, cs[i], B_cs[i], B_qrf, B_qr)
                transposes([qn[:, h, :] for h in range(H)], 6, [B_qn])
                P.op("act", lambda e: e.copy(out=qnT[:].rearrange("p h t -> p (h t)"), in_=bkh(6)), reads=BKr(6), writes=[B_qnT])
                transposes([qr[:, h, :] for h in range(H)], 7, [B_qr])
                P.op("dve", lambda e: e.tensor_copy(out=QT_r[0:64, :, :].rearrange("p h t -> p (h t)"), in_=bkh(7)[0:64, :]),
                     reads=BKr(7), writes=[B_QTr_lo])
                for j in range(2):
                    for h in range(H):
                        bk = j * 2 + h // 4
                        P.op("pe", lambda e, j=j, h=h, bk=bk: e.matmul(bkf(bk)[:, (h % 4) * 128:(h % 4 + 1) * 128],
                                                                       w_ukT_s[:, h, j * 128:(j + 1) * 128], qnT[:, h, :],
                                                                       start=True, stop=True, skip_group_check=True),
                             reads=[B_qnT, B_wA], writes=BKr(bk), sig=(h % 4 == 3))
                for j in range(2):
                    for hg in range(2):
                        bk = j * 2 + hg
                        eng = "act" if hg == 0 else "dve"
                        dst = QT_c[:, j, hg * 4:(hg + 1) * 4, :].rearrange("p h t -> p (h t)")
                        if eng == "act":
                            P.op("act", lambda e, dst=dst, bk=bk: e.copy(out=dst, in_=bkf(bk)), reads=BKr(bk), writes=[B_QTc])
                        else:
                            P.op("dve", lambda e, dst=dst, bk=bk: e.tensor_copy(out=dst, in_=bkf(bk)), reads=BKr(bk), writes=[B_QTc])

                deferred = []
                pre_b = []
                for hg in range(2):
                    ob = [0, 1] if hg == 0 else [2, 3]
                    db = 4
                    sring = [5, 6, 7]
                    nun = n_off + 1

                    def s_mm(u):
                        r = sring[u % 3]
                        diag = (u == n_off)
                        hs = slice(hg * 4, hg * 4 + 4)
                        if diag:
                            l0, l1, l2 = KTown_c[:, 0, :], KTown_c[:, 1, :], KTown_r[:]
                            rd = [B_KTown]
                        else:
                            l0 = KT_c[:, 0, u * 128:(u + 1) * 128]
                            l1 = KT_c[:, 1, u * 128:(u + 1) * 128]
                            l2 = KT_r[:, u * 128:(u + 1) * 128]
                            rd = [B_K[u], B_KTr_const]
                        P.op("pe", lambda e: e.matmul(bkf(r), l0, QT_c[:, 0, hs, :].rearrange("p h t -> p (h t)"), start=True, stop=False),
                             reads=rd + [B_QTc], writes=BKr(r), sig=False)
                        P.op("pe", lambda e: e.matmul(bkf(r), l1, QT_c[:, 1, hs, :].rearrange("p h t -> p (h t)"), start=False, stop=False),
                             reads=rd + [B_QTc], writes=BKr(r), sig=False)
                        P.op("pe", lambda e: e.matmul(bkf(r), l2, QT_r[:, hs, :].rearrange("p h t -> p (h t)"), start=False, stop=(not diag)),
                             reads=rd + [B_QTr_lo, B_QTr_hi], writes=BKr(r), sig=(not diag))
                        if diag:
                            P.op("pe", lambda e: e.matmul(bkf(r), ident[:], triT[:], start=False, stop=True),
                                 reads=[B_const], writes=BKr(r), sig=True)

                    def exp_u(u):
                        r = sring[u % 3]
                        pt = u % 3
                        P.op("act", lambda e: e.activation(out=PT[pt][:], in_=bkf(r), func=AF.Exp, scale=scale),
                             reads=BKr(r), writes=[B_PT[pt]])

                    def pv_mm(u):
                        pt = u % 3
                        diag = (u == n_off)
                        last = (u == nun - 1)
                        vsrc = Vown[:] if diag else Vst[:, u, :]
                        rd = [B_Vown] if diag else [B_K[u]]
                        for hh in range(4):
                            bk = ob[hh // 2]
                            col = (hh % 2) * 256
                            P.op("pe", lambda e, hh=hh, bk=bk, col=col: e.matmul(
                                bkf(bk)[:, col:col + 256], PT[pt][:, hh * 128:(hh + 1) * 128], vsrc,
                                start=(u == 0 and hh % 2 == 0), stop=last, skip_group_check=True),
                                reads=rd + [B_PT[pt]], writes=BKr(bk), sig=(last and hh % 2 == 1))
                            P.op("pe", lambda e, hh=hh: e.matmul(
                                bkf(db)[:, hg * 4 + hh:hg * 4 + hh + 1], PT[pt][:, hh * 128:(hh + 1) * 128], ones_c[:],
                                start=(u == 0 and hh == 0), stop=last, skip_group_check=True),
                                reads=[B_PT[pt], B_ones], writes=BKr(db), sig=(hh == 3))

                    s_mm(0)
                    if nun > 1:
                        s_mm(1)
                    while pre_b:
                        pre_b.pop(0)()
                    for u in range(nun):
                        exp_u(u)
                        if u + 2 < nun:
                            s_mm(u + 2)
                        pv_mm(u)
                        if deferred and u >= 1:
                            deferred.pop(0)()
                    while deferred:
                        deferred.pop(0)()

                    P.op("dve", lambda e: e.reciprocal(out=rden[:], in_=bkf(db)[:, hg * 4:hg * 4 + 4]),
                         reads=BKr(db), writes=[B_rden])
                    for half in range(2):
                        bk = ob[half]
                        P.op("dve", lambda e, half=half, bk=bk: e.tensor_tensor(
                            out=olat[:, half * 2:half * 2 + 2, :],
                            in0=bkf(bk).rearrange("p (h c) -> p h c", h=2),
                            in1=rden[:, half * 2:half * 2 + 2].unsqueeze(2).to_broadcast([128, 2, CKV]), op=ALU.mult),
                            reads=BKr(bk) + [B_rden], writes=[B_olat])

                    def stage_b(ob=ob):
                        transposes([olat[:, hh, j * 128:(j + 1) * 128] for hh in range(4) for j in range(2)], ob[0], [B_olat])
                        P.op("act", lambda e: e.copy(out=olatT[:].rearrange("p h j t -> p (h j t)"), in_=bkh(ob[0])),
                             reads=BKr(ob[0]), writes=[B_olatT])

                    def stage_c(ob=ob, hg=hg, s=s):
                        for hh in range(4):
                            h = hg * 4 + hh
                            for j in range(2):
                                P.op("pe", lambda e, hh=hh, h=h, j=j: e.matmul(
                                    bkf(ob[1])[:, hh * 128:(hh + 1) * 128], w_uv_s[:, j, h * 128:(h + 1) * 128], olatT[:, hh, j, :],
                                    start=(j == 0), stop=(j == 1), skip_group_check=True),
                                    reads=[B_olatT, B_wA], writes=BKr(ob[1]), sig=(hh == 3 and j == 1))
                        P.op("dve", lambda e: e.tensor_copy(out=OT[:, s, hg * 4:hg * 4 + 4, :].rearrange("p h t -> p (h t)"), in_=bkf(ob[1])),
                             reads=BKr(ob[1]), writes=[B_OT[s]])

                    if hg == 0 and DEFER_POST:
                        pre_b.append(stage_b)
                        deferred.extend([stage_c])
                    else:
                        if hg == 1 and s + 1 < NS:
                            pre1(s + 1)
                        stage_b()
                        stage_c()

            dbg_dump("V", Vst[:].rearrange("p k c -> p (k c)"))
            dbg_dump("KTc", KT_c[:].rearrange("p j s -> p (j s)"))
            dbg_dump("KTr", KT_r[:])
            dbg_dump("OT", OT[:].rearrange("p s h t -> p (s h t)"))
            dbg_dump("QTc", QT_c[:].rearrange("p j h t -> p (j h t)"))
            dbg_dump("QTr", QT_r[:].rearrange("p h t -> p (h t)"))
            dbg_dump("cq", cq[:])
            dbg_dump("qn", qn[:].rearrange("p h n -> p (h n)"))
            dbg_dump("qr", qr[:].rearrange("p h n -> p (h n)"))
            dbg_dump("olat", olat[:].rearrange("p h n -> p (h n)"))
            dbg_dump("rden", rden[:])
            dbg_dump("hsb", h_sb[:])
            dbg_dump("stat", stat[:])

        nc.all_engine_barrier()
        with ExitStack() as sbk:
            def sbb(name, shape, dt):
                return sb(name, shape, dt, sbk)

            ring = [sbb("ring%d" % i, [128, SLAB], BF16) for i in range(RING)]
            B_ring = [Buf("ring%d" % i) for i in range(RING)]
            lng = sbb("lng", [128, 4, D], F32)
            lnb = sbb("lnb", [128, 4, D], F32)
            ET = sbb("ET", [128, 2, 16 * 128], BF16)
            qT_full = sbb("qT", [128, 16, 256], BF16)
            ETf = qT_full[:].rearrange("p h t -> p (h t)").bitcast(F32)
            sink_e = sbb("sink_e", [128, 16], F32)
            flags = sbb("flags", [128, 4], F32)
            B_cB = Buf("constB")
            for L4 in range(4):
                P.dma("sp", lng[:, L4, :], ln_g[L4].partition_broadcast(128), "c2", writes=[B_cB], group=True)
                P.dma("sp", lnb[:, L4, :], ln_b[L4].partition_broadcast(128), "c2", writes=[B_cB], group=True)
            P.dma("sp", sink_e[:], sinks.partition_broadcast(128), "c2", writes=[B_cB], group=True)
            P.dma("sp", flags[:], swa_flags, "c2", writes=[B_cB], group=True)
            for t2 in range(2):
                P.dma("sp", ETf, biasT[t2], "c2", writes=[B_cB], group=True)
                t_et = P.op("act", lambda e, t2=t2: e.activation(out=ET[:, t2, :], in_=ETf, func=AF.Exp), reads=[B_cB], writes=[B_cB])
            P.op("act", lambda e: e.activation(out=sink_e[:], in_=sink_e[:], func=AF.Exp), reads=[B_cB], writes=[B_cB])

            xs_all = sbb("xs", [128, 2, 2, D], F32)
            B_xs_all = [[Buf("xs00"), Buf("xs01")], [Buf("xs10"), Buf("xs11")]]
            xs = xs_all[:, 0]
            B_xs = B_xs_all[0]
            cur = {"par": 0}
            zb = [sbb("zb%d" % i, [128, D], BF16) for i in range(2)]
            B_zb = [Buf("zb0"), Buf("zb1")]
            gT = sbb("gT", [128, 4, 8], F32)
            bT = sbb("bT", [128, 4, 8], F32)
            P.dma("sp", gT[:], ln_gT.rearrange("p (l c) -> p l c", l=4), "c2", writes=[B_cB], group=True)
            P.dma("sp", bT[:], ln_bT.rearrange("p (l c) -> p l c", l=4), "c2", writes=[B_cB], group=True)
            aT = sbb("aT", [128, 8, 256], BF16)
            B_aTs = [Buf("aT0"), Buf("aT1")]
            hT = [sbb("hT%d" % i, [128, 4, 256], BF16) for i in range(2)]
            B_hT = [Buf("hT0"), Buf("hT1")]
            bst_all = sbb("bst", [128, 2, 2, 2, 6], F32)
            mv_all = sbb("mv", [128, 2, 2, 4], F32)
            B_st_all = [[Buf("bst00"), Buf("bst01")], [Buf("bst10"), Buf("bst11")]]
            bst = bst_all[:, 0]
            mv = mv_all[:, 0]
            B_st = B_st_all[0]
            qT = qT_full[0:64]
            B_qT = Buf("qT")
            B_qT.w = t_et
            kT = sbb("kT", [64, 4, 256], BF16)
            B_kT = Buf("kT")
            Vaug = sbb("Vaug", [128, 2, 4, 65], BF16)
            B_Va = Buf("Vaug")
            kT_p = sbb("kT_p", [64, 4, 128], BF16)
            Va_p = sbb("Va_p", [128, 4, 65], BF16)
            B_prev = Buf("prev")
            kT_e = sbb("kT_e", [64, 4, 128], BF16)
            Va_e = sbb("Va_e", [128, 4, 65], BF16)
            B_ext = Buf("ext")
            eS = [sbb("eS%d" % i, [128, 512], F32) for i in range(2)]
            B_eS = [Buf("eS0"), Buf("eS1")]
            PB = [sbb("PB%d" % i, [128, 512], BF16) for i in range(4)]
            B_PB = [Buf("PB%d" % i) for i in range(4)]
            osb = sbb("osb", [128, 16, 64], BF16)
            B_osb = Buf("osb")
            oT2 = sbb("oT2", [128, 8, 128], BF16)
            B_oT2 = Buf("oT2")
            rd16 = sbb("rd16", [128, 16], F32)
            B_rd16 = Buf("rd16")

            P.op("dve", lambda e: e.memset(Vaug[:, :, :, 64:65], 1.0), writes=[B_Va])
            P.op("dve", lambda e: e.memset(Va_p[:, :, 64:65], 1.0), writes=[B_prev])
            P.op("dve", lambda e: e.memset(Va_e[:, :, 64:65], 1.0), writes=[B_ext])

            state = {"n": 0}

            def mlp_seq(b0, insert=()):
                seq = [b0]
                for j in range(8):
                    if j + 1 < 8:
                        seq.append(b0 + 2 * (j + 1))
                    if j == 7:
                        seq += list(insert)
                    seq.append(b0 + 2 * j + 1)
                return seq

            slab_plan = []
            slab_plan += [0, 1] + mlp_seq(2) + [18]
            slab_plan += [0, 1]
            for bi_ in range(CH):
                slab_plan += mlp_seq(2) + [18, 19, 20, 21, 22] + mlp_seq(23, insert=([0, 1] if bi_ + 1 < CH else ()))
            issued = {"n": 0}
            slab_ticket = {}

            def issue_slabs(upto):
                while issued["n"] < min(upto, len(slab_plan)):
                    g = issued["n"]
                    r = g % RING
                    P.dma("sp", ring[r][:], wscr[slab_plan[g]], "ring%d" % r, reads=[B_scr], writes=[B_ring[r]])
                    issued["n"] += 1

            def next_slab():
                g = state["n"]
                state["n"] += 1
                issue_slabs(g + 1)
                return g % RING

            def release_and_prefetch():
                issue_slabs(state["n"] + RING - 1)

            def ln_part1a(sl, li):
                P.op("dve", lambda e: e.bn_stats(out=bst[:, sl, 0, :], in_=xs[:, sl, 0:512]), reads=[B_xs[sl]], writes=[B_st[sl]])
                P.op("dve", lambda e: e.bn_stats(out=bst[:, sl, 1, :], in_=xs[:, sl, 512:1024]), reads=[B_xs[sl]], writes=[B_st[sl]])
                P.op("dve", lambda e: e.bn_aggr(out=mv[:, sl, 0:2], in_=bst[:, sl].rearrange("p c (t j) -> p (c t) j", j=3)),
                     writes=[B_st[sl]])
                P.op("act", lambda e: e.activation(out=mv[:, sl, 2:3], in_=mv[:, sl, 1:2], func=AF.Ln, bias=LN_EPS), writes=[B_st[sl]])
                P.op("act", lambda e: e.activation(out=mv[:, sl, 2:3], in_=mv[:, sl, 2:3], func=AF.Exp, scale=-0.5), writes=[B_st[sl]])

            def ln_part1b(sl, li, need_aT=True):
                x_ap = xs[:, sl, :]
                P.op("dve", lambda e: e.scalar_tensor_tensor(out=x_ap, in0=x_ap, scalar=mv[:, sl, 0:1], in1=lng[:, li, :],
                                                             op0=ALU.subtract, op1=ALU.mult),
                     reads=[B_st[sl], B_cB], writes=[B_xs[sl]])
                P.op("dve", lambda e: e.scalar_tensor_tensor(out=x_ap, in0=x_ap, scalar=mv[:, sl, 2:3], in1=lnb[:, li, :],
                                                             op0=ALU.mult, op1=ALU.add),
                     reads=[B_st[sl], B_cB], writes=[B_xs[sl]])
                if need_aT:
                    P.op("act", lambda e: e.copy(out=zb[sl][:], in_=x_ap), reads=[B_xs[sl]], writes=[B_zb[sl]])

            def ln_part1(sl, li, need_aT=True):
                ln_part1a(sl, li)
                ln_part1b(sl, li, need_aT)

            def ln_part2(sl, li, need_aT=True, do_a=True, do_b=True):
                if need_aT and do_a:
                    bk = 6 + sl
                    transposes([zb[sl][:, c * 128:(c + 1) * 128] for c in range(8)], bk, [B_zb[sl]])
                    src = bkh(bk).rearrange("p (c t) -> p c t", c=8)
                    dst = aT[:, :, sl * 128:(sl + 1) * 128]
                    if sl == 0:
                        P.op("act", lambda e: e.copy(out=dst, in_=src), reads=BKr(bk), writes=[B_aTs[sl]])
                    else:
                        P.op("dve", lambda e: e.tensor_copy(out=dst, in_=src), reads=BKr(bk), writes=[B_aTs[sl]])

            def ln_stage(nsl, li, need_aT=True):
                for sl in range(nsl):
                    ln_part1a(sl, li)
                for sl in range(nsl):
                    ln_part1b(sl, li, need_aT)
                for sl in range(nsl):
                    ln_part2(sl, li, need_aT)

            def mlp(nsl, hook=None):
                T = nsl * 128
                accb = [[0, 1], [2, 3]]

                def up(j):
                    ru = next_slab()
                    wu = ring[ru][:].rearrange("p (c n) -> p c n", n=512)
                    hb = hT[j % 2]
                    for fc in range(4):
                        bk = 4 + fc
                        for c in range(8):
                            P.op("pe", lambda e, fc=fc, c=c, bk=bk: e.matmul(
                                bkf(bk)[:, 0:T], wu[:, c, fc * 128:(fc + 1) * 128], aT[:, c, 0:T],
                                start=(c == 0), stop=(c == 7)),
                                reads=[B_ring[ru]] + B_aTs, writes=BKr(bk), sig=(c == 7))
                        src = bkf(bk)[:, 0:T]
                        es = fc % 2
                        P.op("act", lambda e, src=src, es=es: e.activation(out=eS[es][:, 0:T], in_=src, func=AF.Relu),
                             reads=BKr(bk), writes=[B_eS[es]])
                        P.op("dve", lambda e, fc=fc, es=es: e.tensor_tensor(out=hb[:, fc, 0:T], in0=eS[es][:, 0:T], in1=eS[es][:, 0:T], op=ALU.mult),
                             reads=[B_eS[es]], writes=[B_hT[j % 2]])

                def down(j):
                    rd = next_slab()
                    wd = ring[rd][:].rearrange("p (c n) -> p c n", n=1024)
                    hb = hT[j % 2]
                    for sl in range(nsl):
                        for dh in range(2):
                            bk = accb[sl][dh]
                            for fc in range(4):
                                P.op("pe", lambda e, sl=sl, dh=dh, fc=fc, bk=bk: e.matmul(
                                    bkf(bk), hb[:, fc, sl * 128:(sl + 1) * 128], wd[:, fc, dh * 512:(dh + 1) * 512],
                                    start=(j == 0 and fc == 0), stop=(j == 7 and fc == 3)),
                                    reads=[B_hT[j % 2], B_ring[rd]], writes=BKr(bk), sig=(fc == 3))
                    release_and_prefetch()

                up(0)
                for j in range(8):
                    if j + 1 < 8:
                        up(j + 1)
                    if j == 7 and hook is not None:
                        hook()
                    down(j)
                for sl in range(nsl):
                    for dh in range(2):
                        bk = accb[sl][dh]
                        xa = xs[:, sl, dh * 512:(dh + 1) * 512]
                        P.op("dve", lambda e, xa=xa, bk=bk: e.scalar_tensor_tensor(out=xa, in0=xa, scalar=ALPHA, in1=bkf(bk),
                                                                                   op0=ALU.mult, op1=ALU.add),
                             reads=BKr(bk), writes=[B_xs[sl]])

            def swa_kv(nsl, sl_list_global):
                T = nsl * 128
                r = next_slab()
                wk = ring[r][:].rearrange("p (c n) -> p c n", n=512)
                for kh in range(4):
                    for c in range(8):
                        P.op("pe", lambda e, kh=kh, c=c: e.matmul(bkf(4 + kh)[0:64, 0:T],
                                                                   wk[:, c, kh * 64:(kh + 1) * 64], aT[:, c, 0:T],
                                                                   start=(c == 0), stop=(c == 7), skip_group_check=True),
                             reads=[B_ring[r]] + B_aTs, writes=BKr(4 + kh), sig=(c == 7))
                for kh in range(4):
                    P.op("dve", lambda e, kh=kh: e.tensor_copy(out=kT[:, kh, 0:T], in_=bkf(4 + kh)[0:64, 0:T]),
                         reads=BKr(4 + kh), writes=[B_kT])
                for sl in range(nsl):
                    for c in range(8):
                        P.op("pe", lambda e, sl=sl, c=c: e.matmul(bkf(3)[:, sl * 256:(sl + 1) * 256], aT[:, c, sl * 128:(sl + 1) * 128],
                                                                   wk[:, c, 256:512], start=(c == 0), stop=(c == 7), skip_group_check=True),
                             reads=[B_ring[r]] + B_aTs, writes=BKr(3), sig=(c == 7))
                    P.op("act", lambda e, sl=sl: e.copy(out=Vaug[:, sl, :, 0:64],
                                                        in_=bkf(3)[:, sl * 256:(sl + 1) * 256].rearrange("p (k d) -> p k d", k=4)),
                         reads=BKr(3), writes=[B_Va])

            def swa_q(nsl):
                T = nsl * 128
                for half in range(2):
                    r = next_slab()
                    wq = ring[r][:].rearrange("p (c n) -> p c n", n=1024)
                    if half == 0:
                        r0, wq0 = r, wq
                    else:
                        r1, wq1 = r, wq
                for h in range(16):
                    bk = 4 + h % 4
                    colo = 0
                    for c in range(8):
                        wsel = wq0 if c < 4 else wq1
                        rsel = r0 if c < 4 else r1
                        P.op("pe", lambda e, h=h, c=c, wsel=wsel, bk=bk, colo=colo: e.matmul(
                            bkf(bk)[0:64, colo:colo + T], wsel[:, c % 4, h * 64:(h + 1) * 64], aT[:, c, 0:T],
                            start=(c == 0), stop=(c == 7), skip_group_check=True),
                            reads=[B_ring[rsel]] + B_aTs, writes=BKr(bk, colo, colo + T), sig=(c == 7))
                    P.op("act" if h % 2 == 0 else "dve",
                         (lambda e, h=h, bk=bk, colo=colo: e.copy(out=qT[:, h, 0:T], in_=bkf(bk)[0:64, colo:colo + T])) if h % 2 == 0 else
                         (lambda e, h=h, bk=bk, colo=colo: e.tensor_copy(out=qT[:, h, 0:T], in_=bkf(bk)[0:64, colo:colo + T])),
                         reads=BKr(bk, colo, colo + T), writes=[B_qT])
                release_and_prefetch()

            def swa_attn(sl, keys):
                def o_ap(h):
                    return bkf(h // 7)[:, (h % 7) * 65:(h % 7) * 65 + 65]
                first_in_bank = {0: True, 1: True, 2: True}
                nk = len(keys)
                tiles = [(ki, kh) for ki in range(nk) for kh in range(4)]

                def score(t):
                    ki, kh = tiles[t]
                    k_ap, v_ap, et_idx, flag_ap, kbufs = keys[ki]
                    sbk_ = (6, 7, 3)[t % 3]
                    P.op("pe", lambda e: e.matmul(bkf(sbk_), k_ap[:, kh, :], qT[:, kh * 4:kh * 4 + 4, sl * 128:(sl + 1) * 128],
                                                  start=True, stop=True),
                         reads=list(kbufs) + [B_qT], writes=BKr(sbk_))

                def softmax_pv(t):
                    ki, kh = tiles[t]
                    k_ap, v_ap, et_idx, flag_ap, kbufs = keys[ki]
                    sbk_ = (6, 7, 3)[t % 3]
                    es = t % 2
                    pb = t % 4
                    P.op("act", lambda e: e.activation(out=eS[es][:], in_=bkf(sbk_), func=AF.Exp, scale=0.125),
                         reads=BKr(sbk_), writes=[B_eS[es]])
                    et_ap = ET[:, et_idx, kh * 512:(kh + 1) * 512]
                    if flag_ap is None:
                        P.op("dve", lambda e: e.tensor_tensor(out=PB[pb][:], in0=eS[es][:], in1=et_ap, op=ALU.mult),
                             reads=[B_eS[es], B_cB], writes=[B_PB[pb]])
                    else:
                        P.op("dve", lambda e: e.scalar_tensor_tensor(out=PB[pb][:], in0=eS[es][:], scalar=flag_ap, in1=et_ap,
                                                                     op0=ALU.mult, op1=ALU.mult),
                             reads=[B_eS[es], B_cB], writes=[B_PB[pb]])
                    for g in range(4):
                        h = kh * 4 + g
                        bko = h // 7
                        st_ = first_in_bank[bko] and ki == 0
                        if st_:
                            first_in_bank[bko] = False
                        P.op("pe", lambda e, h=h, g=g, st_=st_: e.matmul(
                            o_ap(h), PB[pb][:, g * 128:(g + 1) * 128], v_ap[:, kh, :],
                            start=st_, stop=(ki == nk - 1), skip_group_check=True),
                            reads=list(kbufs) + [B_PB[pb]], writes=BKr(bko), sig=(g == 3))

                nt = len(tiles)
                for t in range(min(3, nt)):
                    score(t)
                for t in range(nt):
                    softmax_pv(t)
                    if t + 3 < nt:
                        score(t + 3)
                for b_ in range(3):
                    nh = 7 if b_ < 2 else 2
                    P.op("dve", lambda e, b_=b_, nh=nh: e.tensor_tensor(
                        out=rd16[:, b_ * 7:b_ * 7 + nh],
                        in0=bkf(b_)[:, 0:nh * 65].rearrange("p (h c) -> p h c", c=65)[:, :, 64],
                        in1=sink_e[:, b_ * 7:b_ * 7 + nh], op=ALU.add),
                        reads=BKr(b_) + [B_cB], writes=[B_rd16])
                P.op("dve", lambda e: e.reciprocal(out=rd16[:], in_=rd16[:]), writes=[B_rd16])
                for b_ in range(3):
                    nh = 7 if b_ < 2 else 2
                    P.op("dve", lambda e, b_=b_, nh=nh: e.tensor_tensor(
                        out=osb[:, b_ * 7:b_ * 7 + nh, :],
                        in0=bkf(b_)[:, 0:nh * 65].rearrange("p (h c) -> p h c", c=65)[:, :, 0:64],
                        in1=rd16[:, b_ * 7:b_ * 7 + nh].unsqueeze(2).to_broadcast([128, nh, 64]), op=ALU.mult),
                        reads=BKr(b_) + [B_rd16], writes=[B_osb])

            def out_proj(lhs_of_chunk, lhs_bufs, r0, r1, banks2):
                for half, r in ((0, r0), (1, r1)):
                    w = ring[r][:].rearrange("p (c n) -> p c n", n=1024)
                    for dh in range(2):
                        for c4 in range(4):
                            c = half * 4 + c4
                            P.op("pe", lambda e, dh=dh, c4=c4, c=c, w=w: e.matmul(
                                bkf(banks2[dh]), lhs_of_chunk(c), w[:, c4, dh * 512:(dh + 1) * 512],
                                start=(c == 0), stop=(c == 7), skip_group_check=True),
                                reads=list(lhs_bufs) + [B_ring[r]], writes=BKr(banks2[dh]), sig=(c4 == 3))

            def residual_from_banks(sl, banks2):
                for dh in range(2):
                    xa = xs[:, sl, dh * 512:(dh + 1) * 512]
                    P.op("dve", lambda e, xa=xa, dh=dh: e.scalar_tensor_tensor(out=xa, in0=xa, scalar=ALPHA, in1=bkf(banks2[dh]),
                                                                               op0=ALU.mult, op1=ALU.add),
                         reads=BKr(banks2[dh]), writes=[B_xs[sl]])

            def set_par(par):
                nonlocal xs, B_xs, mv, bst, B_st
                cur["par"] = par
                xs = xs_all[:, par]
                B_xs = B_xs_all[par]
                mv = mv_all[:, par]
                bst = bst_all[:, par]
                B_st = B_st_all[par]

            def layer0_front(slots):
                nsl = len(slots)
                for sl, s in enumerate(slots):
                    P.dma("sp", xs[:, sl, :], xq[s * 128:(s + 1) * 128, :], "xs%d_%d" % (cur["par"], sl), writes=[B_xs[sl]])
                r0 = next_slab()
                r1 = next_slab()
                pb = [[4, 5], [6, 7]]
                for sl, s in enumerate(slots):
                    out_proj(lambda c, s=s: OT[:, s, c, :], [B_OT[s]], r0, r1, pb[sl])
                release_and_prefetch()
                for sl in range(nsl):
                    residual_from_banks(sl, pb[sl])
                for sl in range(nsl):
                    ln_part1a(sl, 0)
                for sl in range(nsl):
                    ln_part1b(sl, 0)

            def layer0_back(nsl, a_done=False):
                for sl in range(nsl):
                    ln_part2(sl, 0, do_a=not a_done)
                mlp(nsl)
                ln_stage(nsl, 1)

            def save_kv(dst_k, dst_v, sl, bdst):
                P.op("pool", lambda e: e.tensor_copy(out=dst_k[:], in_=kT[:, :, sl * 128:(sl + 1) * 128]), reads=[B_kT], writes=[bdst])
                P.op("pool", lambda e: e.tensor_copy(out=dst_v[:], in_=Vaug[:, sl, :, :]), reads=[B_Va], writes=[bdst])

            set_par(0)
            layer0_front([2 * CH])
            layer0_back(1)
            swa_kv(1, None)
            save_kv(kT_e, Va_e, 0, B_ext)
            release_and_prefetch()

            set_par(1)
            layer0_front([0, 1])
            for bi in range(CH):
                slots = [2 * bi, 2 * bi + 1]
                par_now = (bi + 1) % 2
                set_par(par_now)
                layer0_back(2, a_done=(bi > 0))
                swa_kv(2, slots)
                swa_q(2)
                for sl, s in enumerate(slots):
                    keys = []
                    if s == 0:
                        keys.append((kT_e[:], Va_e[:], 0, flags[:, 0:1], [B_ext]))
                    elif s == CH:
                        keys.append((kT_e[:], Va_e[:], 0, flags[:, 1:2], [B_ext]))
                        keys.append((kT_p[:], Va_p[:], 0, flags[:, 2:3], [B_prev]))
                    elif sl == 0:
                        keys.append((kT_p[:], Va_p[:], 0, None, [B_prev]))
                    else:
                        keys.append((kT[:, :, 0:128], Vaug[:, 0, :, :], 0, None, [B_kT, B_Va]))
                    keys.append((kT[:, :, sl * 128:(sl + 1) * 128], Vaug[:, sl, :, :], 1, None, [B_kT, B_Va]))
                    swa_attn(sl, keys)
                    transposes([osb[:, 2 * c:2 * c + 2, :].rearrange("p h d -> p (h d)") for c in range(8)], 3, [B_osb])
                    P.op("act", lambda e: e.copy(out=oT2[:].rearrange("p c t -> p (c t)"), in_=bkh(3)), reads=BKr(3), writes=[B_oT2])
                    if sl == 0:
                        r0 = next_slab()
                        r1 = next_slab()
                    out_proj(lambda c: oT2[:, c, :], [B_oT2], r0, r1, [4, 5])
                    residual_from_banks(sl, [4, 5])
                    if SPLIT_LN:
                        ln_part1(sl, 2)
                save_kv(kT_p, Va_p, 1, B_prev)
                release_and_prefetch()
                if SPLIT_LN:
                    for sl in range(2):
                        ln_part2(sl, 2)
                else:
                    ln_stage(2, 2)

                def front_next(bi=bi, par_now=par_now):
                    if bi + 1 < CH:
                        set_par((bi + 2) % 2)
                        layer0_front([2 * bi + 2, 2 * bi + 3])
                        set_par(par_now)
                mlp(2, hook=front_next)
                if bi + 1 < CH:
                    set_par((bi + 2) % 2)
                    for sl in range(2):
                        ln_part2(sl, 0, do_b=False)
                    set_par(par_now)
                ln_stage(2, 3, need_aT=False)
                for sl, s in enumerate(slots):
                    P.dma("sp", out[s * 128:(s + 1) * 128, :], xs[:, sl, :], "out%d_%d" % (cur["par"], sl), reads=[B_xs[sl]])
            dbg_dump("xs", xs.rearrange("p s d -> p (s d)"))
            dbg_dump("aT", aT[:].rearrange("p c t -> p (c t)"))
            dbg_dump("osb", osb[:].rearrange("p h d -> p (h d)"))
            dbg_dump("qT", qT[:].rearrange("p h t -> p (h t)"))
            dbg_dump("kT", kT[:].rearrange("p h t -> p (h t)"))
            dbg_dump("ET", ET[:].rearrange("p a b -> p (a b)"))
            for nm in ("out0_0", "out0_1", "out1_0", "out1_1"):
                if nm in P.sem:
                    nc.sync.wait_ge(P.sem[nm], P.cnt[nm])
            P.check()
    return nc


def _t5_bucket_np(dist):
    n = np.maximum(dist, 0)
    max_exact = 16
    nf = np.maximum(n, 1).astype(np.float32)
    large = max_exact + (np.log(nf / np.float32(max_exact)) / np.float32(math.log(128 / max_exact))
                         * np.float32(32 - max_exact)).astype(np.int32)
    large = np.minimum(large, 31)
    return np.where(n < max_exact, n, large)


def _rope_tables(pos):
    half = 32
    inv = (np.float32(10000.0) ** (-(np.arange(half, dtype=np.float32) / np.float32(half)))).astype(np.float32)
    ang = (pos.astype(np.float32)[:, None] * inv[None, :]).astype(np.float32)
    return np.concatenate([np.cos(ang.astype(np.float64)), np.sin(ang.astype(np.float64))], axis=1).astype(np.float32)


_NC_CACHE = {}


def _run(inputs, NB, n_batch):
    CH = NB // 4
    NS = 2 * CH + 1
    S = NB * 128
    bf = ml_dtypes.bfloat16
    f32 = np.float32
    x = np.ascontiguousarray(inputs["x"], dtype=f32)
    assert x.shape == (n_batch, S, D)
    if NB not in _NC_CACHE:
        _NC_CACHE[NB] = build_nc(NB)
    nc = _NC_CACHE[NB]

    ident = np.eye(128, dtype=f32).astype(bf)
    kk = np.arange(128)[:, None]
    qq = np.arange(128)[None, :]
    tri = np.where(kk > qq, NEG, 0.0).astype(f32)
    triT = np.tile(tri, (1, 4)).astype(bf)
    onehot = np.zeros((64, S), dtype=f32)
    for kb in range(NB):
        onehot[kb, kb * 128:(kb + 1) * 128] = 1.0
    onehot = onehot.astype(bf)
    css = _rope_tables(np.arange(S))
    rel_bias = np.asarray(inputs["rel_bias"], dtype=f32)
    ii = np.arange(128)[:, None]
    jj = np.arange(256)[None, :]
    dist = ii + 128 - jj
    bias = rel_bias[_t5_bucket_np(dist)]
    valid = (dist >= 0) & (dist < 128)
    bias = np.where(valid[:, :, None], bias, f32(NEG)).astype(f32)
    biasT = np.ascontiguousarray(bias.transpose(1, 2, 0))
    biasT = biasT.reshape(2, 128, 16 * 128)

    w_uk = np.asarray(inputs["mla_w_uk"], dtype=f32)[0]
    w_ukT = np.ascontiguousarray(w_uk.transpose(2, 1, 0)).reshape(128, H * CKV)
    ln_g = np.stack([inputs["ln_mix_g"][0], inputs["ln_mlp_g"][0], inputs["ln_mix_g"][1], inputs["ln_mlp_g"][1]]).astype(f32)
    ln_b = np.stack([inputs["ln_mix_b"][0], inputs["ln_mlp_b"][0], inputs["ln_mix_b"][1], inputs["ln_mlp_b"][1]]).astype(f32)
    common = {
        "onehot": onehot, "ident": ident, "triT": triT, "css": css,
        "mla_w_in": np.ascontiguousarray(inputs["mla_w_in"][0], dtype=f32),
        "mla_g_q": np.ascontiguousarray(inputs["mla_g_q"][0], dtype=f32),
        "mla_g_kv": np.ascontiguousarray(inputs["mla_g_kv"][0], dtype=f32),
        "mla_w_uq": np.ascontiguousarray(np.concatenate(
            [np.asarray(inputs["mla_w_uq"][0], dtype=f32)[:, :, 0:NOPE].reshape(QR, H * NOPE),
             np.asarray(inputs["mla_w_uq"][0], dtype=f32)[:, :, NOPE:NOPE + RD].reshape(QR, H * RD)], axis=1)),
        "w_ukT": w_ukT,
        "mla_w_uv": np.ascontiguousarray(inputs["mla_w_uv"][0], dtype=f32).reshape(CKV, H * 128),
        "mla_w_o": np.ascontiguousarray(inputs["mla_w_o"][0], dtype=f32),
        "kv_w_shared": np.ascontiguousarray(inputs["kv_w_shared"], dtype=f32),
        "swa_w_q": np.ascontiguousarray(inputs["swa_w_q"][0], dtype=f32),
        "swa_w_o": np.ascontiguousarray(inputs["swa_w_o"][0], dtype=f32),
        "swa_sinks": np.ascontiguousarray(inputs["swa_sinks"][0], dtype=f32),
        "biasT": biasT,
        "mlp_w_up": np.ascontiguousarray(inputs["mlp_w_up"], dtype=f32),
        "mlp_w_down": np.ascontiguousarray(inputs["mlp_w_down"], dtype=f32),
        "ln_g": ln_g, "ln_b": ln_b,
        "ln_gT": np.ascontiguousarray(ln_g.reshape(4, 8, 128).transpose(2, 0, 1)).reshape(128, 32),
        "ln_bT": np.ascontiguousarray(ln_b.reshape(4, 8, 128).transpose(2, 0, 1)).reshape(128, 32),
    }
    xT_all = np.ascontiguousarray(x.reshape(n_batch, NB, 128, 8, 128).transpose(0, 1, 4, 3, 2)).reshape(n_batch, S, D)
    in_maps = []
    for core in range(2 * n_batch):
        b, par = core // 2, core % 2
        blocks = _slot_blocks(par, NB)
        rows = np.concatenate([np.arange(bl * 128, (bl + 1) * 128) for bl in blocks])
        maskrows = np.zeros((NS, 64, 128), dtype=f32)
        for s, bl in enumerate(blocks):
            maskrows[s, bl:, :] = NEG
        flags = np.zeros((128, 4), dtype=f32)
        if par == 0:
            flags[:, 0] = 0.0; flags[:, 1] = 1.0; flags[:, 2] = 0.0
        else:
            flags[:, 0] = 1.0; flags[:, 1] = 0.0; flags[:, 2] = 1.0
        m = dict(common)
        m["xq"] = np.ascontiguousarray(x[b][rows])
        m["xseqT"] = xT_all[b]
        m["xqT"] = np.ascontiguousarray(xT_all[b][rows])
        m["csq"] = np.ascontiguousarray(css[rows])
        m["maskrows"] = maskrows.astype(bf)
        m["swa_flags"] = flags
        in_maps.append(m)
    res = run_bass_kernel_spmd(nc, in_maps, core_ids=list(range(2 * n_batch)))
    if DEBUG:
        global _LAST_RES
        _LAST_RES = res.results
    outp = np.zeros((n_batch, S, D), dtype=f32)
    for core in range(2 * n_batch):
        b, par = core // 2, core % 2
        blocks = _slot_blocks(par, NB)[:2 * CH]
        o = res.results[core]["out"]
        for s, bl in enumerate(blocks):
            outp[b, bl * 128:(bl + 1) * 128] = o[s * 128:(s + 1) * 128]
    return outp


def kernel(**inputs):
    return _run(inputs, 64, 4)
```
